# Optimizing a Trainium2 kernel written in Bass

```python
import math
import jax, jax.numpy as jnp
from jax import lax
import numpy as np

D_MODEL = 1024
BATCH = 8
SEQ = 2048
DEPTH = 2
DEC_BATCH = 128
DEC_SEQ = 4
PAST_LEN = 16384
PAGE_SIZE = 128

N_BRANCH = 4
BRANCH_W = D_MODEL // 2
CONV_W = 4
EPS = 1e-6
SSD_HEADDIM = 64
SSD_HEADS = BRANCH_W // SSD_HEADDIM
SSD_GROUPS = 2
SSD_RPG = SSD_HEADS // SSD_GROUPS
SSD_STATE = 64
SSD_CONV_CH = BRANCH_W + 2 * SSD_GROUPS * SSD_STATE
SSD_CHUNK = 128
S5_GROUP = 16
S5_GROUPS = BRANCH_W // S5_GROUP
S5_STATE = 64
ML_HEADS = 4
ML_HEADDIM = BRANCH_W // ML_HEADS
ML_CHUNK = 128
HG_HEADS = 4
HG_HEADDIM = BRANCH_W // HG_HEADS
HG_CHUNK = 64

IN_SPLITS = (BRANCH_W, SSD_CONV_CH, SSD_HEADS,
             BRANCH_W, BRANCH_W,
             BRANCH_W, BRANCH_W, BRANCH_W, ML_HEADS, ML_HEADS,
             BRANCH_W, BRANCH_W, BRANCH_W, BRANCH_W,
             N_BRANCH * D_MODEL)
D_IN = sum(IN_SPLITS)

kernel_name = "hybrid_ssd_s5_mlstm_hgrn2_decode_step"

F32 = jnp.float32


def rmsnorm(x, w):
    xf = x.astype(F32)
    y = xf * lax.rsqrt(jnp.mean(xf * xf, -1, keepdims=True) + EPS)
    return (y * w.astype(F32)).astype(x.dtype)


def head_rmsnorm(h, w, n_heads):
    b, l, wd = h.shape
    hh = h.reshape(b, l, n_heads, wd // n_heads)
    hh = hh * lax.rsqrt(jnp.mean(hh * hh, -1, keepdims=True) + EPS)
    return hh.reshape(b, l, wd) * w.astype(F32)


def causal_conv(x, buf, w, b):
    L = x.shape[1]
    xp = jnp.concatenate([buf.astype(x.dtype), x], axis=1)
    y = b
    for j in range(CONV_W):
        y = y + xp[:, j:j + L] * w[j]
    return y, xp[:, -(CONV_W - 1):]


def to_chunks(a, q):
    b, l = a.shape[:2]
    return jnp.moveaxis(a.reshape((b, l // q, q) + a.shape[2:]), 1, 0)


def from_chunks(a):
    n, b, q = a.shape[:3]
    return jnp.moveaxis(a, 0, 1).reshape((b, n * q) + a.shape[3:])


def cmul(ar, ai, br, bi):
    return ar * br - ai * bi, ar * bi + ai * br


def ssd_scan(x, a, bm, cm, h0):
    q = math.gcd(x.shape[1], SSD_CHUNK)
    tri = jnp.tril(jnp.ones((q, q), bool))

    def step(h, inp):
        xc, ac, bc, cc = inp
        acum = jnp.cumsum(ac, axis=1)
        seg = acum[:, :, None] - acum[:, None, :]
        decay = jnp.exp(jnp.where(tri[None, :, :, None, None], seg, -jnp.inf))
        cb = jnp.einsum('btgn,bsgn->btsg', cc, bc)
        y = jnp.einsum('btsg,btsgr,bsgrp->btgrp', cb, decay, xc)
        y = y + jnp.einsum('btgn,bgrpn->btgrp', cc, h) * jnp.exp(acum)[..., None]
        to_end = jnp.exp(acum[:, -1:] - acum)
        h = h * jnp.exp(acum[:, -1])[..., None, None] + jnp.einsum('bsgn,bsgr,bsgrp->bgrpn', bc, to_end, xc)
        return h, y

    h, ys = lax.scan(step, h0, (to_chunks(x, q), to_chunks(a, q), to_chunks(bm, q), to_chunks(cm, q)))
    return from_chunks(ys), h


def ssd_branch(z, xbc, dt_raw, conv_buf, h0, p):
    bsz, L, _ = z.shape
    xbc, conv_new = causal_conv(xbc, conv_buf, p['ssd_conv_w'], p['ssd_conv_b'])
    xbc = jax.nn.silu(xbc.astype(F32))
    xs, bm, cm = jnp.split(xbc, [BRANCH_W, BRANCH_W + SSD_GROUPS * SSD_STATE], axis=-1)
    xs = xs.reshape(bsz, L, SSD_GROUPS, SSD_RPG, SSD_HEADDIM)
    bm = bm.reshape(bsz, L, SSD_GROUPS, SSD_STATE)
    cm = cm.reshape(bsz, L, SSD_GROUPS, SSD_STATE)
    dt = jax.nn.softplus(dt_raw.astype(F32) + p['ssd_dt_bias'].astype(F32)).reshape(bsz, L, SSD_GROUPS, SSD_RPG)
    A = -jnp.exp(p['ssd_A_log'].astype(F32)).reshape(SSD_GROUPS, SSD_RPG)
    h0g = h0.astype(F32).reshape(bsz, SSD_GROUPS, SSD_RPG, SSD_HEADDIM, SSD_STATE)
    y, h = ssd_scan(xs * dt[..., None], dt * A, bm, cm, h0g)
    y = y + xs * p['ssd_D'].astype(F32).reshape(SSD_GROUPS, SSD_RPG, 1)
    y = y.reshape(bsz, L, BRANCH_W) * jax.nn.silu(z.astype(F32))
    y = rmsnorm(y, p['ssd_norm_w'])
    h = h.reshape(bsz, SSD_HEADS, SSD_HEADDIM, SSD_STATE)
    return y, conv_new.astype(conv_buf.dtype), h.astype(h0.dtype)


def s5_branch(u, gate, h0r, h0i, p):
    bsz, L, _ = u.shape
    uf = u.astype(F32)
    ug = uf.reshape(bsz, L, S5_GROUPS, S5_GROUP)
    dt = jnp.exp(p['s5_log_dt'].astype(F32))[:, None]
    lr = p['s5_A_re'].astype(F32)
    li = p['s5_A_im'].astype(F32)
    mag = jnp.exp(lr * dt)
    abr, abi = mag * jnp.cos(li * dt), mag * jnp.sin(li * dt)
    den = lr * lr + li * li
    cr = ((abr - 1.0) * lr + abi * li) / den
    ci = (abi * lr - (abr - 1.0) * li) / den
    bbr, bbi = cmul(cr[..., None], ci[..., None], p['s5_B_re'].astype(F32), p['s5_B_im'].astype(F32))
    bur = jnp.einsum('blgc,gpc->blgp', ug, bbr)
    bui = jnp.einsum('blgc,gpc->blgp', ug, bbi)
    ir, ii = cmul(abr, abi, h0r.astype(F32), h0i.astype(F32))
    bur = bur.at[:, 0].add(ir)
    bui = bui.at[:, 0].add(ii)
    a_r = jnp.broadcast_to(abr, (1, L) + abr.shape)
    a_i = jnp.broadcast_to(abi, (1, L) + abi.shape)

    def combine(e1, e2):
        a1r, a1i, b1r, b1i = e1
        a2r, a2i, b2r, b2i = e2
        nar, nai = cmul(a2r, a2i, a1r, a1i)
        nbr, nbi = cmul(a2r, a2i, b1r, b1i)
        return nar, nai, nbr + b2r, nbi + b2i

    _, _, hr, hi = lax.associative_scan(combine, (a_r, a_i, bur, bui), axis=1)
    y = (jnp.einsum('blgp,gcp->blgc', hr, p['s5_C_re'].astype(F32))
         - jnp.einsum('blgp,gcp->blgc', hi, p['s5_C_im'].astype(F32)))
    y = y.reshape(bsz, L, BRANCH_W) + uf * p['s5_D'].astype(F32)
    g = jax.nn.gelu(y)
    y = g * jax.nn.sigmoid(g @ p['s5_glu_w'].astype(F32))
    y = y * jax.nn.silu(gate.astype(F32))
    return y, hr[:, -1].astype(h0r.dtype), hi[:, -1].astype(h0i.dtype)


def mlstm_scan(q, k, v, ig, lf, C0, n0, m0):
    qn = math.gcd(q.shape[1], ML_CHUNK)
    tri = jnp.tril(jnp.ones((qn, qn), bool))

    def step(carry, inp):
        C, n, m = carry
        qc, kc, vc, ic, fc = inp
        b = jnp.cumsum(fc, axis=1)
        dlog = jnp.where(tri[None, :, :, None], b[:, :, None] - b[:, None] + ic[:, None], -jnp.inf)
        inter = b + m[:, None]
        m_t = jnp.maximum(inter, jnp.max(dlog, axis=2))
        w = jnp.exp(dlog - m_t[:, :, None])
        s = jnp.einsum('bthk,bshk->btsh', qc, kc) * w
        scale = jnp.exp(inter - m_t)
        num = jnp.einsum('btsh,bshv->bthv', s, vc) + scale[..., None] * jnp.einsum('bthk,bhkv->bthv', qc, C)
        dot = jnp.sum(s, axis=2) + scale * jnp.einsum('bthk,bhk->bth', qc, n)
        h = num / jnp.maximum(jnp.abs(dot), jnp.exp(-m_t))[..., None]
        m_new = m_t[:, -1]
        w_end = jnp.exp(b[:, -1:] - b + ic - m_new[:, None])
        carry_scale = jnp.exp(b[:, -1] + m - m_new)
        C = carry_scale[..., None, None] * C + jnp.einsum('bsh,bshk,bshv->bhkv', w_end, kc, vc)
        n = carry_scale[..., None] * n + jnp.einsum('bsh,bshk->bhk', w_end, kc)
        return (C, n, m_new), h

    (C, n, m), hs = lax.scan(step, (C0, n0, m0),
                             (to_chunks(q, qn), to_chunks(k, qn), to_chunks(v, qn), to_chunks(ig, qn), to_chunks(lf, qn)))
    return from_chunks(hs), C, n, m


def mlstm_branch(xm, z, o_pre, i_pre, f_pre, conv_buf, C0, n0, m0, p):
    bsz, L, _ = xm.shape
    shp = (bsz, L, ML_HEADS, ML_HEADDIM)
    xc, conv_new = causal_conv(xm, conv_buf, p['ml_conv_w'], p['ml_conv_b'])
    xc = jax.nn.silu(xc.astype(F32))
    xch = xc.reshape(shp)
    q = jnp.einsum('blhc,hck->blhk', xch, p['ml_wq'].astype(F32))
    k = jnp.einsum('blhc,hck->blhk', xch, p['ml_wk'].astype(F32)) * ML_HEADDIM ** -0.5
    v = jnp.einsum('blhc,hcv->blhv', xm.astype(F32).reshape(shp), p['ml_wv'].astype(F32))
    ig = i_pre.astype(F32) + p['ml_i_bias'].astype(F32)
    lf = jax.nn.log_sigmoid(f_pre.astype(F32) + p['ml_f_bias'].astype(F32))
    h, C, n, m = mlstm_scan(q, k, v, ig, lf, C0.astype(F32), n0.astype(F32), m0.astype(F32))
    h = h * jax.nn.sigmoid(o_pre.astype(F32)).reshape(shp)
    hc = h - jnp.mean(h, -1, keepdims=True)
    h = hc * lax.rsqrt(jnp.mean(hc * hc, -1, keepdims=True) + EPS)
    h = h.reshape(bsz, L, BRANCH_W) * p['ml_norm_w'].astype(F32) + p['ml_skip'].astype(F32) * xc
    y = h * jax.nn.silu(z.astype(F32))
    return y, conv_new.astype(conv_buf.dtype), C.astype(C0.dtype), n.astype(n0.dtype), m.astype(m0.dtype)


def hgrn_scan(q, k, v, lf, S0):
    qn = math.gcd(q.shape[1], HG_CHUNK)
    tri = jnp.tril(jnp.ones((qn, qn), bool))

    def step(S, inp):
        qc, kc, vc, lc = inp
        g = jnp.cumsum(lc, axis=1)
        diff = g[:, :, None] - g[:, None, :]
        decay = jnp.exp(jnp.where(tri[None, :, :, None, None], diff, -jnp.inf))
        att = jnp.einsum('bthk,btshk,bshk->btsh', qc, decay, kc)
        o = jnp.einsum('btsh,bshv->bthv', att, vc) + jnp.einsum('bthk,bhkv->bthv', qc * jnp.exp(g), S)
        g_last = g[:, -1]
        S = S * jnp.exp(g_last)[..., None] + jnp.einsum('bshk,bshv->bhkv', kc * jnp.exp(g_last[:, None] - g), vc)
        return S, o

    S, os_ = lax.scan(step, S0, (to_chunks(q, qn), to_chunks(k, qn), to_chunks(v, qn), to_chunks(lf, qn)))
    return from_chunks(os_), S


def hgrn_branch(f_pre, i_in, q_pre, g, S0, lb, p):
    bsz, L, _ = f_pre.shape
    shp = (bsz, L, HG_HEADS, HG_HEADDIM)
    fg = lb + (1.0 - lb) * jax.nn.sigmoid(f_pre.astype(F32))
    lf = jnp.log(fg).reshape(shp)
    k = (1.0 - fg).reshape(shp)
    q = (jax.nn.silu(q_pre.astype(F32)) * HG_HEADDIM ** -0.5).reshape(shp)
    v = i_in.astype(F32).reshape(shp)
    o, S = hgrn_scan(q, k, v, lf, S0.astype(F32))
    o = head_rmsnorm(o.reshape(bsz, L, BRANCH_W), p['hg_norm_w'], HG_HEADS) * jax.nn.silu(g.astype(F32))
    return o, S.astype(S0.dtype)


def layer(x, st, p, lb):
    ssd_conv, ssd_h, s5_re, s5_im, ml_conv, ml_C, ml_n, ml_m, hg_S = st
    bsz, L, _ = x.shape
    proj = rmsnorm(x, p['norm_w']) @ p['w_in']
    (a_z, a_xbc, a_dt, b_u, b_gate, c_x, c_z, c_o, c_i, c_f,
     d_f, d_i, d_q, d_g, merge) = jnp.split(proj, np.cumsum(IN_SPLITS)[:-1].tolist(), axis=-1)
    y_a, ssd_conv, ssd_h = ssd_branch(a_z, a_xbc, a_dt, ssd_conv, ssd_h, p)
    y_b, s5_re, s5_im = s5_branch(b_u, b_gate, s5_re, s5_im, p)
    y_c, ml_conv, ml_C, ml_n, ml_m = mlstm_branch(c_x, c_z, c_o, c_i, c_f, ml_conv, ml_C, ml_n, ml_m, p)
    y_d, hg_S = hgrn_branch(d_f, d_i, d_q, d_g, hg_S, lb, p)
    gates = jax.nn.sigmoid(merge.astype(F32)).reshape(bsz, L, N_BRANCH, D_MODEL)
    merged = gates[:, :, 0] * (y_a @ p['w_branch'][0].astype(F32))
    merged = merged + gates[:, :, 1] * (y_b @ p['w_branch'][1].astype(F32))
    merged = merged + gates[:, :, 2] * (y_c @ p['w_branch'][2].astype(F32))
    merged = merged + gates[:, :, 3] * (y_d @ p['w_branch'][3].astype(F32))
    out = merged.astype(x.dtype) @ p['w_out']
    return x + out, (ssd_conv, ssd_h, s5_re, s5_im, ml_conv, ml_C, ml_n, ml_m, hg_S)


def run_trunk(x, states, w, hg_lb, final_norm_w):
    per_layer = []
    for l in range(DEPTH):
        p = {name: arr[l] for name, arr in w.items()}
        x, new = layer(x, tuple(s[l] for s in states), p, hg_lb[l])
        per_layer.append(new)
    new_states = tuple(jnp.stack(ss, axis=0) for ss in zip(*per_layer))
    return rmsnorm(x, final_norm_w), new_states


def setup_inputs(seed: int = 0) -> dict:
    key = jax.random.key(seed)
    ks = iter(jax.random.split(key, 64))

    def nrm(shape, scale=1.0):
        return scale * jax.random.normal(next(ks), shape, F32)

    def unif(shape, lo, hi):
        return jax.random.uniform(next(ks), shape, F32, lo, hi)

    ssd_dt = jnp.exp(unif((DEPTH, SSD_HEADS), math.log(1e-3), math.log(1e-1)))
    return {
        'x_prompt': nrm((BATCH, SEQ, D_MODEL)),
        'x_sample': nrm((DEC_BATCH, DEC_SEQ, D_MODEL)),
        'state_ssd_conv': nrm((DEPTH, DEC_BATCH, CONV_W - 1, SSD_CONV_CH)),
        'state_ssd': nrm((DEPTH, DEC_BATCH, SSD_HEADS, SSD_HEADDIM, SSD_STATE), 0.1),
        'state_s5_re': nrm((DEPTH, DEC_BATCH, S5_GROUPS, S5_STATE), 0.1),
        'state_s5_im': nrm((DEPTH, DEC_BATCH, S5_GROUPS, S5_STATE), 0.1),
        'state_mlstm_conv': nrm((DEPTH, DEC_BATCH, CONV_W - 1, BRANCH_W)),
        'state_mlstm_C': nrm((DEPTH, DEC_BATCH, ML_HEADS, ML_HEADDIM, ML_HEADDIM), 0.1),
        'state_mlstm_n': nrm((DEPTH, DEC_BATCH, ML_HEADS, ML_HEADDIM), 0.1),
        'state_mlstm_m': nrm((DEPTH, DEC_BATCH, ML_HEADS), 0.5),
        'state_hgrn': nrm((DEPTH, DEC_BATCH, HG_HEADS, HG_HEADDIM, HG_HEADDIM), 0.1),
        'norm_w': 1.0 + nrm((DEPTH, D_MODEL), 0.02),
        'w_in': nrm((DEPTH, D_MODEL, D_IN), D_MODEL ** -0.5),
        'ssd_conv_w': nrm((DEPTH, CONV_W, SSD_CONV_CH), 0.5),
        'ssd_conv_b': nrm((DEPTH, SSD_CONV_CH), 0.02),
        'ssd_dt_bias': ssd_dt + jnp.log(-jnp.expm1(-ssd_dt)),
        'ssd_A_log': jnp.log(unif((DEPTH, SSD_HEADS), 1.0, 16.0)),
        'ssd_D': 1.0 + nrm((DEPTH, SSD_HEADS), 0.1),
        'ssd_norm_w': 1.0 + nrm((DEPTH, BRANCH_W), 0.02),
        's5_A_re': -0.5 + nrm((DEPTH, S5_GROUPS, S5_STATE), 0.01),
        's5_A_im': math.pi * jnp.arange(S5_STATE, dtype=F32) + nrm((DEPTH, S5_GROUPS, S5_STATE), 0.01),
        's5_B_re': nrm((DEPTH, S5_GROUPS, S5_STATE, S5_GROUP), (2.0 * S5_GROUP) ** -0.5),
        's5_B_im': nrm((DEPTH, S5_GROUPS, S5_STATE, S5_GROUP), (2.0 * S5_GROUP) ** -0.5),
        's5_C_re': nrm((DEPTH, S5_GROUPS, S5_GROUP, S5_STATE), (2.0 * S5_STATE) ** -0.5),
        's5_C_im': nrm((DEPTH, S5_GROUPS, S5_GROUP, S5_STATE), (2.0 * S5_STATE) ** -0.5),
        's5_D': nrm((DEPTH, BRANCH_W)),
        's5_log_dt': unif((DEPTH, S5_GROUPS), math.log(1e-3), math.log(1e-1)),
        's5_glu_w': nrm((DEPTH, BRANCH_W, BRANCH_W), BRANCH_W ** -0.5),
        'ml_conv_w': nrm((DEPTH, CONV_W, BRANCH_W), 0.5),
        'ml_conv_b': nrm((DEPTH, BRANCH_W), 0.02),
        'ml_wq': nrm((DEPTH, ML_HEADS, ML_HEADDIM, ML_HEADDIM), ML_HEADDIM ** -0.5),
        'ml_wk': nrm((DEPTH, ML_HEADS, ML_HEADDIM, ML_HEADDIM), ML_HEADDIM ** -0.5),
        'ml_wv': nrm((DEPTH, ML_HEADS, ML_HEADDIM, ML_HEADDIM), ML_HEADDIM ** -0.5),
        'ml_i_bias': nrm((DEPTH, ML_HEADS), 0.1),
        'ml_f_bias': jnp.linspace(3.0, 6.0, ML_HEADS, dtype=F32) + nrm((DEPTH, ML_HEADS), 0.1),
        'ml_norm_w': 1.0 + nrm((DEPTH, BRANCH_W), 0.02),
        'ml_skip': 1.0 + nrm((DEPTH, BRANCH_W), 0.02),
        'hg_lb_logits': nrm((DEPTH, BRANCH_W)),
        'hg_norm_w': 1.0 + nrm((DEPTH, BRANCH_W), 0.02),
        'w_branch': nrm((DEPTH, N_BRANCH, BRANCH_W, D_MODEL), BRANCH_W ** -0.5),
        'w_out': nrm((DEPTH, D_MODEL, D_MODEL), D_MODEL ** -0.5),
        'final_norm_w': 1.0 + nrm((D_MODEL,), 0.02),
    }


def reference(x_prompt, x_sample, state_ssd_conv, state_ssd, state_s5_re, state_s5_im,
              state_mlstm_conv, state_mlstm_C, state_mlstm_n, state_mlstm_m, state_hgrn,
              norm_w, w_in, ssd_conv_w, ssd_conv_b, ssd_dt_bias, ssd_A_log, ssd_D, ssd_norm_w,
              s5_A_re, s5_A_im, s5_B_re, s5_B_im, s5_C_re, s5_C_im, s5_D, s5_log_dt, s5_glu_w,
              ml_conv_w, ml_conv_b, ml_wq, ml_wk, ml_wv, ml_i_bias, ml_f_bias, ml_norm_w, ml_skip,
              hg_lb_logits, hg_norm_w, w_branch, w_out, final_norm_w):
    w = {'norm_w': norm_w, 'w_in': w_in,
         'ssd_conv_w': ssd_conv_w, 'ssd_conv_b': ssd_conv_b, 'ssd_dt_bias': ssd_dt_bias,
         'ssd_A_log': ssd_A_log, 'ssd_D': ssd_D, 'ssd_norm_w': ssd_norm_w,
         's5_A_re': s5_A_re, 's5_A_im': s5_A_im, 's5_B_re': s5_B_re, 's5_B_im': s5_B_im,
         's5_C_re': s5_C_re, 's5_C_im': s5_C_im, 's5_D': s5_D, 's5_log_dt': s5_log_dt, 's5_glu_w': s5_glu_w,
         'ml_conv_w': ml_conv_w, 'ml_conv_b': ml_conv_b, 'ml_wq': ml_wq, 'ml_wk': ml_wk, 'ml_wv': ml_wv,
         'ml_i_bias': ml_i_bias, 'ml_f_bias': ml_f_bias, 'ml_norm_w': ml_norm_w, 'ml_skip': ml_skip,
         'hg_norm_w': hg_norm_w, 'w_branch': w_branch, 'w_out': w_out}
    lb_cum = jnp.cumsum(jax.nn.softmax(hg_lb_logits.astype(F32), axis=0), axis=0)
    hg_lb = lb_cum - lb_cum[0]
    sample_states = (state_ssd_conv, state_ssd, state_s5_re, state_s5_im, state_mlstm_conv,
                     state_mlstm_C, state_mlstm_n, state_mlstm_m, state_hgrn)
    prompt_states = tuple(jnp.zeros((DEPTH, x_prompt.shape[0]) + s.shape[2:], x_prompt.dtype) for s in sample_states)
    y_prompt, new_p = run_trunk(x_prompt, prompt_states, w, hg_lb, final_norm_w)
    y_sample, new_s = run_trunk(x_sample, sample_states, w, hg_lb, final_norm_w)
    p_ssd_conv, p_ssd, p_s5_re, p_s5_im, p_ml_conv, p_ml_C, p_ml_n, p_ml_m, p_hgrn = new_p
    s_ssd_conv, s_ssd, s_s5_re, s_s5_im, s_ml_conv, s_ml_C, s_ml_n, s_ml_m, s_hgrn = new_s
    return (y_prompt, y_sample, p_ssd_conv, s_ssd_conv, p_ssd, s_ssd, p_s5_re, s_s5_re, p_s5_im, s_s5_im,
            p_ml_conv, s_ml_conv, p_ml_C, s_ml_C, p_ml_n, s_ml_n, p_ml_m, s_ml_m, p_hgrn, s_hgrn)
```

```python
import contextlib
import math
import numpy as np
import concourse.bass as bass
import concourse.mybir as mybir
from concourse.bass_utils import run_bass_kernel_spmd

F32 = mybir.dt.float32
I32 = mybir.dt.int32
AF = mybir.ActivationFunctionType
ALU = mybir.AluOpType
AX = mybir.AxisListType

L = 2
D = 1024
NCORE = 8
SEQ = 2048
NSEQ_S = 16
QS = 4
EPS = 1e-6
D_IN = 10000
NEG = -1.0e30
SSD_CUT = 99
NORM_CUT = 99
SSD_VAR = 0
TWO_PI = 2.0 * math.pi


class _Op:
    __slots__ = ("eng", "fn", "deps", "is_dma", "idx", "sig", "needed")

    def __init__(self, eng, fn, is_dma):
        self.eng = eng
        self.fn = fn
        self.deps = set()
        self.is_dma = is_dma
        self.sig = None
        self.needed = False


def _region(ap):
    t = ap.tensor
    tn = type(t).__name__
    if not (tn.startswith("SBTensor") or tn.startswith("PSum")):
        return None
    fs = 1
    for s in list(t.shape)[1:]:
        fs *= int(s)
    off = int(ap.offset)
    p0 = off // fs
    f0 = off % fs
    dims = list(ap.ap)
    pstep, pcnt = int(dims[0][0]), int(dims[0][1])
    if pstep == 0 or pcnt == 1:
        p1 = p0 + 1
    else:
        assert pstep == fs, (t.name, pstep, fs)
        p1 = p0 + pcnt
    ext = 1
    for st, cn in dims[1:]:
        ext += (int(cn) - 1) * abs(int(st))
    return (t.name, p0, p1, f0, f0 + ext)


def _isap(x):
    return x is not None and not isinstance(x, (int, float))


class Prog:
    def __init__(self, nc):
        self.nc = nc
        self.ops = []
        self.acc = {}
        self.stack = contextlib.ExitStack()
        self.psum_banks = []
        self.psum_i = 0
        self.arena = None
        self.aoff = 0
        self.tags = []

    def sb(self, name, shape, dtype=F32):
        return self.stack.enter_context(self.nc.sbuf_tensor(name, list(shape), dtype))

    def alloc_psum(self, n=8):
        for i in range(n):
            self.psum_banks.append(
                self.stack.enter_context(self.nc.psum_tensor("psb%d" % i, [128, 512], F32)))

    def ps(self):
        t = self.psum_banks[self.psum_i % len(self.psum_banks)]
        self.psum_i += 1
        return t

    def al(self, *shape, rows=128):
        n = 1
        for s in shape:
            n *= s
        off = self.aoff
        self.aoff += n
        assert self.aoff <= self.arena_n, ("arena overflow", self.aoff)
        v = self.arena[0:rows, off:off + n]
        if len(shape) == 2:
            v = v.rearrange("p (a b) -> p a b", a=shape[0])
        elif len(shape) == 3:
            v = v.rearrange("p (a b c) -> p a b c", a=shape[0], b=shape[1])
        return v

    def op(self, eng, fn, reads, writes, is_dma=False):
        o = _Op(eng, fn, is_dma)
        o.idx = len(self.ops)
        self.tags.append(getattr(self, "cur_tag", ""))
        engkey = ("dma", o.idx) if is_dma else eng
        rr = [r for r in (_region(a) for a in reads if _isap(a)) if r]
        ww = [r for r in (_region(a) for a in writes if _isap(a)) if r]
        if eng == "pe":
            ww = [(n, 0, 128, 0, 512) if n.startswith("psb") else (n, p0, p1, f0, f1) for (n, p0, p1, f0, f1) in ww]
        for (n, p0, p1, f0, f1) in rr:
            for e in self.acc.get(n, ()):
                if e[5] and e[0] < p1 and p0 < e[1] and e[2] < f1 and f0 < e[3]:
                    o.deps.add(e[4])
        for (n, p0, p1, f0, f1) in ww:
            for e in self.acc.get(n, ()):
                if e[0] < p1 and p0 < e[1] and e[2] < f1 and f0 < e[3]:
                    o.deps.add(e[4])
        for (n, p0, p1, f0, f1) in rr:
            if n.startswith("psb"):
                for e in self.acc.get(n, ()):
                    if (not e[5]) and e[6] != engkey:
                        o.deps.add(e[4])
        for (n, p0, p1, f0, f1) in ww:
            lst = self.acc.setdefault(n, [])
            lst[:] = [e for e in lst if not (p0 <= e[0] and e[1] <= p1 and f0 <= e[2] and e[3] <= f1)]
            lst.append([p0, p1, f0, f1, o.idx, True, engkey])
        for (n, p0, p1, f0, f1) in rr:
            lst = self.acc.setdefault(n, [])
            if not is_dma:
                lst[:] = [e for e in lst if not ((not e[5]) and e[6] == engkey and p0 <= e[0] and e[1] <= p1
                                                 and f0 <= e[2] and e[3] <= f1)]
            lst.append([p0, p1, f0, f1, o.idx, False, engkey])
        o.deps.discard(o.idx)
        self.ops.append(o)
        return o

    def mm(self, out, lhsT, rhs, start=True, stop=True):
        rd = [lhsT, rhs] + ([] if start else [out])
        return self.op("pe", lambda e: e.matmul(out, lhsT, rhs, start=start, stop=stop), rd, [out])

    def tr(self, out, in_, ident):
        return self.op("pe", lambda e: e.transpose(out, in_, ident), [in_, ident], [out])

    def act(self, out, in_, func, bias=0.0, scale=1.0):
        rd = [in_, bias, scale]
        return self.op("act", lambda e: e.activation(out, in_, func, bias=bias, scale=scale), rd, [out])

    def tt(self, out, in0, in1, op, eng="dve"):
        return self.op(eng, lambda e: e.tensor_tensor(out, in0, in1, op), [in0, in1], [out])

    def ts(self, out, in0, s1, op0, s2=None, op1=None, eng="dve"):
        rd = [in0, s1, s2]
        if op1 is None:
            return self.op(eng, lambda e: e.tensor_scalar(out, in0, s1, None, op0), rd, [out])
        return self.op(eng, lambda e: e.tensor_scalar(out, in0, s1, s2, op0, op1), rd, [out])

    def stt(self, out, in0, scalar, in1, op0, op1):
        rd = [in0, in1, scalar]
        return self.op("dve", lambda e: e.scalar_tensor_tensor(out, in0, scalar, in1, op0, op1), rd, [out])

    def copy(self, out, in_, eng="dve"):
        if eng == "act":
            return self.op("act", lambda e: e.activation(out, in_, AF.Copy), [in_], [out])
        return self.op(eng, lambda e: e.tensor_copy(out, in_), [in_], [out])

    def red(self, out, in_, op, axis=AX.X):
        return self.op("dve", lambda e: e.tensor_reduce(out, in_, axis, op), [in_], [out])

    def scan(self, out, d0, d1, init, op0=ALU.mult, op1=ALU.add):
        rd = [d0, d1, init]
        return self.op("dve", lambda e: e.tensor_tensor_scan(out, d0, d1, init, op0, op1), rd, [out])

    def recip(self, out, in_):
        return self.op("dve", lambda e: e.reciprocal(out, in_), [in_], [out])

    def memset(self, out, v, eng="dve"):
        return self.op(eng, lambda e: e.memset(out, v), [], [out])

    def dma(self, out, in_, q="sp"):
        return self.op(q, lambda e: e.dma_start(out=out, in_=in_), [in_], [out], is_dma=True)

    def emit(self, n_dma_sems=12):
        nc = self.nc
        ops = self.ops
        for o in ops:
            if o.is_dma:
                o.needed = True
        engs = ["pe", "act", "dve", "pool", "sp"]
        stack = self.stack
        csem = {e: stack.enter_context(nc.semaphore("cs_" + e)) for e in engs}
        dsem = {e: [stack.enter_context(nc.semaphore("ds_%s%d" % (e, i))) for i in range(n_dma_sems)]
                for e in ("sp", "pool", "act")}
        dcount = {e: [0] * n_dma_sems for e in dsem}
        dlast = {e: [None] * n_dma_sems for e in dsem}
        drr = {e: 0 for e in dsem}
        for o in ops:
            if o.is_dma:
                k = drr[o.eng] % n_dma_sems
                drr[o.eng] += 1
                if dlast[o.eng][k] is not None:
                    o.deps.add(dlast[o.eng][k])
                dlast[o.eng][k] = o.idx
        for o in ops:
            if o.eng == "pe":
                o.deps = {d for d in o.deps if ops[d].eng != "pe"}
        for o in ops:
            for d in o.deps:
                ops[d].needed = True
        ccount = {e: 0 for e in engs}
        drr = {e: 0 for e in dsem}
        for o in ops:
            if o.is_dma:
                k = drr[o.eng] % n_dma_sems
                drr[o.eng] += 1
                dcount[o.eng][k] += 16
                o.sig = (dsem[o.eng][k], dcount[o.eng][k], ("d", o.eng, k))
            elif o.needed:
                ccount[o.eng] += 1
                o.sig = (csem[o.eng], ccount[o.eng], ("c", o.eng))
        per = {e: [o for o in ops if o.eng == e] for e in engs}
        self.trace = {e: [] for e in engs}
        self.stats = {e: len(per[e]) for e in engs}

        def run(engname, eng):
            waited = {}
            for o in per[engname]:
                need = {}
                for d in o.deps:
                    s, v, key = ops[d].sig
                    if waited.get(key, 0) >= v:
                        continue
                    if key not in need or need[key][1] < v:
                        need[key] = (s, v)
                for key, (s, v) in need.items():
                    eng.wait_ge(s, v)
                    waited[key] = v
                self.trace[engname].append(([(k_, v_[1]) for k_, v_ in need.items()], o.sig[2] if o.sig else None, o.idx))
                ins = o.fn(eng)
                if o.sig is not None:
                    ins.then_inc(o.sig[0], 16 if o.is_dma else 1)
            if engname in dsem:
                for k in range(n_dma_sems):
                    if dcount[engname][k] > 0 and waited.get(("d", engname, k), 0) < dcount[engname][k]:
                        eng.wait_ge(dsem[engname][k], dcount[engname][k])

        with nc.Block() as block:
            @block.tensor
            def _(e):
                run("pe", e)

            @block.scalar
            def _(e):
                run("act", e)

            @block.vector
            def _(e):
                run("dve", e)

            @block.gpsimd
            def _(e):
                run("pool", e)

            @block.sync
            def _(e):
                run("sp", e)
        self.stack.close()

    def check_deadlock(self):
        sem = {}
        pos = {e: 0 for e in self.trace}
        total = sum(len(v) for v in self.trace.values())
        done = 0
        while done < total:
            prog = False
            for e, lst in self.trace.items():
                while pos[e] < len(lst):
                    waits, sig, idx = lst[pos[e]]
                    if all(sem.get(k, 0) >= v for k, v in waits):
                        if sig is not None:
                            sem[sig] = sem.get(sig, 0) + (16 if sig[0] == "d" else 1)
                        pos[e] += 1
                        done += 1
                        prog = True
                    else:
                        break
            if not prog:
                return {e: (pos[e], self.trace[e][pos[e]] if pos[e] < len(self.trace[e]) else None) for e in self.trace}
        return None


O_AZ = 0
O_XBC = 512
O_DT = 1280
O_U = 1288
O_GATE = 1800
O_CX = 2312
O_CZ = 2824
O_CO = 3336
O_CI = 3848
O_CF = 3852
O_DF = 3856
O_DI = 4368
O_DQ = 4880
O_DG = 5392
O_MERGE = 5904

C_ID = 0
C_ONES = 128
C_NEGU = 256
C_NEGL = 384
C_U01 = 512
C_SEL128 = 640
C_SEL4 = 768
C_TAU = 896
C_MASK4 = 960
C_MASKH_P = 1024
C_MASKH_S = 1536
NCONST = 1792

PP_NORMW = 0
PP_SCONV = 8
PP_DTB = 48
PP_ALOG = 49
PP_SSDD = 50
PP_SSDNW = 54
PP_S5D = 58
PP_ARE = 62
PP_AIM = 78
PP_LDT = 94
PP_MCONV = 110
PP_MIB = 130
PP_MFB = 131
PP_MNW = 132
PP_MSKIP = 136
PP_HL0 = 140
PP_HL1 = 144
PP_HNW = 148
NPP = 152

WB = 4096
NBUF = 3


def _host_consts():
    c = np.zeros((128, NCONST), np.float32)
    i = np.arange(128)[:, None]
    j = np.arange(128)[None, :]
    c[:, C_ID:C_ID + 128] = (i == j)
    c[:, C_ONES:C_ONES + 128] = 1.0
    c[:, C_NEGU:C_NEGU + 128] = np.where(j >= i, 0.0, NEG)
    c[:, C_NEGL:C_NEGL + 128] = np.where(j <= i, 0.0, NEG)
    c[:, C_U01:C_U01 + 128] = (j >= i)
    c[127, C_SEL128:C_SEL128 + 128] = 1.0
    c[3, C_SEL4:C_SEL4 + 128] = 1.0
    c[:, C_TAU:C_TAU + 64] = np.arange(1, 65)[None, :]
    c[:, C_MASK4:C_MASK4 + 64] = (np.arange(64) % 4 != 0)[None, :]
    c[:, C_MASKH_P:C_MASKH_P + 512] = (np.arange(512) % 64 != 0)[None, :]
    c[:, C_MASKH_S:C_MASKH_S + 256] = (np.arange(256) % 4 != 0)[None, :]
    return c


def _col(v, ntile):
    return np.ascontiguousarray(np.asarray(v, np.float32).reshape(ntile, 128).T)


def _host_pp(inp):
    pp = np.zeros((L, 128, NPP), np.float32)
    for l in range(L):
        p = pp[l]
        p[:, PP_NORMW:PP_NORMW + 8] = _col(inp["norm_w"][l], 8)
        cw = inp["ssd_conv_w"][l]
        cb = inp["ssd_conv_b"][l]
        blocks = [(0, 128), (128, 128), (256, 128), (384, 128), (512, 64), (576, 64), (640, 64), (704, 64)]
        for bi, (c0, n) in enumerate(blocks):
            for jj in range(4):
                p[0:n, PP_SCONV + bi * 5 + jj] = cw[jj, c0:c0 + n]
            p[0:n, PP_SCONV + bi * 5 + 4] = cb[c0:c0 + n]
        p[0:8, PP_DTB] = inp["ssd_dt_bias"][l]
        p[0:8, PP_ALOG] = inp["ssd_A_log"][l]
        p[:, PP_SSDD:PP_SSDD + 4] = _col(np.repeat(inp["ssd_D"][l], 64), 4)
        p[:, PP_SSDNW:PP_SSDNW + 4] = _col(inp["ssd_norm_w"][l], 4)
        p[:, PP_S5D:PP_S5D + 4] = _col(inp["s5_D"][l], 4)
        p[:, PP_ARE:PP_ARE + 16] = _col(inp["s5_A_re"][l].reshape(-1), 16)
        p[:, PP_AIM:PP_AIM + 16] = _col(inp["s5_A_im"][l].reshape(-1), 16)
        p[:, PP_LDT:PP_LDT + 16] = _col(np.repeat(inp["s5_log_dt"][l], 64), 16)
        mw = inp["ml_conv_w"][l]
        mb = inp["ml_conv_b"][l]
        for bi in range(4):
            for jj in range(4):
                p[:, PP_MCONV + bi * 5 + jj] = mw[jj, bi * 128:(bi + 1) * 128]
            p[:, PP_MCONV + bi * 5 + 4] = mb[bi * 128:(bi + 1) * 128]
        p[0:4, PP_MIB] = inp["ml_i_bias"][l]
        p[0:4, PP_MFB] = inp["ml_f_bias"][l]
        p[:, PP_MNW:PP_MNW + 4] = _col(inp["ml_norm_w"][l], 4)
        p[:, PP_MSKIP:PP_MSKIP + 4] = _col(inp["ml_skip"][l], 4)
        p[:, PP_HL0:PP_HL0 + 4] = _col(inp["hg_lb_logits"][0], 4)
        p[:, PP_HL1:PP_HL1 + 4] = _col(inp["hg_lb_logits"][1], 4)
        p[:, PP_HNW:PP_HNW + 4] = _col(inp["hg_norm_w"][l], 4)
    return pp


def _host_s5_bd(inp):
    bbd = np.zeros((L, 2, 128, 4, 512), np.float32)
    cbd = np.zeros((L, 2, 128, 16, 128), np.float32)
    for l in range(L):
        for ri, (bn, cn) in enumerate((("s5_B_re", "s5_C_re"), ("s5_B_im", "s5_C_im"))):
            B = inp[bn][l]
            C = inp[cn][l]
            for g in range(32):
                ct = g // 8
                gl8 = g % 8
                sl = gl8 // 2
                g2 = gl8 % 2
                st = ct * 4 + sl
                bbd[l, ri, gl8 * 16:(gl8 + 1) * 16, ct, sl * 128 + g2 * 64: sl * 128 + (g2 + 1) * 64] = B[g].T
                cbd[l, ri, g2 * 64:(g2 + 1) * 64, st, gl8 * 16:(gl8 + 1) * 16] = C[g].T
    return bbd, cbd


def build(n_ptiles=16, with_sample=True, stages=("ssd", "s5", "ml", "hg", "merge", "proj")):
    nc = bass.Bass("TRN2", target_bir_lowering=False)
    NT = n_ptiles * 128

    def din(name, shape):
        return nc.dram_tensor(name, list(shape), F32, kind="ExternalInput").ap()

    def dout(name, shape):
        return nc.dram_tensor(name, list(shape), F32, kind="ExternalOutput").ap()

    xp_d = din("xp", [NT, D])
    xs_d = din("xs", [NSEQ_S * QS, D])
    st_sconv = din("st_sconv", [L, NSEQ_S * 3, 768])
    st_ssd = din("st_ssd", [L, NSEQ_S, 8, 64, 64])
    st_s5re = din("st_s5re", [L, NSEQ_S, 2048])
    st_s5im = din("st_s5im", [L, NSEQ_S, 2048])
    st_mconv = din("st_mconv", [L, NSEQ_S * 3, 512])
    st_mC = din("st_mC", [L, NSEQ_S, 4, 128, 128])
    st_mn = din("st_mn", [L, NSEQ_S, 4, 128])
    st_mm = din("st_mm", [L, NSEQ_S, 4])
    st_hg = din("st_hg", [L, NSEQ_S, 4, 128, 128])
    w_in = din("w_in", [L, D, D_IN])
    w_br = din("w_br", [L, 4, 512, D])
    w_out = din("w_out", [L, D, D])
    w_glu = din("w_glu", [L, 512, 512])
    w_mq = din("w_mq", [L, 4, 128, 128])
    w_mk = din("w_mk", [L, 4, 128, 128])
    w_mv = din("w_mv", [L, 4, 128, 128])
    bbd_d = din("bbd", [L, 2, 128, 4, 512])
    cbd_d = din("cbd", [L, 2, 128, 16, 128])
    const_d = din("consts", [128, NCONST])
    pp_d = din("pp", [L, 128, NPP])
    fnw_d = din("fnw", [1, D])

    y_p = dout("y_p", [NT, D])
    y_s = dout("y_s", [NSEQ_S * QS, D])
    o_sconv_p = dout("o_sconv_p", [L, 3, 768])
    o_sconv_s = dout("o_sconv_s", [L, NSEQ_S * 3, 768])
    o_ssd_p = dout("o_ssd_p", [L, 8, 64, 64])
    o_ssd_s = dout("o_ssd_s", [L, NSEQ_S, 8, 64, 64])
    o_s5re_p = dout("o_s5re_p", [L, 16, 128])
    o_s5re_s = dout("o_s5re_s", [L, NSEQ_S, 2048])
    o_s5im_p = dout("o_s5im_p", [L, 16, 128])
    o_s5im_s = dout("o_s5im_s", [L, NSEQ_S, 2048])
    o_mconv_p = dout("o_mconv_p", [L, 3, 512])
    o_mconv_s = dout("o_mconv_s", [L, NSEQ_S * 3, 512])
    o_mC_p = dout("o_mC_p", [L, 4, 128, 128])
    o_mC_s = dout("o_mC_s", [L, NSEQ_S, 4, 128, 128])
    o_mn_p = dout("o_mn_p", [L, 4, 128])
    o_mn_s = dout("o_mn_s", [L, NSEQ_S, 4, 128])
    o_mm_p = dout("o_mm_p", [L, 1, 4])
    o_mm_s = dout("o_mm_s", [L, NSEQ_S, 4])
    o_hg_p = dout("o_hg_p", [L, 4, 128, 128])
    o_hg_s = dout("o_hg_s", [L, NSEQ_S, 4, 128, 128])

    P = Prog(nc)
    P.alloc_psum(8)
    ARENA_N = 9216 + 512
    P.arena = P.sb("arena", [128, ARENA_N])
    P.arena_n = ARENA_N

    HT_P = P.sb("hT_p", [128, L, 2, 256])
    CT = P.sb("consts_t", [128, NCONST])
    PPT = P.sb("pp_t", [128, L, NPP])
    FNW = P.sb("fnw_t", [128, D])
    WBUF = [P.sb("wbuf%d" % i, [128, WB]) for i in range(NBUF)]
    wctr = [0]

    def wnext():
        b = WBUF[wctr[0] % NBUF]
        wctr[0] += 1
        return b

    ident = CT[:, C_ID:C_ID + 128]
    ones = CT[:, C_ONES:C_ONES + 128]

    AH = P.sb("ssdA", [128, L])
    LB = P.sb("hg_lb", [128, L, 4])
    OML = P.sb("hg_oml", [128, L, 4])
    COS = P.sb("s5cos", [128, L, 16, 64])
    SIN = P.sb("s5sin", [128, L, 16, 64])
    RHO = P.sb("s5rho", [128, L, 16])
    CR = P.sb("s5cr", [128, L, 16])
    CI = P.sb("s5ci", [128, L, 16])
    E2R = P.sb("s5e2r", [128, L, 16, 64])
    E2I = P.sb("s5e2i", [128, L, 16, 64])

    XTK = P.sb("x_tok", [128, D])
    PTK = [P.sb("ptk%d" % i, [128, 520]) for i in range(2)]
    XN = P.sb("xnT", [128, 8, 128])
    PJ_Z = P.sb("pj_z", [128, 4, 128])
    XPS = P.sb("xp_ssd", [128, 8, 131])
    PJ_DT = P.sb("pj_dt", [128, 128])
    PJ_U = P.sb("pj_u", [128, 4, 128])
    PJ_GATE = P.sb("pj_gate", [128, 4, 128])
    XPM = P.sb("xp_ml", [128, 4, 131])
    PJ_CZ = P.sb("pj_cz", [128, 4, 128])
    PJ_CO = P.sb("pj_co", [128, 4, 128])
    PJ_CI = P.sb("pj_ci", [128, 128])
    PJ_CF = P.sb("pj_cf", [128, 128])
    PJ_DF = P.sb("pj_df", [128, 4, 128])
    PJ_DI = P.sb("pj_di", [128, 4, 128])
    PJ_DQ = P.sb("pj_dq", [128, 4, 128])
    PJ_DG = P.sb("pj_dg", [128, 4, 128])
    YB = [P.sb("ybr%d" % i, [128, 4, 128]) for i in range(4)]
    MRGK = P.sb("merged_tok", [128, D])

    HIST_S = P.sb("hist_s", [128, L, 8, 3])
    HIST_M = P.sb("hist_m", [128, L, 4, 3])
    S5R_P = P.sb("s5r_p", [128, L, 16])
    S5I_P = P.sb("s5i_p", [128, L, 16])
    MC_P = P.sb("mC_p", [128, L, 4, 128])
    MN_P = P.sb("mn_p", [128, L, 4, 2])
    MM_P = P.sb("mm_p", [128, L, 4])
    HG_P = P.sb("hg_p", [128, L, 4, 128])
    HT_S = P.sb("hT_s", [128, 2, 256])
    HNAT = P.sb("hnat", [128, 8, 64])
    S5R_S = P.sb("s5r_s", [128, 16, NSEQ_S])
    S5I_S = P.sb("s5i_s", [128, 16, NSEQ_S])
    MC_S = P.sb("mC_s", [128, 4, 128])
    MN_S = P.sb("mn_s", [128, 4, 2])
    MM_S = P.sb("mm_s", [128, 4])
    HG_S = P.sb("hg_s", [128, 4, 128])

    def pc(l, col, n=1, rows=128):
        return PPT[0:rows, l, col:col + n]

    P.dma(CT[:], const_d[:, :])
    P.dma(PPT[:], pp_d.rearrange("l p c -> p l c"))
    P.dma(FNW[:], fnw_d[0:1, :].partition_broadcast(128))
    for t_, v_ in ((HIST_S, 0.0), (HIST_M, 0.0), (S5R_P, 0.0), (S5I_P, 0.0), (MC_P, 0.0),
                   (MN_P, 0.0), (MM_P, 0.0), (HG_P, 0.0)):
        P.memset(t_[:], v_)
    P.memset(HT_P[:], 0.0)
    for t_ in (XPS, XPM, PJ_DT, PJ_CI, PJ_CF, P.arena):
        P.memset(t_[:], 0.0)

    def range_reduce(a, tmpf, tmpi):
        P.ts(tmpf, a, 1.0 / TWO_PI, ALU.mult)
        P.copy(tmpi, tmpf)
        P.copy(tmpf, tmpi)
        P.stt(a, tmpf, -TWO_PI, a, ALU.mult, ALU.add)
        P.ts(tmpf, a, math.pi, ALU.is_gt, TWO_PI, ALU.mult)
        P.tt(a, a, tmpf, ALU.subtract)
        P.ts(tmpf, a, -math.pi, ALU.is_lt, TWO_PI, ALU.mult)
        P.tt(a, a, tmpf, ALU.add)
        P.ts(a, a, math.pi, ALU.min, -math.pi, ALU.max)

    for l in range(L):
        P.aoff = 0
        P.act(AH[0:8, l:l + 1], pc(l, PP_ALOG, 1, 8), AF.Exp)
        P.ts(AH[0:8, l:l + 1], AH[0:8, l:l + 1], -1.0, ALU.mult)
        if l == 0:
            P.memset(LB[:, 0, :], 0.0)
        else:
            dl_ = P.al(4)
            P.tt(dl_, pc(l, PP_HL1, 4), pc(l, PP_HL0, 4), ALU.subtract)
            P.act(LB[:, l, :], dl_, AF.Sigmoid)
        P.ts(OML[:, l, :], LB[:, l, :], -1.0, ALU.mult, 1.0, ALU.add)
        dt = P.al(16)
        P.act(dt, pc(l, PP_LDT, 16), AF.Exp)
        lrdt = P.al(16)
        P.tt(lrdt, pc(l, PP_ARE, 16), dt, ALU.mult)
        P.act(RHO[:, l, :], lrdt, AF.Exp)
        th = P.al(16)
        P.tt(th, pc(l, PP_AIM, 16), dt, ALU.mult)
        ang = P.al(16, 64)
        tmpf = P.al(16, 64)
        tau = CT[:, C_TAU:C_TAU + 64]
        P.tt(ang, th.unsqueeze(2).to_broadcast([128, 16, 64]), tau.unsqueeze(1).to_broadcast([128, 16, 64]),
             ALU.mult)
        ang2 = P.al(16, 64)
        P.ts(ang2, ang, math.pi / 2.0, ALU.add)
        ti3 = P.al(16, 64).bitcast(I32)
        range_reduce(ang, tmpf, ti3)
        range_reduce(ang2, tmpf, ti3)
        P.act(SIN[:, l, :, :], ang, AF.Sin)
        P.act(COS[:, l, :, :], ang2, AF.Sin)
        abr = P.al(16)
        abi = P.al(16)
        P.tt(abr, RHO[:, l, :], COS[:, l, :, 0], ALU.mult)
        P.tt(abi, RHO[:, l, :], SIN[:, l, :, 0], ALU.mult)
        am1 = P.al(16)
        P.ts(am1, abr, -1.0, ALU.add)
        lr = pc(l, PP_ARE, 16)
        li = pc(l, PP_AIM, 16)
        den = P.al(16)
        t0 = P.al(16)
        P.tt(den, lr, lr, ALU.mult)
        P.tt(t0, li, li, ALU.mult)
        P.tt(den, den, t0, ALU.add)
        P.recip(den, den)
        t1 = P.al(16)
        P.tt(t0, am1, lr, ALU.mult)
        P.tt(t1, abi, li, ALU.mult)
        P.tt(t0, t0, t1, ALU.add)
        P.tt(CR[:, l, :], t0, den, ALU.mult)
        P.tt(t0, abi, lr, ALU.mult)
        P.tt(t1, am1, li, ALU.mult)
        P.tt(t0, t0, t1, ALU.subtract)
        P.tt(CI[:, l, :], t0, den, ALU.mult)
        crb_ = CR[:, l, :].unsqueeze(2).to_broadcast([128, 16, 64])
        cib_ = CI[:, l, :].unsqueeze(2).to_broadcast([128, 16, 64])
        P.tt(ang, COS[:, l, :, :], crb_, ALU.mult)
        P.tt(ang2, SIN[:, l, :, :], cib_, ALU.mult)
        P.tt(E2R[:, l, :, :], ang, ang2, ALU.add)
        P.tt(ang, COS[:, l, :, :], cib_, ALU.mult)
        P.tt(ang2, SIN[:, l, :, :], crb_, ALU.mult)
        P.tt(E2I[:, l, :, :], ang, ang2, ALU.subtract)

    def rmsnorm_fm(src, n, T, wcol_l, wcol, dst, dmodel):
        sq = P.al(n, T)
        P.act(sq, src, AF.Square)
        ps = P.ps()
        for k in range(n):
            P.mm(ps[:, 0:T], ones, sq[:, k, :], start=(k == 0), stop=(k == n - 1))
        rstd = P.al(T)
        P.act(rstd, ps[:, 0:T], AF.Sqrt, bias=EPSB[:, 0:1], scale=1.0 / dmodel)
        P.recip(rstd, rstd)
        for k in range(n):
            P.stt(dst[:, k, :], src[:, k, :], pc(wcol_l, wcol + k), rstd, ALU.mult, ALU.mult)

    EPSB = P.sb("epsb", [128, 2])
    P.memset(EPSB[:, 0:1], EPS)
    P.memset(EPSB[:, 1:2], 1.0)

    def bc3(ap2, n_mid=None, n_last=None):
        rows = ap2.shape[0]
        if n_last is not None:
            return ap2.unsqueeze(2).to_broadcast([rows, ap2.shape[1], n_last])
        return ap2.unsqueeze(1).to_broadcast([rows, n_mid, ap2.shape[1]])

    def stream(dst_view, src):
        P.dma(dst_view, src)

    def proj_blocks(l, T, nseq, Q):
        w_l = w_in[l].rearrange("(kt p) c -> p kt c", p=128)
        xps4 = XPS[:, :, 0:nseq * (Q + 3)].rearrange("p b (s q) -> p b s q", s=nseq)
        xpm4 = XPM[:, :, 0:nseq * (Q + 3)].rearrange("p b (s q) -> p b s q", s=nseq)

        def seqv(ps_ap):
            return ps_ap.rearrange("p (s q) -> p s q", s=nseq)

        groups = []

        def ev_act(dst_fn, func, bias=None, scale=1.0):
            def f(ps_ap, bi, rows):
                P.act(dst_fn(bi, rows), ps_ap, func, bias=(bias(rows) if bias else 0.0), scale=scale)
            return f

        def simple(col0, tile, func):
            blks = [(i * 128, 128, None) for i in range(4)]
            groups.append((col0, 512, blks,
                           (lambda ps2, t=tile, f=func: P.act(t[:, :, 0:T], ps2[:, 0:4 * T].rearrange("p (i t) -> p i t", i=4), f))))

        simple(O_AZ, PJ_Z, AF.Silu)
        groups.append((O_XBC, 512, [(i * 128, 128, None) for i in range(4)],
                       (lambda ps2: P.act(xps4[:, 0:4, :, 3:3 + Q],
                                          ps2[:, 0:4 * T].rearrange("p (i s q) -> p i s q", i=4, s=nseq), AF.Copy))))
        blks = [(j * 64, 64, (lambda ps_ap, bi, rows, j=j: P.act(xps4[0:rows, 4 + j, :, 3:3 + Q], seqv(ps_ap), AF.Copy)))
                for j in range(4)]
        blks.append((256, 8, (lambda ps_ap, bi, rows: P.act(PJ_DT[0:rows, 0:T], ps_ap, AF.Copy))))
        groups.append((O_XBC + 512, 264, blks))
        simple(O_U, PJ_U, AF.Copy)
        simple(O_GATE, PJ_GATE, AF.Silu)
        groups.append((O_CX, 512, [(i * 128, 128, None) for i in range(4)],
                       (lambda ps2: P.act(xpm4[:, 0:4, :, 3:3 + Q],
                                          ps2[:, 0:4 * T].rearrange("p (i s q) -> p i s q", i=4, s=nseq), AF.Copy))))
        simple(O_CZ, PJ_CZ, AF.Silu)
        simple(O_CO, PJ_CO, AF.Sigmoid)
        groups.append((O_CI, 8, [
            (0, 4, (lambda ps_ap, bi, rows: P.act(PJ_CI[0:rows, 0:T], ps_ap, AF.Identity, bias=pc(l, PP_MIB, 1, 4)))),
            (4, 4, (lambda ps_ap, bi, rows: P.act(PJ_CF[0:rows, 0:T], ps_ap, AF.Copy)))]))
        simple(O_DF, PJ_DF, AF.Sigmoid)
        simple(O_DI, PJ_DI, AF.Copy)
        simple(O_DQ, PJ_DQ, AF.Silu)
        simple(O_DG, PJ_DG, AF.Silu)

        tags = [0, 0, 0, 1, 1, 2, 2, 2, 2, 3, 3, 3, 3]
        assert len(tags) == len(groups)

        def run_group(grp, gi):
            col0, ncols, blks = grp[0], grp[1], grp[2]
            gevac = grp[3] if len(grp) > 3 else None
            wb = wnext()
            wv = wb[:, 0:8 * ncols].rearrange("p (k c) -> p k c", k=8)
            stream(wv, w_l[:, :, col0:col0 + ncols])
            ps = P.ps()
            for kt in range(8):
                P.mm(ps[0:T, 0:ncols], XN[:, kt, 0:T], wv[:, kt, 0:ncols], start=(kt == 0), stop=(kt == 7))
            tk = PTK[gi % 2]
            P.copy(tk[0:T, 0:ncols], ps[0:T, 0:ncols], eng=("act" if gi % 2 == 0 else "dve"))
            for b0 in range(0, len(blks), 4):
                sub = blks[b0:b0 + 4]
                ps2 = P.ps()
                for bj, (co, n, evac) in enumerate(sub):
                    P.tr(ps2[0:n, bj * T:(bj + 1) * T], tk[0:T, co:co + n], ident[0:T, 0:T])
                if gevac is not None:
                    gevac(ps2)
                else:
                    for bj, (co, n, evac) in enumerate(sub):
                        evac(ps2[0:n, bj * T:(bj + 1) * T], None, n)

        return [(tags[gi], (lambda g=grp, gi=gi: run_group(g, gi))) for gi, grp in enumerate(groups)]

    pending = []

    def pump(n=1):
        for _ in range(n):
            if pending:
                pending.pop(0)[1]()

    def flush_upto(tag):
        while pending and pending[0][0] <= tag:
            pending.pop(0)[1]()

    def conv_fm(xp4, nblk_rows, l, ppbase, acc4, Q):
        for bi, rows in enumerate(nblk_rows):
            c = ppbase + bi * 5
            P.ts(acc4[0:rows, bi], xp4[0:rows, bi, :, 0:Q], pc(l, c, 1, rows), ALU.mult,
                 pc(l, c + 4, 1, rows), ALU.add)
            for j in range(1, 4):
                P.stt(acc4[0:rows, bi], xp4[0:rows, bi, :, j:j + Q], pc(l, c + j, 1, rows), acc4[0:rows, bi],
                      ALU.mult, ALU.add)

    def ssd_branch(l, T, nseq, Q, sample, first, last, core_out):
        xps4 = XPS[:, :, 0:nseq * (Q + 3)].rearrange("p b (s q) -> p b s q", s=nseq)
        rows8 = [128] * 4 + [64] * 4
        acc = P.al(8, nseq, Q)
        conv_fm(xps4, rows8, l, PP_SCONV, acc, Q)
        xc = P.al(8, T)
        accf = acc.rearrange("p b s q -> p b (s q)")
        P.act(xc[:, 0:4, :], accf[:, 0:4, :], AF.Silu)
        P.act(xc[0:64, 4:8, :], accf[0:64, 4:8, :], AF.Silu)
        if SSD_CUT == 1:
            P.memset(YB[0][:], 0.0)
            return
        if not sample:
            P.copy(HIST_S[:, l, :, :], xps4[:, :, 0, Q:Q + 3], eng="act")
        dte = P.al(T)
        P.act(dte[0:8], PJ_DT[0:8, 0:T], AF.Exp, bias=pc(l, PP_DTB, 1, 8))
        dtT = P.al(T)
        P.act(dtT[0:8], dte[0:8], AF.Ln, bias=EPSB[0:8, 1:2])
        aT = P.al(T)
        P.ts(aT[0:8], dtT[0:8], AH[0:8, l:l + 1], ALU.mult)
        acT = P.al(T)
        if sample:
            P.scan(acT[0:8], CT[0:8, C_MASK4:C_MASK4 + T], aT[0:8], 0.0)
        else:
            P.scan(acT[0:8], CT[0:8, C_ONES:C_ONES + T], aT[0:8], 0.0)
        if SSD_CUT == 2:
            P.memset(YB[0][:], 0.0)
            return
        ytT = P.al(4, T)
        mark = P.aoff
        for s in range(nseq):
            P.aoff = mark
            c0 = s * Q
            sl = slice(c0, c0 + Q)
            if sample:
                hT = HT_S[0:64]
                P.dma(HNAT[0:64], st_ssd[l, s].rearrange("h p n -> p h n"))
                psx = P.ps()
                for g in range(2):
                    for r in range(4):
                        P.tr(psx[0:64, (g * 4 + r) * 64:(g * 4 + r + 1) * 64], HNAT[0:64, 4 * g + r, :], ident[0:64, 0:64])
                P.copy(HT_S[0:64].rearrange("p g c -> p (g c)"), psx[0:64, 0:512])
            else:
                hT = HT_P[0:64, l]
            psA = P.ps()
            for i in range(4):
                P.tr(psA[0:Q, i * 128:(i + 1) * 128], xc[:, i, sl], ident)
            psB = P.ps()
            for g in range(2):
                P.tr(psB[0:Q, g * 64:(g + 1) * 64], xc[0:64, 4 + g, sl], ident[0:64, 0:64])
            P.tr(psB[0:Q, 128:136], dtT[0:8, sl], ident[0:8, 0:8])
            P.tr(psB[0:Q, 136:144], acT[0:8, sl], ident[0:8, 0:8])
            btok = P.al(128)
            dtac = P.al(16)
            P.copy(btok[0:Q], psB[0:Q, 0:128], eng="act")
            P.copy(dtac[0:Q], psB[0:Q, 128:144], eng="act")
            dt_tok = dtac[0:Q, 0:8]
            ac_tok = dtac[0:Q, 8:16]
            xdt = P.al(8, 64)
            P.tt(xdt[0:Q], psA[0:Q, :].rearrange("p (h c) -> p h c", h=8), bc3(dt_tok, n_last=64), ALU.mult)
            if SSD_CUT == 3:
                break
            adiag = P.al(8, Q)
            P.tt(adiag[0:Q], bc3(ident[0:Q, 0:Q], n_mid=8), bc3(ac_tok, n_last=Q), ALU.mult)
            pump(3)
            nb = 2 if Q == 128 else 1
            hb = 8 // nb
            psR = [P.ps() for _ in range(nb)]
            for b_ in range(nb):
                P.mm(psR[b_][:, 0:hb * Q].rearrange("p (h t) -> p h t", h=hb), ones[0:Q, :],
                     adiag[0:Q, b_ * hb:(b_ + 1) * hb, :])
            alast = P.al(8)
            for b_ in range(nb):
                P.copy(alast[:, b_ * hb:(b_ + 1) * hb],
                       psR[b_][:, 0:hb * Q].rearrange("p (h t) -> p h t", h=hb)[:, :, Q - 1], eng="act")
            dec = P.al(8, Q)
            for b_ in range(nb):
                P.tt(dec[0:Q, b_ * hb:(b_ + 1) * hb, :],
                     psR[b_][0:Q, 0:hb * Q].rearrange("p (h t) -> p h t", h=hb),
                     bc3(ac_tok[:, b_ * hb:(b_ + 1) * hb], n_last=Q), ALU.subtract)
            P.tt(dec[0:Q], dec[0:Q], bc3(CT[0:Q, C_NEGU:C_NEGU + Q], n_mid=8), ALU.add)
            P.act(dec[0:Q], dec[0:Q], AF.Exp)
            if SSD_CUT == 4:
                break
            psC = P.ps()
            for g in range(2):
                P.mm(psC[0:Q, g * Q:(g + 1) * Q], xc[0:64, 4 + g, sl], xc[0:64, 6 + g, sl])
            MT = P.al(8, Q)
            for g in range(2):
                P.tt(MT[0:Q, 4 * g:4 * g + 4, :], dec[0:Q, 4 * g:4 * g + 4, :],
                     bc3(psC[0:Q, g * Q:(g + 1) * Q], n_mid=4), ALU.mult)
            pump(3)
            psY = P.ps()
            for h in range(8):
                P.mm(psY[0:Q, h * 64:(h + 1) * 64], MT[0:Q, h, :], xdt[0:Q, h, :])
            psS = P.ps()
            for g in range(2):
                P.mm(psS[0:Q, g * 256:(g + 1) * 256], xc[0:64, 6 + g, sl], hT[:, g, :])
            eac = P.al(8)
            P.act(eac[0:Q], ac_tok, AF.Exp)
            ytok = P.al(8, 64)
            P.tt(ytok[0:Q], psS[0:Q, :].rearrange("p (h c) -> p h c", h=8), bc3(eac[0:Q], n_last=64), ALU.mult)
            P.tt(ytok[0:Q], ytok[0:Q], psY[0:Q, :].rearrange("p (h c) -> p h c", h=8), ALU.add)
            if SSD_CUT == 6:
                break
            elast = P.al(8)
            P.act(elast, alast, AF.Exp)
            if SSD_CUT == 61:
                break
            toend = P.al(8)
            P.tt(toend[0:Q], alast[0:Q], ac_tok, ALU.subtract)
            P.act(toend[0:Q], toend[0:Q], AF.Exp)
            xe = P.al(8, 64)
            P.tt(xe[0:Q], xdt[0:Q], bc3(toend[0:Q], n_last=64), ALU.mult)
            if SSD_CUT == 62:
                break
            pump(3)
            psH = P.ps()
            for g in range(2):
                P.mm(psH[0:64, g * 256:(g + 1) * 256], btok[0:Q, g * 64:(g + 1) * 64],
                     xe[0:Q, 4 * g:4 * g + 4, :].rearrange("p h c -> p (h c)"))
            if SSD_CUT == 63:
                break
            hsc = P.al(2, 256)
            for g in range(2):
                for r in range(4):
                    P.act(hsc[0:64, g, r * 64:(r + 1) * 64], hT[:, g, r * 64:(r + 1) * 64], AF.Copy,
                          scale=elast[0:64, 4 * g + r:4 * g + r + 1])
            for g in range(2):
                P.tt(hT[:, g, :], hsc[0:64, g, :], psH[0:64, g * 256:(g + 1) * 256], ALU.add)
            pump(1)
            psT = P.ps()
            if SSD_CUT == 64:
                break
            yf = ytok[0:Q].rearrange("p h c -> p (h c)")
            for i in range(4):
                P.tr(psT[:, i * Q:(i + 1) * Q], yf[:, i * 128:(i + 1) * 128], ident[0:Q, 0:Q])
            P.copy(ytT[:, :, sl], psT[:, 0:4 * Q].rearrange("p (i q) -> p i q", i=4), eng="act")
            if SSD_CUT == 8:
                break
            if sample or last:
                pso = P.ps()
                for g in range(2):
                    for r in range(4):
                        P.tr(pso[0:64, (4 * g + r) * 64:(4 * g + r + 1) * 64], hT[:, g, r * 64:(r + 1) * 64],
                             ident[0:64, 0:64])
                hout = P.al(8, 64)
                P.copy(hout[0:64], pso[0:64, :].rearrange("p (h n) -> p h n", h=8))
                dst = o_ssd_s[l, s] if sample else o_ssd_p[l]
                P.dma(dst.rearrange("h p n -> p h n"), hout[0:64], q="act")
        P.aoff = mark
        if sample or last:
            cvi = P.al(8, nseq, 3)
            if sample:
                P.copy(cvi, xps4[:, :, :, Q:Q + 3])
            else:
                P.copy(cvi[:, :, 0, :], HIST_S[:, l, :, :])
            n3 = nseq * 3
            psv = [P.ps(), P.ps()]
            cols = [0, 128, 256, 384, 512, 576, 640, 704]
            for bi in range(8):
                rows = rows8[bi]
                pv = psv[0] if cols[bi] < 512 else psv[1]
                cc = cols[bi] % 512
                P.tr(pv[0:n3, cc:cc + rows], cvi[0:rows, bi].rearrange("p s j -> p (s j)"), ident[0:rows, 0:rows])
            cvo = P.al(768)
            P.copy(cvo[0:n3, 0:512], psv[0][0:n3, 0:512])
            P.copy(cvo[0:n3, 512:768], psv[1][0:n3, 0:256])
            dst = o_sconv_s[l] if sample else o_sconv_p[l]
            P.dma(dst, cvo[0:n3, :], q="act")
        yg = P.al(4, T)
        for i in range(4):
            P.stt(yg[:, i, :], xc[:, i, :], pc(l, PP_SSDD + i), ytT[:, i, :], ALU.mult, ALU.add)
        P.tt(yg, yg, PJ_Z[:, :, 0:T], ALU.mult)
        rmsnorm_fm(yg, 4, T, l, PP_SSDNW, YB[0][:, :, 0:T], 512.0)

    def s5_branch(l, T, nseq, Q, sample, last):
        wb = wnext()
        Bv = wb[:, 0:4096].rearrange("p (r c m) -> p r c m", r=2, c=4)
        stream(Bv[:, 0], bbd_d[l, 0])
        stream(Bv[:, 1], bbd_d[l, 1])
        wc = wnext()
        Cv = wc[:, 0:4096].rearrange("p (r s m) -> p r s m", r=2, s=16)
        stream(Cv[:, 0], cbd_d[l, 0])
        stream(Cv[:, 1], cbd_d[l, 1])
        wg = wnext()
        Gv = wg[:, 0:2048].rearrange("p (c m) -> p c m", c=4)
        stream(Gv, w_glu[l].rearrange("(c p) m -> p c m", p=128))
        if sample:
            nat = P.al(2048)
            for (src, dstt) in ((st_s5re, S5R_S), (st_s5im, S5I_S)):
                P.dma(nat[0:NSEQ_S], src[l])
                psx = P.ps()
                for st in range(16):
                    P.tr(psx[:, st * 16:(st + 1) * 16], nat[0:NSEQ_S, st * 128:(st + 1) * 128], ident[0:16, 0:16])
                P.copy(dstt[:].rearrange("p s b -> p (s b)"), psx[:, 0:256])
            subs = [(0, NSEQ_S, QS)]
            SR, SI = S5R_S[:], S5I_S[:]
        else:
            subs = [(0, 1, 64), (64, 1, 64)]
            SR, SI = S5R_P[:, l], S5I_P[:, l]
        HR = P.al(16, T)
        NHI = P.al(16, T)
        mark = P.aoff
        for (c0, ns, q) in subs:
            P.aoff = mark
            W = ns * q
            wre = P.al(16, W)
            wim = P.al(16, W)
            t1 = P.al(4, W)
            t2 = P.al(4, W)
            for ct in range(4):
                ps = P.ps()
                for sl_ in range(4):
                    P.mm(ps[:, sl_ * 128:sl_ * 128 + W], Bv[:, 0, ct, sl_ * 128:(sl_ + 1) * 128], PJ_U[:, ct, c0:c0 + W])
                    P.mm(ps[:, sl_ * 128 + 64:sl_ * 128 + 64 + W], Bv[:, 1, ct, sl_ * 128:(sl_ + 1) * 128],
                         PJ_U[:, ct, c0:c0 + W])
                p4 = ps[:, :].rearrange("p (s r w) -> p s r w", s=4, r=2)
                pr = p4[:, :, 0, 0:W]
                pi = p4[:, :, 1, 0:W]
                stv = slice(ct * 4, ct * 4 + 4)
                if ns == 1:
                    e2r = E2R[:, l, stv, 0:q]
                    e2i = E2I[:, l, stv, 0:q]
                    v = lambda a: a
                else:
                    e2r = E2R[:, l, stv, 0:q].unsqueeze(2).to_broadcast([128, 4, ns, q])
                    e2i = E2I[:, l, stv, 0:q].unsqueeze(2).to_broadcast([128, 4, ns, q])
                    v = lambda a: a.rearrange("p s (b q) -> p s b q", b=ns)
                P.tt(v(t1), v(pr), e2r, ALU.mult)
                P.tt(v(t2), v(pi), e2i, ALU.mult)
                P.tt(wre[:, stv, :], t1, t2, ALU.subtract)
                P.tt(v(t1), v(pi), e2r, ALU.mult)
                P.tt(v(t2), v(pr), e2i, ALU.mult)
                P.tt(wim[:, stv, :], t1, t2, ALU.add)
            gre = P.al(16, W)
            gim = P.al(16, W)
            if ns == 1:
                for st in range(16):
                    rb = RHO[:, l, st:st + 1].to_broadcast([128, W])
                    P.scan(gre[:, st, :], rb, wre[:, st, :], SR[:, st:st + 1])
                    P.scan(gim[:, st, :], rb, wim[:, st, :], SI[:, st:st + 1])
            else:
                tmp = P.al(16, ns)
                w4r = wre.rearrange("p s (b q) -> p s b q", b=ns)
                w4i = wim.rearrange("p s (b q) -> p s b q", b=ns)
                P.tt(tmp, SR[:], bc3(RHO[:, l, :], n_last=ns), ALU.mult)
                P.tt(w4r[:, :, :, 0], w4r[:, :, :, 0], tmp, ALU.add)
                P.tt(tmp, SI[:], bc3(RHO[:, l, :], n_last=ns), ALU.mult)
                P.tt(w4i[:, :, :, 0], w4i[:, :, :, 0], tmp, ALU.add)
                rm3 = nat[:, 0:1024].rearrange("p (s w) -> p s w", s=16)
                P.tt(rm3, RHO[:, l, :].unsqueeze(2).to_broadcast([128, 16, 64]),
                     CT[:, C_MASK4:C_MASK4 + 64].unsqueeze(1).to_broadcast([128, 16, 64]), ALU.mult)
                rm = nat[:, 0:1024]
                P.scan(gre.rearrange("p s w -> p (s w)"), rm, wre.rearrange("p s w -> p (s w)"), 0.0)
                P.scan(gim.rearrange("p s w -> p (s w)"), rm, wim.rearrange("p s w -> p (s w)"), 0.0)
            if ns == 1:
                cs = COS[:, l, :, 0:q]
                sn = SIN[:, l, :, 0:q]
                v = lambda a: a
            else:
                cs = COS[:, l, :, 0:q].unsqueeze(2).to_broadcast([128, 16, ns, q])
                sn = SIN[:, l, :, 0:q].unsqueeze(2).to_broadcast([128, 16, ns, q])
                v = lambda a: a.rearrange("p s (b q) -> p s b q", b=ns)
            t1 = wre
            t2 = wim
            P.tt(v(t1), v(gre), cs, ALU.mult)
            P.tt(v(t2), v(gim), sn, ALU.mult)
            P.tt(HR[:, :, c0:c0 + W], t1, t2, ALU.subtract)
            P.tt(v(t1), v(gre), sn, ALU.mult)
            P.tt(v(t2), v(gim), cs, ALU.mult)
            P.stt(NHI[:, :, c0:c0 + W], t1, -1.0, t2, ALU.mult, ALU.subtract)
            if ns == 1:
                P.copy(SR, HR[:, :, c0 + W - 1], eng="act")
                P.ts(SI, NHI[:, :, c0 + W - 1], -1.0, ALU.mult)
            else:
                h4 = HR[:, :, c0:c0 + W].rearrange("p s (b q) -> p s b q", b=ns)
                n4 = NHI[:, :, c0:c0 + W].rearrange("p s (b q) -> p s b q", b=ns)
                P.copy(SR[:], h4[:, :, :, q - 1], eng="act")
                P.ts(SI[:], n4[:, :, :, q - 1], -1.0, ALU.mult)
        P.aoff = mark
        if sample or last:
            for (srct, dstd_s, dstd_p) in ((SR, o_s5re_s, o_s5re_p), (SI, o_s5im_s, o_s5im_p)):
                if sample:
                    so = P.al(2048)
                    for b_ in range(4):
                        pq = P.ps()
                        for st in range(4 * b_, 4 * b_ + 4):
                            P.tr(pq[0:NSEQ_S, (st % 4) * 128:(st % 4 + 1) * 128], srct[:, st, :], ident)
                        P.copy(so[0:NSEQ_S, b_ * 512:(b_ + 1) * 512], pq[0:NSEQ_S, 0:512])
                    P.dma(dstd_s[l], so[0:NSEQ_S, :], q="act")
                else:
                    pso = P.ps()
                    P.tr(pso[0:16, 0:128], srct, ident)
                    so = P.al(128)
                    P.copy(so[0:16], pso[0:16, 0:128])
                    P.dma(dstd_p[l], so[0:16], q="act")
        psY = P.ps()
        for ct in range(4):
            k = 0
            for sl_ in range(4):
                st = ct * 4 + sl_
                P.mm(psY[:, ct * T:(ct + 1) * T], Cv[:, 0, st, :], HR[:, st, :], start=(k == 0), stop=False)
                k += 1
                P.mm(psY[:, ct * T:(ct + 1) * T], Cv[:, 1, st, :], NHI[:, st, :], start=False, stop=(sl_ == 3))
        yb = P.al(4, T)
        for ct in range(4):
            P.stt(yb[:, ct, :], PJ_U[:, ct, 0:T], pc(l, PP_S5D + ct), psY[:, ct * T:(ct + 1) * T], ALU.mult, ALU.add)
        gg = P.al(4, T)
        P.act(gg, yb, AF.Gelu_apprx_tanh)
        psG = P.ps()
        for co in range(4):
            for ct in range(4):
                P.mm(psG[:, co * T:(co + 1) * T], Gv[:, ct, co * 128:(co + 1) * 128], gg[:, ct, :],
                     start=(ct == 0), stop=(ct == 3))
        sg = P.al(4, T)
        P.act(sg, psG[:, 0:4 * T].rearrange("p (c t) -> p c t", c=4), AF.Sigmoid)
        P.tt(sg, sg, gg, ALU.mult)
        P.tt(YB[1][:, :, 0:T], sg, PJ_GATE[:, :, 0:T], ALU.mult)

    def ml_branch(l, T, nseq, Q, sample, last):
        wb = wnext()
        Wq = wb[:, 0:512].rearrange("p (h k) -> p h k", h=4)
        Wk = wb[:, 512:1024].rearrange("p (h k) -> p h k", h=4)
        Wv = wb[:, 1024:1536].rearrange("p (h k) -> p h k", h=4)
        stream(Wq, w_mq[l].rearrange("h c k -> c h k"))
        stream(Wk, w_mk[l].rearrange("h c k -> c h k"))
        stream(Wv, w_mv[l].rearrange("h c k -> c h k"))
        xpm4 = XPM[:, :, 0:nseq * (Q + 3)].rearrange("p b (s q) -> p b s q", s=nseq)
        acc = P.al(4, nseq, Q)
        conv_fm(xpm4, [128] * 4, l, PP_MCONV, acc, Q)
        xcm = P.al(4, T)
        P.act(xcm, acc.rearrange("p b s q -> p b (s q)"), AF.Silu)
        if not sample:
            P.copy(HIST_M[:, l, :, :], xpm4[:, :, 0, Q:Q + 3], eng="act")
        e_ = P.al(T)
        P.act(e_[0:4], PJ_CF[0:4, 0:T], AF.Exp, bias=NFB[0:4, l:l + 1], scale=-1.0)
        sp = P.al(T)
        P.act(sp[0:4], e_[0:4], AF.Ln, bias=EPSB[0:4, 1:2])
        lf = P.al(T)
        P.ts(lf[0:4], sp[0:4], -1.0, ALU.mult)
        bT = P.al(T)
        if sample:
            P.scan(bT[0:4], CT[0:4, C_MASK4:C_MASK4 + T], lf[0:4], 0.0)
        else:
            P.scan(bT[0:4], CT[0:4, C_ONES:C_ONES + T], lf[0:4], 0.0)
        rT = P.al(T)
        P.tt(rT[0:4], PJ_CI[0:4, 0:T], bT[0:4], ALU.subtract)
        hnT = P.al(4, T)
        mark = P.aoff
        sel = CT[:, C_SEL4:C_SEL4 + 128] if sample else CT[:, C_SEL128:C_SEL128 + 128]
        for s in range(nseq):
            P.aoff = mark
            c0 = s * Q
            sl = slice(c0, c0 + Q)
            if sample:
                Cst, Nst, Mst = MC_S[:], MN_S[:], MM_S[:]
                P.dma(MC_S[:], st_mC[l, s].rearrange("h k v -> k h v"))
                nat = P.al(128)
                P.dma(nat[0:4], st_mn[l, s])
                psx = P.ps()
                P.tr(psx[:, 0:4], nat[0:4, :], ident[0:4, 0:4])
                P.copy(MN_S[:, :, 0], psx[:, 0:4])
                P.copy(MN_S[:, :, 1], psx[:, 0:4])
                P.dma(MM_S[:], st_mm[l, s:s + 1, :].partition_broadcast(128))
            else:
                Cst, Nst, Mst = MC_P[:, l], MN_P[:, l], MM_P[:, l]
            tok = P.al(12)
            psg = P.ps()
            P.tr(psg[0:Q, 0:4], bT[0:4, sl], ident[0:4, 0:4])
            P.tr(psg[0:Q, 4:8], rT[0:4, sl], ident[0:4, 0:4])
            P.copy(tok[0:Q, 0:8], psg[0:Q, 0:8], eng="act")
            b_tok = tok[0:Q, 0:4]
            r_tok = tok[0:Q, 4:8]
            m_t = tok[0:Q, 8:12]
            psq = P.ps()
            psk = P.ps()
            for h in range(4):
                P.mm(psq[:, h * Q:(h + 1) * Q], Wq[:, h, :], xcm[:, h, sl])
                P.mm(psk[:, h * Q:(h + 1) * Q], Wk[:, h, :], xcm[:, h, sl])
            qT = P.al(4, Q)
            kT = P.al(4, Q)
            P.copy(qT, psq[:, 0:4 * Q].rearrange("p (h q) -> p h q", h=4), eng="act")
            P.act(kT, psk[:, 0:4 * Q].rearrange("p (h q) -> p h q", h=4), AF.Copy, scale=128.0 ** -0.5)
            pskt = P.ps()
            psvt = P.ps()
            for h in range(4):
                P.mm(pskt[0:Q, h * 128:(h + 1) * 128], xcm[:, h, sl], Wk[:, h, :])
                P.mm(psvt[0:Q, h * 128:(h + 1) * 128], xpm4[:, h, s, 3:3 + Q], Wv[:, h, :])
            vtok = P.al(4, 128)
            P.copy(vtok[0:Q], psvt[0:Q, :].rearrange("p (h v) -> p h v", h=4), eng="act")
            ktok = P.al(4, 128)
            P.copy(ktok[0:Q], pskt[0:Q, :].rearrange("p (h v) -> p h v", h=4), eng="act")
            if not sample:
                pump_merge(1)
            rdiag = P.al(4, Q)
            P.tt(rdiag[0:Q], bc3(ident[0:Q, 0:Q], n_mid=4), bc3(r_tok, n_last=Q), ALU.mult)
            psR = P.ps()
            P.mm(psR[:, 0:4 * Q].rearrange("p (h t) -> p h t", h=4), ones[0:Q, :], rdiag[0:Q])
            dl = P.al(4, Q)
            P.tt(dl[0:Q], psR[0:Q, 0:4 * Q].rearrange("p (h t) -> p h t", h=4), bc3(b_tok, n_last=Q), ALU.add)
            P.tt(dl[0:Q], dl[0:Q], bc3(CT[0:Q, C_NEGL:C_NEGL + Q], n_mid=4), ALU.add)
            mx = P.al(4)
            P.red(mx[0:Q], dl[0:Q], ALU.max)
            bm = P.al(4)
            P.tt(bm[0:Q], b_tok, Mst[0:Q], ALU.add)
            P.tt(m_t, bm[0:Q], mx[0:Q], ALU.max)
            negm = P.al(4)
            P.ts(negm[0:Q], m_t, -1.0, ALU.mult)
            Wt = P.al(4, Q)
            for h in range(4):
                P.act(Wt[0:Q, h, :], dl[0:Q, h, :], AF.Exp, bias=negm[0:Q, h:h + 1])
            psqk = P.ps()
            for h in range(4):
                P.mm(psqk[0:Q, h * Q:(h + 1) * Q], qT[:, h, :], kT[:, h, :])
            S = P.al(4, Q)
            P.tt(S[0:Q], Wt[0:Q], psqk[0:Q, 0:4 * Q].rearrange("p (h t) -> p h t", h=4), ALU.mult)
            dotin = P.al(4)
            P.red(dotin[0:Q], S[0:Q], ALU.add)
            psst = P.ps()
            for h in range(4):
                P.tr(psst[0:Q, h * Q:(h + 1) * Q], S[0:Q, h, :], ident[0:Q, 0:Q])
            ST = P.al(4, Q)
            P.copy(ST[0:Q], psst[0:Q, 0:4 * Q].rearrange("p (h t) -> p h t", h=4), eng="act")
            psn = P.ps()
            psi = P.ps()
            psqn = P.ps()
            for h in range(4):
                P.mm(psn[0:Q, h * 128:(h + 1) * 128], ST[0:Q, h, :], vtok[0:Q, h, :])
                P.mm(psi[0:Q, h * 128:(h + 1) * 128], qT[:, h, :], Cst[:, h, :])
                P.mm(psqn[0:Q, 2 * h:2 * h + 2], qT[:, h, :], Nst[:, h, :])
            scl = P.al(4)
            P.tt(scl[0:Q], bm[0:Q], m_t, ALU.subtract)
            P.act(scl[0:Q], scl[0:Q], AF.Exp)
            num = P.al(4, 128)
            P.tt(num[0:Q], psi[0:Q, :].rearrange("p (h v) -> p h v", h=4), bc3(scl[0:Q], n_last=128), ALU.mult)
            P.tt(num[0:Q], num[0:Q], psn[0:Q, :].rearrange("p (h v) -> p h v", h=4), ALU.add)
            dot = P.al(4)
            P.tt(dot[0:Q], psqn[0:Q, 0:8].rearrange("p (h t) -> p h t", h=4)[:, :, 0], scl[0:Q], ALU.mult)
            P.tt(dot[0:Q], dot[0:Q], dotin[0:Q], ALU.add)
            emn = P.al(4)
            emn2 = P.al(4)
            P.act(emn[0:Q], negm[0:Q], AF.Exp)
            P.ts(emn2[0:Q], dot[0:Q], -1.0, ALU.mult)
            P.tt(dot[0:Q], dot[0:Q], emn2[0:Q], ALU.max)
            P.tt(dot[0:Q], dot[0:Q], emn[0:Q], ALU.max)
            P.recip(dot[0:Q], dot[0:Q])
            P.tt(num[0:Q], num[0:Q], bc3(dot[0:Q], n_last=128), ALU.mult)
            if not sample:
                pump_merge(1)
            pso = P.ps()
            for h in range(4):
                P.tr(pso[0:Q, h * 128:(h + 1) * 128], PJ_CO[:, h, sl], ident)
            P.tt(num[0:Q], num[0:Q], pso[0:Q, :].rearrange("p (h v) -> p h v", h=4), ALU.mult)
            mean = P.al(4)
            P.red(mean[0:Q], num[0:Q], ALU.add)
            P.ts(mean[0:Q], mean[0:Q], 1.0 / 128.0, ALU.mult)
            P.tt(num[0:Q], num[0:Q], bc3(mean[0:Q], n_last=128), ALU.subtract)
            sq = P.al(4, 128)
            P.tt(sq[0:Q], num[0:Q], num[0:Q], ALU.mult)
            var = P.al(4)
            P.red(var[0:Q], sq[0:Q], ALU.add)
            P.act(var[0:Q], var[0:Q], AF.Sqrt, bias=EPSB[0:Q, 0:1], scale=1.0 / 128.0)
            P.recip(var[0:Q], var[0:Q])
            P.tt(num[0:Q], num[0:Q], bc3(var[0:Q], n_last=128), ALU.mult)
            pst = P.ps()
            for h in range(4):
                P.tr(pst[:, h * Q:(h + 1) * Q], num[0:Q, h, :], ident[0:Q, 0:Q])
            P.copy(hnT[:, :, sl], pst[:, 0:4 * Q].rearrange("p (h q) -> p h q", h=4), eng="act")
            psl = P.ps()
            P.mm(psl[:, 0:12], sel[0:Q, :], tok[0:Q, 0:12])
            L12 = P.al(12)
            P.copy(L12, psl[:, 0:12])
            lm = P.al(4)
            P.tt(lm, L12[:, 0:4], L12[:, 8:12], ALU.subtract)
            wend = P.al(4)
            P.tt(wend[0:Q], r_tok, lm[0:Q], ALU.add)
            P.act(wend[0:Q], wend[0:Q], AF.Exp)
            P.ts(wend[0:Q], wend[0:Q], 128.0 ** -0.5, ALU.mult)
            carry = P.al(4)
            P.tt(carry, lm, Mst, ALU.add)
            P.act(carry, carry, AF.Exp)
            kw = P.al(4, 128)
            P.tt(kw[0:Q], ktok[0:Q], bc3(wend[0:Q], n_last=128), ALU.mult)
            psc = P.ps()
            psdn = P.ps()
            for h in range(4):
                P.mm(psc[:, h * 128:(h + 1) * 128], kw[0:Q, h, :], vtok[0:Q, h, :])
                P.mm(psdn[:, 2 * h:2 * h + 2], kw[0:Q, h, :], ones[0:Q, 0:2])
            csc = P.al(4, 128)
            nsc = P.al(4, 2)
            for h in range(4):
                P.act(csc[:, h, :], Cst[:, h, :], AF.Copy, scale=carry[:, h:h + 1])
                P.act(nsc[:, h, :], Nst[:, h, :], AF.Copy, scale=carry[:, h:h + 1])
            P.tt(Cst, csc, psc[:, :].rearrange("p (h v) -> p h v", h=4), ALU.add)
            P.tt(Nst, nsc, psdn[:, 0:8].rearrange("p (h t) -> p h t", h=4), ALU.add)
            P.copy(Mst, L12[:, 8:12], eng="act")
            if sample or last:
                P.dma((o_mC_s[l, s] if sample else o_mC_p[l]).rearrange("h k v -> k h v"), Cst, q="act")
                psx = P.ps()
                ncp = P.al(4)
                P.copy(ncp, Nst[:, :, 0], eng="act")
                P.tr(psx[0:4, 0:128], ncp, ident)
                no = P.al(128)
                P.copy(no[0:4], psx[0:4, 0:128])
                P.dma(o_mn_s[l, s] if sample else o_mn_p[l], no[0:4], q="act")
                mo = P.al(4)
                P.copy(mo[0:1], Mst[0:1])
                P.dma(o_mm_s[l, s:s + 1, :] if sample else o_mm_p[l], mo[0:1], q="act")
        P.aoff = mark
        if sample or last:
            cvi = P.al(4, nseq, 3)
            if sample:
                P.copy(cvi, xpm4[:, :, :, Q:Q + 3])
            else:
                P.copy(cvi[:, :, 0, :], HIST_M[:, l, :, :])
            n3 = nseq * 3
            psv = P.ps()
            for bi in range(4):
                P.tr(psv[0:n3, bi * 128:(bi + 1) * 128], cvi[:, bi].rearrange("p s j -> p (s j)"), ident)
            cvo = P.al(512)
            P.copy(cvo[0:n3], psv[0:n3, 0:512])
            P.dma(o_mconv_s[l] if sample else o_mconv_p[l], cvo[0:n3], q="act")
        yy = P.al(4, T)
        for h in range(4):
            P.ts(yy[:, h, :], hnT[:, h, :], pc(l, PP_MNW + h), ALU.mult)
            P.stt(yy[:, h, :], xcm[:, h, :], pc(l, PP_MSKIP + h), yy[:, h, :], ALU.mult, ALU.add)
        P.tt(YB[2][:, :, 0:T], yy, PJ_CZ[:, :, 0:T], ALU.mult)

    NFB = P.sb("nfb", [128, L])
    for l in range(L):
        P.ts(NFB[0:4, l:l + 1], pc(l, PP_MFB, 1, 4), -1.0, ALU.mult)

    def hg_branch(l, T, nseq, Q, sample, last):
        q = QS if sample else 64
        nsub = T // q
        fg = P.al(4, T)
        for ct in range(4):
            P.ts(fg[:, ct, :], PJ_DF[:, ct, 0:T], OML[:, l, ct:ct + 1], ALU.mult, LB[:, l, ct:ct + 1], ALU.add)
        lf = P.al(4, T)
        P.act(lf, fg, AF.Ln)
        kT = P.al(4, T)
        P.ts(kT, fg, -1.0, ALU.mult, 1.0, ALU.add)
        qT = P.al(4, T)
        P.ts(qT, PJ_DQ[:, :, 0:T], 128.0 ** -0.5, ALU.mult)
        gT = P.al(4, T)
        mk = CT[:, C_MASKH_S:C_MASKH_S + 4 * T] if sample else CT[:, C_MASKH_P:C_MASKH_P + 4 * T]
        P.scan(gT.rearrange("p c t -> p (c t)"), mk, lf.rearrange("p c t -> p (c t)"), 0.0)
        eg = P.al(4, T)
        P.act(eg, gT, AF.Exp)
        qe = P.al(4, T)
        P.tt(qe, qT, eg, ALU.mult)
        onT = P.al(4, T)
        mark = P.aoff
        for s in range(nsub):
            P.aoff = mark
            c0 = s * q
            sl = slice(c0, c0 + q)
            if sample:
                Sst = HG_S[:]
                P.dma(HG_S[:], st_hg[l, s].rearrange("h k v -> k h v"))
            else:
                Sst = HG_P[:, l]
            gref = gT[:, :, c0 + q // 2]
            glast = gT[:, :, c0 + q - 1]
            Dx = P.al(3, 4, q)
            P.tt(Dx[:, 0], gT[:, :, sl], bc3(gref, n_last=q), ALU.subtract)
            P.ts(Dx[:, 1], Dx[:, 0], -1.0, ALU.mult)
            P.tt(Dx[:, 2], bc3(glast, n_last=q), gT[:, :, sl], ALU.subtract)
            P.act(Dx, Dx, AF.Exp)
            qg = P.al(4, q)
            kg = P.al(4, q)
            kd = P.al(4, q)
            P.tt(qg, qT[:, :, sl], Dx[:, 0], ALU.mult)
            P.tt(kg, kT[:, :, sl], Dx[:, 1], ALU.mult)
            P.tt(kd, kT[:, :, sl], Dx[:, 2], ALU.mult)
            psa = P.ps()
            for h in range(4):
                P.mm(psa[0:q, h * q:(h + 1) * q], kg[:, h, :], qg[:, h, :])
            attT = P.al(4, q)
            P.tt(attT[0:q], psa[0:q, 0:4 * q].rearrange("p (h t) -> p h t", h=4),
                 bc3(CT[0:q, C_U01:C_U01 + q], n_mid=4), ALU.mult)
            psv = P.ps()
            psk = P.ps()
            for h in range(4):
                P.tr(psv[0:q, h * 128:(h + 1) * 128], PJ_DI[:, h, sl], ident)
                P.tr(psk[0:q, h * 128:(h + 1) * 128], kd[:, h, :], ident)
            vtok = P.al(4, 128)
            kdtok = P.al(4, 128)
            P.copy(vtok[0:q], psv[0:q, :].rearrange("p (h v) -> p h v", h=4), eng="act")
            P.copy(kdtok[0:q], psk[0:q, :].rearrange("p (h v) -> p h v", h=4), eng="act")
            pump_merge(2)
            pso = P.ps()
            for h in range(4):
                P.mm(pso[0:q, h * 128:(h + 1) * 128], attT[0:q, h, :], vtok[0:q, h, :], start=True, stop=False)
                P.mm(pso[0:q, h * 128:(h + 1) * 128], qe[:, h, sl], Sst[:, h, :], start=False, stop=True)
            o = P.al(4, 128)
            P.copy(o[0:q], pso[0:q, :].rearrange("p (h v) -> p h v", h=4), eng="act")
            sq = P.al(4, 128)
            P.tt(sq[0:q], o[0:q], o[0:q], ALU.mult)
            ssq = P.al(4)
            P.red(ssq[0:q], sq[0:q], ALU.add)
            P.act(ssq[0:q], ssq[0:q], AF.Sqrt, bias=EPSB[0:q, 0:1], scale=1.0 / 128.0)
            P.recip(ssq[0:q], ssq[0:q])
            P.tt(o[0:q], o[0:q], bc3(ssq[0:q], n_last=128), ALU.mult)
            pst = P.ps()
            for h in range(4):
                P.tr(pst[:, h * q:(h + 1) * q], o[0:q, h, :], ident[0:q, 0:q])
            P.copy(onT[:, :, sl], pst[:, 0:4 * q].rearrange("p (h t) -> p h t", h=4), eng="act")
            egl = P.al(4)
            P.act(egl, glast, AF.Exp)
            pss = P.ps()
            for h in range(4):
                P.mm(pss[:, h * 128:(h + 1) * 128], kdtok[0:q, h, :], vtok[0:q, h, :])
            ssc = P.al(4, 128)
            for h in range(4):
                P.act(ssc[:, h, :], Sst[:, h, :], AF.Copy, scale=egl[:, h:h + 1])
            P.tt(Sst, ssc, pss[:, :].rearrange("p (h v) -> p h v", h=4), ALU.add)
            if sample or (last and s == nsub - 1):
                P.dma((o_hg_s[l, s] if sample else o_hg_p[l]).rearrange("h k v -> k h v"), Sst, q="act")
        P.aoff = mark
        yy = P.al(4, T)
        for ct in range(4):
            P.ts(yy[:, ct, :], onT[:, ct, :], pc(l, PP_HNW + ct), ALU.mult)
        P.tt(YB[3][:, :, 0:T], yy, PJ_DG[:, :, 0:T], ALU.mult)

    def norm_in(l, T):
        if NORM_CUT == 0:
            return
        sq = P.al(D)
        P.act(sq[0:T], XTK[0:T, :], AF.Square)
        ss = P.al(1)
        P.red(ss[0:T], sq[0:T], ALU.add)
        P.act(ss[0:T], ss[0:T], AF.Sqrt, bias=EPSB[0:T, 0:1], scale=1.0 / D)
        P.recip(ss[0:T], ss[0:T])
        if NORM_CUT == 1:
            return
        P.ts(sq[0:T], XTK[0:T, :], ss[0:T, 0:1], ALU.mult)
        if NORM_CUT == 2:
            return
        for half in range(2):
            ps = P.ps()
            for j in range(4):
                kt = half * 4 + j
                P.tr(ps[:, j * T:(j + 1) * T], sq[0:T, kt * 128:(kt + 1) * 128], ident[0:T, 0:T])
            if NORM_CUT == 3:
                continue
            for j in range(4):
                kt = half * 4 + j
                if NORM_CUT == 15:
                    P.copy(XN[:, kt, 0:T], ps[:, j * T:(j + 1) * T])
                elif NORM_CUT == 16:
                    P.ts(XN[:, kt, 0:T], ps[:, j * T:(j + 1) * T], pc(l, PP_NORMW + kt), ALU.mult)
                elif NORM_CUT == 13:
                    P.copy(XN[:, kt, 0:T], ps[:, j * T:(j + 1) * T], eng=("act" if j % 2 == 0 else "dve"))
                elif NORM_CUT == 14:
                    P.act(XN[:, kt, 0:T], ps[:, j * T:(j + 1) * T], AF.Copy, scale=EPSB[:, 1:2])
                elif (NORM_CUT == 11) or (NORM_CUT not in (12,) and j % 2 == 0):
                    P.act(XN[:, kt, 0:T], ps[:, j * T:(j + 1) * T], AF.Copy, scale=pc(l, PP_NORMW + kt))
                else:
                    P.ts(XN[:, kt, 0:T], ps[:, j * T:(j + 1) * T], pc(l, PP_NORMW + kt), ALU.mult)

    mctx = {"todo": [], "l": 0, "T": 128}

    def merge_branch(l, T, i):
        w_l = w_in[l].rearrange("(kt p) c -> p kt c", p=128)
        gt = []
        for half in range(2):
            wb = wnext()
            wv = wb[:, 0:4096].rearrange("p (k c) -> p k c", k=8)
            c0 = O_MERGE + i * 1024 + half * 512
            stream(wv, w_l[:, :, c0:c0 + 512])
            ps = P.ps()
            for kt in range(8):
                P.mm(ps[0:T, :], XN[:, kt, 0:T], wv[:, kt, :], start=(kt == 0), stop=(kt == 7))
            g = PTK[half]
            P.act(g[0:T, 0:512], ps[0:T, :], AF.Sigmoid)
            gt.append(g)
        wb = wnext()
        wv = wb[:, 0:4096].rearrange("p (c d) -> p c d", c=4)
        stream(wv, w_br[l, i].rearrange("(c p) d -> p c d", p=128))
        for half in range(2):
            hs = slice(half * 512, (half + 1) * 512)
            ps = P.ps()
            for ct in range(4):
                P.mm(ps[0:T, :], YB[i][:, ct, 0:T], wv[:, ct, hs], start=(ct == 0), stop=(ct == 3))
            g = gt[half]
            if i == 0:
                P.tt(MRGK[0:T, hs], ps[0:T, :], g[0:T, 0:512], ALU.mult)
            else:
                P.tt(g[0:T, 0:512], ps[0:T, :], g[0:T, 0:512], ALU.mult)
                P.tt(MRGK[0:T, hs], MRGK[0:T, hs], g[0:T, 0:512], ALU.add)

    def pump_merge(max_i):
        if mctx["todo"] and mctx["todo"][0] <= max_i:
            i = mctx["todo"].pop(0)
            merge_branch(mctx["l"], mctx["T"], i)

    def merge_out(l, T):
        while mctx["todo"]:
            merge_branch(l, T, mctx["todo"].pop(0))
        mt8 = P.al(8, T)
        for half in range(2):
            ps = P.ps()
            for j in range(4):
                kt = half * 4 + j
                P.tr(ps[:, j * T:(j + 1) * T], MRGK[0:T, kt * 128:(kt + 1) * 128], ident[0:T, 0:T])
            P.copy(mt8[:, half * 4:half * 4 + 4, :], ps[:, 0:4 * T].rearrange("p (j t) -> p j t", j=4),
                   eng=("act" if half == 0 else "dve"))
        for half in range(2):
            hs = slice(half * 512, (half + 1) * 512)
            wb = wnext()
            wv = wb[:, 0:4096].rearrange("p (k c) -> p k c", k=8)
            stream(wv, w_out[l].rearrange("(kt p) d -> p kt d", p=128)[:, :, hs])
            ps = P.ps()
            for kt in range(8):
                P.mm(ps[0:T, :], mt8[:, kt, :], wv[:, kt, :], start=(kt == 0), stop=(kt == 7))
            P.tt(XTK[0:T, hs], XTK[0:T, hs], ps[0:T, :], ALU.add)

    tiles = [("p", i) for i in range(n_ptiles)]
    if with_sample:
        tiles.append(("s", 0))
    for (kind, ti) in tiles:
        sample = kind == "s"
        T = NSEQ_S * QS if sample else 128
        nseq = NSEQ_S if sample else 1
        Q = QS if sample else 128
        first = (ti == 0)
        last = (ti == n_ptiles - 1)
        P.aoff = 0
        src = xs_d[:, :] if sample else xp_d[ti * 128:(ti + 1) * 128, :]
        P.dma(XTK[0:T, :], src)
        for l in range(L):
            P.aoff = 0
            xps4 = XPS[:, :, 0:nseq * (Q + 3)].rearrange("p b (s q) -> p b s q", s=nseq)
            xpm4 = XPM[:, :, 0:nseq * (Q + 3)].rearrange("p b (s q) -> p b s q", s=nseq)
            if sample:
                for (src_d, xp4, nb_, cols) in ((st_sconv, xps4, 8, [0, 128, 256, 384, 512, 576, 640, 704]),
                                                (st_mconv, xpm4, 4, [0, 128, 256, 384])):
                    nch = 768 if nb_ == 8 else 512
                    nat = P.al(nch)
                    P.dma(nat[0:48], src_d[l])
                    for bi in range(nb_):
                        rows = 128 if (nb_ == 4 or bi < 4) else 64
                        psx = P.ps()
                        P.tr(psx[0:rows, 0:48], nat[0:48, cols[bi]:cols[bi] + rows], ident[0:48, 0:48])
                        P.copy(xp4[0:rows, bi, :, 0:3], psx[0:rows, 0:48].rearrange("p (s j) -> p s j", s=NSEQ_S))
            else:
                P.copy(xps4[:, :, 0, 0:3], HIST_S[:, l, :, :], eng="act")
                P.copy(xpm4[:, :, 0, 0:3], HIST_M[:, l, :, :], eng="act")
            P.aoff = 0
            norm_in(l, T)
            mctx["todo"] = [0, 1, 2, 3] if "merge" in stages else []
            mctx["l"] = l
            mctx["T"] = T
            P.cur_tag = "%s%d_proj" % (kind, ti)
            pending[:] = proj_blocks(l, T, nseq, Q) if "proj" in stages else []
            for bi_, (nm_, fn_) in enumerate((("ssd", lambda: ssd_branch(l, T, nseq, Q, sample, first, last, None)),
                                              ("s5", lambda: s5_branch(l, T, nseq, Q, sample, last)),
                                              ("ml", lambda: ml_branch(l, T, nseq, Q, sample, last)),
                                              ("hg", lambda: hg_branch(l, T, nseq, Q, sample, last)))):
                P.aoff = 0
                flush_upto(bi_)
                P.cur_tag = "%s%d_%s" % (kind, ti, nm_)
                if nm_ in stages:
                    fn_()
                else:
                    P.memset(YB[bi_][:], 0.0)
            P.aoff = 0
            P.cur_tag = "%s%d_merge" % (kind, ti)
            flush_upto(99)
            if "merge" in stages:
                merge_out(l, T)
        P.aoff = 0
        sq = P.al(D)
        P.act(sq[0:T], XTK[0:T, :], AF.Square)
        ss = P.al(1)
        P.red(ss[0:T], sq[0:T], ALU.add)
        P.act(ss[0:T], ss[0:T], AF.Sqrt, bias=EPSB[0:T, 0:1], scale=1.0 / D)
        P.recip(ss[0:T], ss[0:T])
        xo = P.al(D)
        P.stt(xo[0:T], XTK[0:T, :], ss[0:T, 0:1], FNW[0:T, :], ALU.mult, ALU.mult)
        dst = y_s[:, :] if sample else y_p[ti * 128:(ti + 1) * 128, :]
        P.dma(dst, xo[0:T], q="act")

    P.emit()
    return nc, P


_CACHE = {}


def _prep_inputs(inp, n_ptiles=16):
    f = lambda a: np.ascontiguousarray(np.asarray(a, np.float32))
    consts = _host_consts()
    pp = _host_pp(inp)
    bbd, cbd = _host_s5_bd(inp)
    shared = {
        "w_in": f(inp["w_in"]), "w_br": f(inp["w_branch"]), "w_out": f(inp["w_out"]),
        "w_glu": f(inp["s5_glu_w"]), "w_mq": f(inp["ml_wq"]), "w_mk": f(inp["ml_wk"]), "w_mv": f(inp["ml_wv"]),
        "bbd": bbd, "cbd": cbd, "consts": consts, "pp": pp,
        "fnw": f(inp["final_norm_w"]).reshape(1, D),
    }
    maps = []
    for c in range(NCORE):
        b0 = c * NSEQ_S
        sl = slice(b0, b0 + NSEQ_S)
        m = dict(shared)
        m["xp"] = f(inp["x_prompt"][c, :n_ptiles * 128])
        m["xs"] = f(inp["x_sample"][sl]).reshape(NSEQ_S * QS, D)
        m["st_sconv"] = f(inp["state_ssd_conv"][:, sl]).reshape(L, NSEQ_S * 3, 768)
        m["st_ssd"] = f(inp["state_ssd"][:, sl])
        m["st_s5re"] = f(inp["state_s5_re"][:, sl]).reshape(L, NSEQ_S, 2048)
        m["st_s5im"] = f(inp["state_s5_im"][:, sl]).reshape(L, NSEQ_S, 2048)
        m["st_mconv"] = f(inp["state_mlstm_conv"][:, sl]).reshape(L, NSEQ_S * 3, 512)
        m["st_mC"] = f(inp["state_mlstm_C"][:, sl])
        m["st_mn"] = f(inp["state_mlstm_n"][:, sl])
        m["st_mm"] = f(inp["state_mlstm_m"][:, sl])
        m["st_hg"] = f(inp["state_hgrn"][:, sl])
        maps.append(m)
    return maps


def _gather(res, n_ptiles=16):
    R = res
    cat1 = lambda k, shp: np.stack([r[k] for r in R], axis=1).reshape(shp)
    cats = lambda k, shp: np.concatenate([r[k].reshape((L, NSEQ_S) + r[k].shape[2:]) if False else r[k] for r in R], axis=1)
    y_p = np.stack([r["y_p"] for r in R], axis=0)
    y_s = np.concatenate([r["y_s"].reshape(NSEQ_S, QS, D) for r in R], axis=0)
    B = NCORE

    def P_(k, tail):
        return np.stack([r[k].reshape((L,) + tail) for r in R], axis=1)

    def S_(k, tail):
        return np.concatenate([r[k].reshape((L, NSEQ_S) + tail) for r in R], axis=1)

    outs = (
        y_p, y_s,
        P_("o_sconv_p", (3, 768)), S_("o_sconv_s", (3, 768)),
        P_("o_ssd_p", (8, 64, 64)), S_("o_ssd_s", (8, 64, 64)),
        P_("o_s5re_p", (32, 64)), S_("o_s5re_s", (32, 64)),
        P_("o_s5im_p", (32, 64)), S_("o_s5im_s", (32, 64)),
        P_("o_mconv_p", (3, 512)), S_("o_mconv_s", (3, 512)),
        P_("o_mC_p", (4, 128, 128)), S_("o_mC_s", (4, 128, 128)),
        P_("o_mn_p", (4, 128)), S_("o_mn_s", (4, 128)),
        P_("o_mm_p", (4,)), S_("o_mm_s", (4,)),
        P_("o_hg_p", (4, 128, 128)), S_("o_hg_s", (4, 128, 128)),
    )
    return tuple(np.ascontiguousarray(o.astype(np.float32)) for o in outs)


def kernel(**inputs):
    n_ptiles = inputs["x_prompt"].shape[1] // 128
    nc, _ = build(n_ptiles=n_ptiles)
    maps = _prep_inputs(inputs, n_ptiles)
    res = run_bass_kernel_spmd(nc, maps, core_ids=list(range(NCORE)))
    return _gather(res.results, n_ptiles)
```

```python
import contextlib
import math
import numpy as np
import concourse.bass as bass
import concourse.mybir as mybir
from concourse.bass_utils import run_bass_kernel_spmd

F32 = mybir.dt.float32
I32 = mybir.dt.int32
AF = mybir.ActivationFunctionType
ALU = mybir.AluOpType
AX = mybir.AxisListType

L = 2
D = 1024
NCORE = 8
SEQ = 2048
NSEQ_S = 16
QS = 4
EPS = 1e-6
D_IN = 10000
NEG = -1.0e30
SSD_CUT = 99
NORM_CUT = 99
SSD_VAR = 0
TWO_PI = 2.0 * math.pi


class _Op:
    __slots__ = ("eng", "fn", "deps", "is_dma", "idx", "sig", "needed")

    def __init__(self, eng, fn, is_dma):
        self.eng = eng
        self.fn = fn
        self.deps = set()
        self.is_dma = is_dma
        self.sig = None
        self.needed = False


def _region(ap):
    t = ap.tensor
    tn = type(t).__name__
    if not (tn.startswith("SBTensor") or tn.startswith("PSum")):
        return None
    fs = 1
    for s in list(t.shape)[1:]:
        fs *= int(s)
    off = int(ap.offset)
    p0 = off // fs
    f0 = off % fs
    dims = list(ap.ap)
    pstep, pcnt = int(dims[0][0]), int(dims[0][1])
    if pstep == 0 or pcnt == 1:
        p1 = p0 + 1
    else:
        assert pstep == fs, (t.name, pstep, fs)
        p1 = p0 + pcnt
    ext = 1
    for st, cn in dims[1:]:
        ext += (int(cn) - 1) * abs(int(st))
    return (t.name, p0, p1, f0, f0 + ext)


def _isap(x):
    return x is not None and not isinstance(x, (int, float))


class Prog:
    def __init__(self, nc):
        self.nc = nc
        self.ops = []
        self.acc = {}
        self.stack = contextlib.ExitStack()
        self.psum_banks = []
        self.psum_i = 0
        self.arena = None
        self.aoff = 0
        self.tags = []

    def sb(self, name, shape, dtype=F32):
        return self.stack.enter_context(self.nc.sbuf_tensor(name, list(shape), dtype))

    def alloc_psum(self, n=8):
        for i in range(n):
            self.psum_banks.append(
                self.stack.enter_context(self.nc.psum_tensor("psb%d" % i, [128, 512], F32)))

    def ps(self):
        t = self.psum_banks[self.psum_i % len(self.psum_banks)]
        self.psum_i += 1
        return t

    def al(self, *shape, rows=128):
        n = 1
        for s in shape:
            n *= s
        off = self.aoff
        self.aoff += n
        assert self.aoff <= self.arena_n, ("arena overflow", self.aoff)
        v = self.arena[0:rows, off:off + n]
        if len(shape) == 2:
            v = v.rearrange("p (a b) -> p a b", a=shape[0])
        elif len(shape) == 3:
            v = v.rearrange("p (a b c) -> p a b c", a=shape[0], b=shape[1])
        return v

    def op(self, eng, fn, reads, writes, is_dma=False):
        o = _Op(eng, fn, is_dma)
        o.idx = len(self.ops)
        self.tags.append(getattr(self, "cur_tag", ""))
        engkey = ("dma", o.idx) if is_dma else eng
        rr = [r for r in (_region(a) for a in reads if _isap(a)) if r]
        ww = [r for r in (_region(a) for a in writes if _isap(a)) if r]
        if eng == "pe":
            ww = [(n, 0, 128, 0, 512) if n.startswith("psb") else (n, p0, p1, f0, f1) for (n, p0, p1, f0, f1) in ww]
        for (n, p0, p1, f0, f1) in rr:
            for e in self.acc.get(n, ()):
                if e[5] and e[0] < p1 and p0 < e[1] and e[2] < f1 and f0 < e[3]:
                    o.deps.add(e[4])
        for (n, p0, p1, f0, f1) in ww:
            for e in self.acc.get(n, ()):
                if e[0] < p1 and p0 < e[1] and e[2] < f1 and f0 < e[3]:
                    o.deps.add(e[4])
        for (n, p0, p1, f0, f1) in rr:
            if n.startswith("psb"):
                for e in self.acc.get(n, ()):
                    if (not e[5]) and e[6] != engkey:
                        o.deps.add(e[4])
        for (n, p0, p1, f0, f1) in ww:
            lst = self.acc.setdefault(n, [])
            lst[:] = [e for e in lst if not (p0 <= e[0] and e[1] <= p1 and f0 <= e[2] and e[3] <= f1)]
            lst.append([p0, p1, f0, f1, o.idx, True, engkey])
        for (n, p0, p1, f0, f1) in rr:
            lst = self.acc.setdefault(n, [])
            if not is_dma:
                lst[:] = [e for e in lst if not ((not e[5]) and e[6] == engkey and p0 <= e[0] and e[1] <= p1
                                                 and f0 <= e[2] and e[3] <= f1)]
            lst.append([p0, p1, f0, f1, o.idx, False, engkey])
        o.deps.discard(o.idx)
        self.ops.append(o)
        return o

    def mm(self, out, lhsT, rhs, start=True, stop=True):
        rd = [lhsT, rhs] + ([] if start else [out])
        return self.op("pe", lambda e: e.matmul(out, lhsT, rhs, start=start, stop=stop), rd, [out])

    def tr(self, out, in_, ident):
        return self.op("pe", lambda e: e.transpose(out, in_, ident), [in_, ident], [out])

    def act(self, out, in_, func, bias=0.0, scale=1.0):
        rd = [in_, bias, scale]
        return self.op("act", lambda e: e.activation(out, in_, func, bias=bias, scale=scale), rd, [out])

    def tt(self, out, in0, in1, op, eng="dve"):
        return self.op(eng, lambda e: e.tensor_tensor(out, in0, in1, op), [in0, in1], [out])

    def ts(self, out, in0, s1, op0, s2=None, op1=None, eng="dve"):
        rd = [in0, s1, s2]
        if op1 is None:
            return self.op(eng, lambda e: e.tensor_scalar(out, in0, s1, None, op0), rd, [out])
        return self.op(eng, lambda e: e.tensor_scalar(out, in0, s1, s2, op0, op1), rd, [out])

    def stt(self, out, in0, scalar, in1, op0, op1):
        rd = [in0, in1, scalar]
        return self.op("dve", lambda e: e.scalar_tensor_tensor(out, in0, scalar, in1, op0, op1), rd, [out])

    def copy(self, out, in_, eng="dve"):
        if eng == "act":
            return self.op("act", lambda e: e.activation(out, in_, AF.Copy), [in_], [out])
        return self.op(eng, lambda e: e.tensor_copy(out, in_), [in_], [out])

    def red(self, out, in_, op, axis=AX.X):
        return self.op("dve", lambda e: e.tensor_reduce(out, in_, axis, op), [in_], [out])

    def scan(self, out, d0, d1, init, op0=ALU.mult, op1=ALU.add):
        rd = [d0, d1, init]
        return self.op("dve", lambda e: e.tensor_tensor_scan(out, d0, d1, init, op0, op1), rd, [out])

    def recip(self, out, in_):
        return self.op("dve", lambda e: e.reciprocal(out, in_), [in_], [out])

    def memset(self, out, v, eng="dve"):
        return self.op(eng, lambda e: e.memset(out, v), [], [out])

    def dma(self, out, in_, q="sp"):
        return self.op(q, lambda e: e.dma_start(out=out, in_=in_), [in_], [out], is_dma=True)

    def emit(self, n_dma_sems=12):
        nc = self.nc
        ops = self.ops
        for o in ops:
            if o.is_dma:
                o.needed = True
        engs = ["pe", "act", "dve", "pool", "sp"]
        stack = self.stack
        csem = {e: stack.enter_context(nc.semaphore("cs_" + e)) for e in engs}
        dsem = {e: [stack.enter_context(nc.semaphore("ds_%s%d" % (e, i))) for i in range(n_dma_sems)]
                for e in ("sp", "pool", "act")}
        dcount = {e: [0] * n_dma_sems for e in dsem}
        dlast = {e: [None] * n_dma_sems for e in dsem}
        drr = {e: 0 for e in dsem}
        for o in ops:
            if o.is_dma:
                k = drr[o.eng] % n_dma_sems
                drr[o.eng] += 1
                if dlast[o.eng][k] is not None:
                    o.deps.add(dlast[o.eng][k])
                dlast[o.eng][k] = o.idx
        for o in ops:
            if o.eng == "pe":
                o.deps = {d for d in o.deps if ops[d].eng != "pe"}
        for o in ops:
            for d in o.deps:
                ops[d].needed = True
        ccount = {e: 0 for e in engs}
        drr = {e: 0 for e in dsem}
        for o in ops:
            if o.is_dma:
                k = drr[o.eng] % n_dma_sems
                drr[o.eng] += 1
                dcount[o.eng][k] += 16
                o.sig = (dsem[o.eng][k], dcount[o.eng][k], ("d", o.eng, k))
            elif o.needed:
                ccount[o.eng] += 1
                o.sig = (csem[o.eng], ccount[o.eng], ("c", o.eng))
        per = {e: [o for o in ops if o.eng == e] for e in engs}
        self.trace = {e: [] for e in engs}
        self.stats = {e: len(per[e]) for e in engs}

        def run(engname, eng):
            waited = {}
            for o in per[engname]:
                need = {}
                for d in o.deps:
                    s, v, key = ops[d].sig
                    if waited.get(key, 0) >= v:
                        continue
                    if key not in need or need[key][1] < v:
                        need[key] = (s, v)
                for key, (s, v) in need.items():
                    eng.wait_ge(s, v)
                    waited[key] = v
                self.trace[engname].append(([(k_, v_[1]) for k_, v_ in need.items()], o.sig[2] if o.sig else None, o.idx))
                ins = o.fn(eng)
                if o.sig is not None:
                    ins.then_inc(o.sig[0], 16 if o.is_dma else 1)
            if engname in dsem:
                for k in range(n_dma_sems):
                    if dcount[engname][k] > 0 and waited.get(("d", engname, k), 0) < dcount[engname][k]:
                        eng.wait_ge(dsem[engname][k], dcount[engname][k])

        with nc.Block() as block:
            @block.tensor
            def _(e):
                run("pe", e)

            @block.scalar
            def _(e):
                run("act", e)

            @block.vector
            def _(e):
                run("dve", e)

            @block.gpsimd
            def _(e):
                run("pool", e)

            @block.sync
            def _(e):
                run("sp", e)
        self.stack.close()

    def check_deadlock(self):
        sem = {}
        pos = {e: 0 for e in self.trace}
        total = sum(len(v) for v in self.trace.values())
        done = 0
        while done < total:
            prog = False
            for e, lst in self.trace.items():
                while pos[e] < len(lst):
                    waits, sig, idx = lst[pos[e]]
                    if all(sem.get(k, 0) >= v for k, v in waits):
                        if sig is not None:
                            sem[sig] = sem.get(sig, 0) + (16 if sig[0] == "d" else 1)
                        pos[e] += 1
                        done += 1
                        prog = True
                    else:
                        break
            if not prog:
                return {e: (pos[e], self.trace[e][pos[e]] if pos[e] < len(self.trace[e]) else None) for e in self.trace}
        return None


O_AZ = 0
O_XBC = 512
O_DT = 1280
O_U = 1288
O_GATE = 1800
O_CX = 2312
O_CZ = 2824
O_CO = 3336
O_CI = 3848
O_CF = 3852
O_DF = 3856
O_DI = 4368
O_DQ = 4880
O_DG = 5392
O_MERGE = 5904

C_ID = 0
C_ONES = 128
C_NEGU = 256
C_NEGL = 384
C_U01 = 512
C_SEL128 = 640
C_SEL4 = 768
C_TAU = 896
C_MASK4 = 960
C_MASKH_P = 1024
C_MASKH_S = 1536
NCONST = 1792

PP_NORMW = 0
PP_SCONV = 8
PP_DTB = 48
PP_ALOG = 49
PP_SSDD = 50
PP_SSDNW = 54
PP_S5D = 58
PP_ARE = 62
PP_AIM = 78
PP_LDT = 94
PP_MCONV = 110
PP_MIB = 130
PP_MFB = 131
PP_MNW = 132
PP_MSKIP = 136
PP_HL0 = 140
PP_HL1 = 144
PP_HNW = 148
NPP = 152

WB = 4096
NBUF = 3


def _host_consts():
    c = np.zeros((128, NCONST), np.float32)
    i = np.arange(128)[:, None]
    j = np.arange(128)[None, :]
    c[:, C_ID:C_ID + 128] = (i == j)
    c[:, C_ONES:C_ONES + 128] = 1.0
    c[:, C_NEGU:C_NEGU + 128] = np.where(j >= i, 0.0, NEG)
    c[:, C_NEGL:C_NEGL + 128] = np.where(j <= i, 0.0, NEG)
    c[:, C_U01:C_U01 + 128] = (j >= i)
    c[127, C_SEL128:C_SEL128 + 128] = 1.0
    c[3, C_SEL4:C_SEL4 + 128] = 1.0
    c[:, C_TAU:C_TAU + 64] = np.arange(1, 65)[None, :]
    c[:, C_MASK4:C_MASK4 + 64] = (np.arange(64) % 4 != 0)[None, :]
    c[:, C_MASKH_P:C_MASKH_P + 512] = (np.arange(512) % 64 != 0)[None, :]
    c[:, C_MASKH_S:C_MASKH_S + 256] = (np.arange(256) % 4 != 0)[None, :]
    return c


def _col(v, ntile):
    return np.ascontiguousarray(np.asarray(v, np.float32).reshape(ntile, 128).T)


def _host_pp(inp):
    pp = np.zeros((L, 128, NPP), np.float32)
    for l in range(L):
        p = pp[l]
        p[:, PP_NORMW:PP_NORMW + 8] = _col(inp["norm_w"][l], 8)
        cw = inp["ssd_conv_w"][l]
        cb = inp["ssd_conv_b"][l]
        blocks = [(0, 128), (128, 128), (256, 128), (384, 128), (512, 64), (576, 64), (640, 64), (704, 64)]
        for bi, (c0, n) in enumerate(blocks):
            for jj in range(4):
                p[0:n, PP_SCONV + bi * 5 + jj] = cw[jj, c0:c0 + n]
            p[0:n, PP_SCONV + bi * 5 + 4] = cb[c0:c0 + n]
        p[0:8, PP_DTB] = inp["ssd_dt_bias"][l]
        p[0:8, PP_ALOG] = inp["ssd_A_log"][l]
        p[:, PP_SSDD:PP_SSDD + 4] = _col(np.repeat(inp["ssd_D"][l], 64), 4)
        p[:, PP_SSDNW:PP_SSDNW + 4] = _col(inp["ssd_norm_w"][l], 4)
        p[:, PP_S5D:PP_S5D + 4] = _col(inp["s5_D"][l], 4)
        p[:, PP_ARE:PP_ARE + 16] = _col(inp["s5_A_re"][l].reshape(-1), 16)
        p[:, PP_AIM:PP_AIM + 16] = _col(inp["s5_A_im"][l].reshape(-1), 16)
        p[:, PP_LDT:PP_LDT + 16] = _col(np.repeat(inp["s5_log_dt"][l], 64), 16)
        mw = inp["ml_conv_w"][l]
        mb = inp["ml_conv_b"][l]
        for bi in range(4):
            for jj in range(4):
                p[:, PP_MCONV + bi * 5 + jj] = mw[jj, bi * 128:(bi + 1) * 128]
            p[:, PP_MCONV + bi * 5 + 4] = mb[bi * 128:(bi + 1) * 128]
        p[0:4, PP_MIB] = inp["ml_i_bias"][l]
        p[0:4, PP_MFB] = inp["ml_f_bias"][l]
        p[:, PP_MNW:PP_MNW + 4] = _col(inp["ml_norm_w"][l], 4)
        p[:, PP_MSKIP:PP_MSKIP + 4] = _col(inp["ml_skip"][l], 4)
        p[:, PP_HL0:PP_HL0 + 4] = _col(inp["hg_lb_logits"][0], 4)
        p[:, PP_HL1:PP_HL1 + 4] = _col(inp["hg_lb_logits"][1], 4)
        p[:, PP_HNW:PP_HNW + 4] = _col(inp["hg_norm_w"][l], 4)
    return pp


def _host_s5_bd(inp):
    bbd = np.zeros((L, 2, 128, 4, 512), np.float32)
    cbd = np.zeros((L, 2, 128, 16, 128), np.float32)
    for l in range(L):
        for ri, (bn, cn) in enumerate((("s5_B_re", "s5_C_re"), ("s5_B_im", "s5_C_im"))):
            B = inp[bn][l]
            C = inp[cn][l]
            for g in range(32):
                ct = g // 8
                gl8 = g % 8
                sl = gl8 // 2
                g2 = gl8 % 2
                st = ct * 4 + sl
                bbd[l, ri, gl8 * 16:(gl8 + 1) * 16, ct, sl * 128 + g2 * 64: sl * 128 + (g2 + 1) * 64] = B[g].T
                cbd[l, ri, g2 * 64:(g2 + 1) * 64, st, gl8 * 16:(gl8 + 1) * 16] = C[g].T
    return bbd, cbd


def build(n_ptiles=16, with_sample=True, stages=("ssd", "s5", "ml", "hg", "merge", "proj")):
    nc = bass.Bass("TRN2", target_bir_lowering=False)
    NT = n_ptiles * 128

    def din(name, shape):
        return nc.dram_tensor(name, list(shape), F32, kind="ExternalInput").ap()

    def dout(name, shape):
        return nc.dram_tensor(name, list(shape), F32, kind="ExternalOutput").ap()

    xp_d = din("xp", [NT, D])
    xs_d = din("xs", [NSEQ_S * QS, D])
    st_sconv = din("st_sconv", [L, NSEQ_S * 3, 768])
    st_ssd = din("st_ssd", [L, NSEQ_S, 8, 64, 64])
    st_s5re = din("st_s5re", [L, NSEQ_S, 2048])
    st_s5im = din("st_s5im", [L, NSEQ_S, 2048])
    st_mconv = din("st_mconv", [L, NSEQ_S * 3, 512])
    st_mC = din("st_mC", [L, NSEQ_S, 4, 128, 128])
    st_mn = din("st_mn", [L, NSEQ_S, 4, 128])
    st_mm = din("st_mm", [L, NSEQ_S, 4])
    st_hg = din("st_hg", [L, NSEQ_S, 4, 128, 128])
    w_in = din("w_in", [L, D, D_IN])
    w_br = din("w_br", [L, 4, 512, D])
    w_out = din("w_out", [L, D, D])
    w_glu = din("w_glu", [L, 512, 512])
    w_mq = din("w_mq", [L, 4, 128, 128])
    w_mk = din("w_mk", [L, 4, 128, 128])
    w_mv = din("w_mv", [L, 4, 128, 128])
    bbd_d = din("bbd", [L, 2, 128, 4, 512])
    cbd_d = din("cbd", [L, 2, 128, 16, 128])
    const_d = din("consts", [128, NCONST])
    pp_d = din("pp", [L, 128, NPP])
    fnw_d = din("fnw", [1, D])

    y_p = dout("y_p", [NT, D])
    y_s = dout("y_s", [NSEQ_S * QS, D])
    o_sconv_p = dout("o_sconv_p", [L, 3, 768])
    o_sconv_s = dout("o_sconv_s", [L, NSEQ_S * 3, 768])
    o_ssd_p = dout("o_ssd_p", [L, 8, 64, 64])
    o_ssd_s = dout("o_ssd_s", [L, NSEQ_S, 8, 64, 64])
    o_s5re_p = dout("o_s5re_p", [L, 16, 128])
    o_s5re_s = dout("o_s5re_s", [L, NSEQ_S, 2048])
    o_s5im_p = dout("o_s5im_p", [L, 16, 128])
    o_s5im_s = dout("o_s5im_s", [L, NSEQ_S, 2048])
    o_mconv_p = dout("o_mconv_p", [L, 3, 512])
    o_mconv_s = dout("o_mconv_s", [L, NSEQ_S * 3, 512])
    o_mC_p = dout("o_mC_p", [L, 4, 128, 128])
    o_mC_s = dout("o_mC_s", [L, NSEQ_S, 4, 128, 128])
    o_mn_p = dout("o_mn_p", [L, 4, 128])
    o_mn_s = dout("o_mn_s", [L, NSEQ_S, 4, 128])
    o_mm_p = dout("o_mm_p", [L, 1, 4])
    o_mm_s = dout("o_mm_s", [L, NSEQ_S, 4])
    o_hg_p = dout("o_hg_p", [L, 4, 128, 128])
    o_hg_s = dout("o_hg_s", [L, NSEQ_S, 4, 128, 128])

    P = Prog(nc)
    P.alloc_psum(8)
    ARENA_N = 9216 + 512
    P.arena = P.sb("arena", [128, ARENA_N])
    P.arena_n = ARENA_N

    HT_P = P.sb("hT_p", [128, L, 2, 256])
    CT = P.sb("consts_t", [128, NCONST])
    PPT = P.sb("pp_t", [128, L, NPP])
    FNW = P.sb("fnw_t", [128, D])
    WBUF = [P.sb("wbuf%d" % i, [128, WB]) for i in range(NBUF)]
    wctr = [0]

    def wnext():
        b = WBUF[wctr[0] % NBUF]
        wctr[0] += 1
        return b

    ident = CT[:, C_ID:C_ID + 128]
    ones = CT[:, C_ONES:C_ONES + 128]

    AH = P.sb("ssdA", [128, L])
    LB = P.sb("hg_lb", [128, L, 4])
    OML = P.sb("hg_oml", [128, L, 4])
    COS = P.sb("s5cos", [128, L, 16, 64])
    SIN = P.sb("s5sin", [128, L, 16, 64])
    RHO = P.sb("s5rho", [128, L, 16])
    CR = P.sb("s5cr", [128, L, 16])
    CI = P.sb("s5ci", [128, L, 16])
    E2R = P.sb("s5e2r", [128, L, 16, 64])
    E2I = P.sb("s5e2i", [128, L, 16, 64])

    XTK = P.sb("x_tok", [128, D])
    PTK = [P.sb("ptk%d" % i, [128, 520]) for i in range(2)]
    XN = P.sb("xnT", [128, 8, 128])
    PJ_Z = P.sb("pj_z", [128, 4, 128])
    XPS = P.sb("xp_ssd", [128, 8, 131])
    PJ_DT = P.sb("pj_dt", [128, 128])
    PJ_U = P.sb("pj_u", [128, 4, 128])
    PJ_GATE = P.sb("pj_gate", [128, 4, 128])
    XPM = P.sb("xp_ml", [128, 4, 131])
    PJ_CZ = P.sb("pj_cz", [128, 4, 128])
    PJ_CO = P.sb("pj_co", [128, 4, 128])
    PJ_CI = P.sb("pj_ci", [128, 128])
    PJ_CF = P.sb("pj_cf", [128, 128])
    PJ_DF = P.sb("pj_df", [128, 4, 128])
    PJ_DI = P.sb("pj_di", [128, 4, 128])
    PJ_DQ = P.sb("pj_dq", [128, 4, 128])
    PJ_DG = P.sb("pj_dg", [128, 4, 128])
    YB = [P.sb("ybr%d" % i, [128, 4, 128]) for i in range(4)]
    MRGK = P.sb("merged_tok", [128, D])

    HIST_S = P.sb("hist_s", [128, L, 8, 3])
    HIST_M = P.sb("hist_m", [128, L, 4, 3])
    S5R_P = P.sb("s5r_p", [128, L, 16])
    S5I_P = P.sb("s5i_p", [128, L, 16])
    MC_P = P.sb("mC_p", [128, L, 4, 128])
    MN_P = P.sb("mn_p", [128, L, 4, 2])
    MM_P = P.sb("mm_p", [128, L, 4])
    HG_P = P.sb("hg_p", [128, L, 4, 128])
    HT_S = P.sb("hT_s", [128, 2, 256])
    HNAT = P.sb("hnat", [128, 8, 64])
    S5R_S = P.sb("s5r_s", [128, 16, NSEQ_S])
    S5I_S = P.sb("s5i_s", [128, 16, NSEQ_S])
    MC_S = P.sb("mC_s", [128, 4, 128])
    MN_S = P.sb("mn_s", [128, 4, 2])
    MM_S = P.sb("mm_s", [128, 4])
    HG_S = P.sb("hg_s", [128, 4, 128])

    def pc(l, col, n=1, rows=128):
        return PPT[0:rows, l, col:col + n]

    P.dma(CT[:], const_d[:, :])
    P.dma(PPT[:], pp_d.rearrange("l p c -> p l c"))
    P.dma(FNW[:], fnw_d[0:1, :].partition_broadcast(128))
    for t_, v_ in ((HIST_S, 0.0), (HIST_M, 0.0), (S5R_P, 0.0), (S5I_P, 0.0), (MC_P, 0.0),
                   (MN_P, 0.0), (MM_P, 0.0), (HG_P, 0.0)):
        P.memset(t_[:], v_)
    P.memset(HT_P[:], 0.0)
    for t_ in (XPS, XPM, PJ_DT, PJ_CI, PJ_CF, P.arena):
        P.memset(t_[:], 0.0)

    def range_reduce(a, tmpf, tmpi):
        P.ts(tmpf, a, 1.0 / TWO_PI, ALU.mult)
        P.copy(tmpi, tmpf)
        P.copy(tmpf, tmpi)
        P.stt(a, tmpf, -TWO_PI, a, ALU.mult, ALU.add)
        P.ts(tmpf, a, math.pi, ALU.is_gt, TWO_PI, ALU.mult)
        P.tt(a, a, tmpf, ALU.subtract)
        P.ts(tmpf, a, -math.pi, ALU.is_lt, TWO_PI, ALU.mult)
        P.tt(a, a, tmpf, ALU.add)
        P.ts(a, a, math.pi, ALU.min, -math.pi, ALU.max)

    for l in range(L):
        P.aoff = 0
        P.act(AH[0:8, l:l + 1], pc(l, PP_ALOG, 1, 8), AF.Exp)
        P.ts(AH[0:8, l:l + 1], AH[0:8, l:l + 1], -1.0, ALU.mult)
        if l == 0:
            P.memset(LB[:, 0, :], 0.0)
        else:
            dl_ = P.al(4)
            P.tt(dl_, pc(l, PP_HL1, 4), pc(l, PP_HL0, 4), ALU.subtract)
            P.act(LB[:, l, :], dl_, AF.Sigmoid)
        P.ts(OML[:, l, :], LB[:, l, :], -1.0, ALU.mult, 1.0, ALU.add)
        dt = P.al(16)
        P.act(dt, pc(l, PP_LDT, 16), AF.Exp)
        lrdt = P.al(16)
        P.tt(lrdt, pc(l, PP_ARE, 16), dt, ALU.mult)
        P.act(RHO[:, l, :], lrdt, AF.Exp)
        th = P.al(16)
        P.tt(th, pc(l, PP_AIM, 16), dt, ALU.mult)
        ang = P.al(16, 64)
        tmpf = P.al(16, 64)
        tau = CT[:, C_TAU:C_TAU + 64]
        P.tt(ang, th.unsqueeze(2).to_broadcast([128, 16, 64]), tau.unsqueeze(1).to_broadcast([128, 16, 64]),
             ALU.mult)
        ang2 = P.al(16, 64)
        P.ts(ang2, ang, math.pi / 2.0, ALU.add)
        ti3 = P.al(16, 64).bitcast(I32)
        range_reduce(ang, tmpf, ti3)
        range_reduce(ang2, tmpf, ti3)
        P.act(SIN[:, l, :, :], ang, AF.Sin)
        P.act(COS[:, l, :, :], ang2, AF.Sin)
        abr = P.al(16)
        abi = P.al(16)
        P.tt(abr, RHO[:, l, :], COS[:, l, :, 0], ALU.mult)
        P.tt(abi, RHO[:, l, :], SIN[:, l, :, 0], ALU.mult)
        am1 = P.al(16)
        P.ts(am1, abr, -1.0, ALU.add)
        lr = pc(l, PP_ARE, 16)
        li = pc(l, PP_AIM, 16)
        den = P.al(16)
        t0 = P.al(16)
        P.tt(den, lr, lr, ALU.mult)
        P.tt(t0, li, li, ALU.mult)
        P.tt(den, den, t0, ALU.add)
        P.recip(den, den)
        t1 = P.al(16)
        P.tt(t0, am1, lr, ALU.mult)
        P.tt(t1, abi, li, ALU.mult)
        P.tt(t0, t0, t1, ALU.add)
        P.tt(CR[:, l, :], t0, den, ALU.mult)
        P.tt(t0, abi, lr, ALU.mult)
        P.tt(t1, am1, li, ALU.mult)
        P.tt(t0, t0, t1, ALU.subtract)
        P.tt(CI[:, l, :], t0, den, ALU.mult)
        crb_ = CR[:, l, :].unsqueeze(2).to_broadcast([128, 16, 64])
        cib_ = CI[:, l, :].unsqueeze(2).to_broadcast([128, 16, 64])
        P.tt(ang, COS[:, l, :, :], crb_, ALU.mult)
        P.tt(ang2, SIN[:, l, :, :], cib_, ALU.mult)
        P.tt(E2R[:, l, :, :], ang, ang2, ALU.add)
        P.tt(ang, COS[:, l, :, :], cib_, ALU.mult)
        P.tt(ang2, SIN[:, l, :, :], crb_, ALU.mult)
        P.tt(E2I[:, l, :, :], ang, ang2, ALU.subtract)

    def rmsnorm_fm(src, n, T, wcol_l, wcol, dst, dmodel):
        sq = P.al(n, T)
        P.act(sq, src, AF.Square)
        ps = P.ps()
        for k in range(n):
            P.mm(ps[:, 0:T], ones, sq[:, k, :], start=(k == 0), stop=(k == n - 1))
        rstd = P.al(T)
        P.act(rstd, ps[:, 0:T], AF.Sqrt, bias=EPSB[:, 0:1], scale=1.0 / dmodel)
        P.recip(rstd, rstd)
        for k in range(n):
            P.stt(dst[:, k, :], src[:, k, :], pc(wcol_l, wcol + k), rstd, ALU.mult, ALU.mult)

    EPSB = P.sb("epsb", [128, 2])
    P.memset(EPSB[:, 0:1], EPS)
    P.memset(EPSB[:, 1:2], 1.0)

    def bc3(ap2, n_mid=None, n_last=None):
        rows = ap2.shape[0]
        if n_last is not None:
            return ap2.unsqueeze(2).to_broadcast([rows, ap2.shape[1], n_last])
        return ap2.unsqueeze(1).to_broadcast([rows, n_mid, ap2.shape[1]])

    def stream(dst_view, src):
        P.dma(dst_view, src)

    def proj_blocks(l, T, nseq, Q):
        w_l = w_in[l].rearrange("(kt p) c -> p kt c", p=128)
        xps4 = XPS[:, :, 0:nseq * (Q + 3)].rearrange("p b (s q) -> p b s q", s=nseq)
        xpm4 = XPM[:, :, 0:nseq * (Q + 3)].rearrange("p b (s q) -> p b s q", s=nseq)

        def seqv(ps_ap):
            return ps_ap.rearrange("p (s q) -> p s q", s=nseq)

        groups = []

        def ev_act(dst_fn, func, bias=None, scale=1.0):
            def f(ps_ap, bi, rows):
                P.act(dst_fn(bi, rows), ps_ap, func, bias=(bias(rows) if bias else 0.0), scale=scale)
            return f

        def simple(col0, tile, func):
            blks = [(i * 128, 128, None) for i in range(4)]
            groups.append((col0, 512, blks,
                           (lambda ps2, t=tile, f=func: P.act(t[:, :, 0:T], ps2[:, 0:4 * T].rearrange("p (i t) -> p i t", i=4), f))))

        simple(O_AZ, PJ_Z, AF.Silu)
        groups.append((O_XBC, 512, [(i * 128, 128, None) for i in range(4)],
                       (lambda ps2: P.act(xps4[:, 0:4, :, 3:3 + Q],
                                          ps2[:, 0:4 * T].rearrange("p (i s q) -> p i s q", i=4, s=nseq), AF.Copy))))
        blks = [(j * 64, 64, (lambda ps_ap, bi, rows, j=j: P.act(xps4[0:rows, 4 + j, :, 3:3 + Q], seqv(ps_ap), AF.Copy)))
                for j in range(4)]
        blks.append((256, 8, (lambda ps_ap, bi, rows: P.act(PJ_DT[0:rows, 0:T], ps_ap, AF.Copy))))
        groups.append((O_XBC + 512, 264, blks))
        simple(O_U, PJ_U, AF.Copy)
        simple(O_GATE, PJ_GATE, AF.Silu)
        groups.append((O_CX, 512, [(i * 128, 128, None) for i in range(4)],
                       (lambda ps2: P.act(xpm4[:, 0:4, :, 3:3 + Q],
                                          ps2[:, 0:4 * T].rearrange("p (i s q) -> p i s q", i=4, s=nseq), AF.Copy))))
        simple(O_CZ, PJ_CZ, AF.Silu)
        simple(O_CO, PJ_CO, AF.Sigmoid)
        groups.append((O_CI, 8, [
            (0, 4, (lambda ps_ap, bi, rows: P.act(PJ_CI[0:rows, 0:T], ps_ap, AF.Identity, bias=pc(l, PP_MIB, 1, 4)))),
            (4, 4, (lambda ps_ap, bi, rows: P.act(PJ_CF[0:rows, 0:T], ps_ap, AF.Copy)))]))
        simple(O_DF, PJ_DF, AF.Sigmoid)
        simple(O_DI, PJ_DI, AF.Copy)
        simple(O_DQ, PJ_DQ, AF.Silu)
        simple(O_DG, PJ_DG, AF.Silu)

        tags = [0, 0, 0, 1, 1, 2, 2, 2, 2, 3, 3, 3, 3]
        assert len(tags) == len(groups)

        def run_group(grp, gi):
            col0, ncols, blks = grp[0], grp[1], grp[2]
            gevac = grp[3] if len(grp) > 3 else None
            wb = wnext()
            wv = wb[:, 0:8 * ncols].rearrange("p (k c) -> p k c", k=8)
            stream(wv, w_l[:, :, col0:col0 + ncols])
            ps = P.ps()
            for kt in range(8):
                P.mm(ps[0:T, 0:ncols], XN[:, kt, 0:T], wv[:, kt, 0:ncols], start=(kt == 0), stop=(kt == 7))
            tk = PTK[gi % 2]
            P.copy(tk[0:T, 0:ncols], ps[0:T, 0:ncols], eng=("act" if gi % 2 == 0 else "dve"))
            for b0 in range(0, len(blks), 4):
                sub = blks[b0:b0 + 4]
                ps2 = P.ps()
                for bj, (co, n, evac) in enumerate(sub):
                    P.tr(ps2[0:n, bj * T:(bj + 1) * T], tk[0:T, co:co + n], ident[0:T, 0:T])
                if gevac is not None:
                    gevac(ps2)
                else:
                    for bj, (co, n, evac) in enumerate(sub):
                        evac(ps2[0:n, bj * T:(bj + 1) * T], None, n)

        return [(tags[gi], (lambda g=grp, gi=gi: run_group(g, gi))) for gi, grp in enumerate(groups)]

    pending = []

    def pump(n=1):
        for _ in range(n):
            if pending:
                pending.pop(0)[1]()

    def flush_upto(tag):
        while pending and pending[0][0] <= tag:
            pending.pop(0)[1]()

    def conv_fm(xp4, nblk_rows, l, ppbase, acc4, Q):
        for bi, rows in enumerate(nblk_rows):
            c = ppbase + bi * 5
            P.ts(acc4[0:rows, bi], xp4[0:rows, bi, :, 0:Q], pc(l, c, 1, rows), ALU.mult,
                 pc(l, c + 4, 1, rows), ALU.add)
            for j in range(1, 4):
                P.stt(acc4[0:rows, bi], xp4[0:rows, bi, :, j:j + Q], pc(l, c + j, 1, rows), acc4[0:rows, bi],
                      ALU.mult, ALU.add)

    def ssd_branch(l, T, nseq, Q, sample, first, last, core_out):
        xps4 = XPS[:, :, 0:nseq * (Q + 3)].rearrange("p b (s q) -> p b s q", s=nseq)
        rows8 = [128] * 4 + [64] * 4
        acc = P.al(8, nseq, Q)
        conv_fm(xps4, rows8, l, PP_SCONV, acc, Q)
        xc = P.al(8, T)
        accf = acc.rearrange("p b s q -> p b (s q)")
        P.act(xc[:, 0:4, :], accf[:, 0:4, :], AF.Silu)
        P.act(xc[0:64, 4:8, :], accf[0:64, 4:8, :], AF.Silu)
        if SSD_CUT == 1:
            P.memset(YB[0][:], 0.0)
            return
        if not sample:
            P.copy(HIST_S[:, l, :, :], xps4[:, :, 0, Q:Q + 3], eng="act")
        dte = P.al(T)
        P.act(dte[0:8], PJ_DT[0:8, 0:T], AF.Exp, bias=pc(l, PP_DTB, 1, 8))
        dtT = P.al(T)
        P.act(dtT[0:8], dte[0:8], AF.Ln, bias=EPSB[0:8, 1:2])
        aT = P.al(T)
        P.ts(aT[0:8], dtT[0:8], AH[0:8, l:l + 1], ALU.mult)
        acT = P.al(T)
        if sample:
            P.scan(acT[0:8], CT[0:8, C_MASK4:C_MASK4 + T], aT[0:8], 0.0)
        else:
            P.scan(acT[0:8], CT[0:8, C_ONES:C_ONES + T], aT[0:8], 0.0)
        if SSD_CUT == 2:
            P.memset(YB[0][:], 0.0)
            return
        ytT = P.al(4, T)
        mark = P.aoff
        for s in range(nseq):
            P.aoff = mark
            c0 = s * Q
            sl = slice(c0, c0 + Q)
            if sample:
                hT = HT_S[0:64]
                P.dma(HNAT[0:64], st_ssd[l, s].rearrange("h p n -> p h n"))
                psx = P.ps()
                for g in range(2):
                    for r in range(4):
                        P.tr(psx[0:64, (g * 4 + r) * 64:(g * 4 + r + 1) * 64], HNAT[0:64, 4 * g + r, :], ident[0:64, 0:64])
                P.copy(HT_S[0:64].rearrange("p g c -> p (g c)"), psx[0:64, 0:512])
            else:
                hT = HT_P[0:64, l]
            psA = P.ps()
            for i in range(4):
                P.tr(psA[0:Q, i * 128:(i + 1) * 128], xc[:, i, sl], ident)
            psB = P.ps()
            for g in range(2):
                P.tr(psB[0:Q, g * 64:(g + 1) * 64], xc[0:64, 4 + g, sl], ident[0:64, 0:64])
            P.tr(psB[0:Q, 128:136], dtT[0:8, sl], ident[0:8, 0:8])
            P.tr(psB[0:Q, 136:144], acT[0:8, sl], ident[0:8, 0:8])
            btok = P.al(128)
            dtac = P.al(16)
            P.copy(btok[0:Q], psB[0:Q, 0:128], eng="act")
            P.copy(dtac[0:Q], psB[0:Q, 128:144], eng="act")
            dt_tok = dtac[0:Q, 0:8]
            ac_tok = dtac[0:Q, 8:16]
            xdt = P.al(8, 64)
            P.tt(xdt[0:Q], psA[0:Q, :].rearrange("p (h c) -> p h c", h=8), bc3(dt_tok, n_last=64), ALU.mult)
            if SSD_CUT == 3:
                break
            adiag = P.al(8, Q)
            P.tt(adiag[0:Q], bc3(ident[0:Q, 0:Q], n_mid=8), bc3(ac_tok, n_last=Q), ALU.mult)
            pump(3)
            nb = 2 if Q == 128 else 1
            hb = 8 // nb
            psR = [P.ps() for _ in range(nb)]
            for b_ in range(nb):
                P.mm(psR[b_][:, 0:hb * Q].rearrange("p (h t) -> p h t", h=hb), ones[0:Q, :],
                     adiag[0:Q, b_ * hb:(b_ + 1) * hb, :])
            alast = P.al(8)
            for b_ in range(nb):
                P.copy(alast[:, b_ * hb:(b_ + 1) * hb],
                       psR[b_][:, 0:hb * Q].rearrange("p (h t) -> p h t", h=hb)[:, :, Q - 1], eng="act")
            dec = P.al(8, Q)
            for b_ in range(nb):
                P.tt(dec[0:Q, b_ * hb:(b_ + 1) * hb, :],
                     psR[b_][0:Q, 0:hb * Q].rearrange("p (h t) -> p h t", h=hb),
                     bc3(ac_tok[:, b_ * hb:(b_ + 1) * hb], n_last=Q), ALU.subtract)
            P.tt(dec[0:Q], dec[0:Q], bc3(CT[0:Q, C_NEGU:C_NEGU + Q], n_mid=8), ALU.add)
            P.act(dec[0:Q], dec[0:Q], AF.Exp)
            if SSD_CUT == 4:
                break
            psC = P.ps()
            for g in range(2):
                P.mm(psC[0:Q, g * Q:(g + 1) * Q], xc[0:64, 4 + g, sl], xc[0:64, 6 + g, sl])
            MT = P.al(8, Q)
            for g in range(2):
                P.tt(MT[0:Q, 4 * g:4 * g + 4, :], dec[0:Q, 4 * g:4 * g + 4, :],
                     bc3(psC[0:Q, g * Q:(g + 1) * Q], n_mid=4), ALU.mult)
            pump(3)
            psY = P.ps()
            for h in range(8):
                P.mm(psY[0:Q, h * 64:(h + 1) * 64], MT[0:Q, h, :], xdt[0:Q, h, :])
            psS = P.ps()
            for g in range(2):
                P.mm(psS[0:Q, g * 256:(g + 1) * 256], xc[0:64, 6 + g, sl], hT[:, g, :])
            eac = P.al(8)
            P.act(eac[0:Q], ac_tok, AF.Exp)
            ytok = P.al(8, 64)
            P.tt(ytok[0:Q], psS[0:Q, :].rearrange("p (h c) -> p h c", h=8), bc3(eac[0:Q], n_last=64), ALU.mult)
            P.tt(ytok[0:Q], ytok[0:Q], psY[0:Q, :].rearrange("p (h c) -> p h c", h=8), ALU.add)
            if SSD_CUT == 6:
                break
            elast = P.al(8)
            P.act(elast, alast, AF.Exp)
            if SSD_CUT == 61:
                break
            toend = P.al(8)
            P.tt(toend[0:Q], alast[0:Q], ac_tok, ALU.subtract)
            P.act(toend[0:Q], toend[0:Q], AF.Exp)
            xe = P.al(8, 64)
            P.tt(xe[0:Q], xdt[0:Q], bc3(toend[0:Q], n_last=64), ALU.mult)
            if SSD_CUT == 62:
                break
            pump(3)
            psH = P.ps()
            for g in range(2):
                P.mm(psH[0:64, g * 256:(g + 1) * 256], btok[0:Q, g * 64:(g + 1) * 64],
                     xe[0:Q, 4 * g:4 * g + 4, :].rearrange("p h c -> p (h c)"))
            if SSD_CUT == 63:
                break
            hsc = P.al(2, 256)
            for g in range(2):
                for r in range(4):
                    P.act(hsc[0:64, g, r * 64:(r + 1) * 64], hT[:, g, r * 64:(r + 1) * 64], AF.Copy,
                          scale=elast[0:64, 4 * g + r:4 * g + r + 1])
            for g in range(2):
                P.tt(hT[:, g, :], hsc[0:64, g, :], psH[0:64, g * 256:(g + 1) * 256], ALU.add)
            pump(1)
            psT = P.ps()
            if SSD_CUT == 64:
                break
            yf = ytok[0:Q].rearrange("p h c -> p (h c)")
            for i in range(4):
                P.tr(psT[:, i * Q:(i + 1) * Q], yf[:, i * 128:(i + 1) * 128], ident[0:Q, 0:Q])
            P.copy(ytT[:, :, sl], psT[:, 0:4 * Q].rearrange("p (i q) -> p i q", i=4), eng="act")
            if SSD_CUT == 8:
                break
            if sample or last:
                pso = P.ps()
                for g in range(2):
                    for r in range(4):
                        P.tr(pso[0:64, (4 * g + r) * 64:(4 * g + r + 1) * 64], hT[:, g, r * 64:(r + 1) * 64],
                             ident[0:64, 0:64])
                hout = P.al(8, 64)
                P.copy(hout[0:64], pso[0:64, :].rearrange("p (h n) -> p h n", h=8))
                dst = o_ssd_s[l, s] if sample else o_ssd_p[l]
                P.dma(dst.rearrange("h p n -> p h n"), hout[0:64], q="act")
        P.aoff = mark
        if sample or last:
            cvi = P.al(8, nseq, 3)
            if sample:
                P.copy(cvi, xps4[:, :, :, Q:Q + 3])
            else:
                P.copy(cvi[:, :, 0, :], HIST_S[:, l, :, :])
            n3 = nseq * 3
            psv = [P.ps(), P.ps()]
            cols = [0, 128, 256, 384, 512, 576, 640, 704]
            for bi in range(8):
                rows = rows8[bi]
                pv = psv[0] if cols[bi] < 512 else psv[1]
                cc = cols[bi] % 512
                P.tr(pv[0:n3, cc:cc + rows], cvi[0:rows, bi].rearrange("p s j -> p (s j)"), ident[0:rows, 0:rows])
            cvo = P.al(768)
            P.copy(cvo[0:n3, 0:512], psv[0][0:n3, 0:512])
            P.copy(cvo[0:n3, 512:768], psv[1][0:n3, 0:256])
            dst = o_sconv_s[l] if sample else o_sconv_p[l]
            P.dma(dst, cvo[0:n3, :], q="act")
        yg = P.al(4, T)
        for i in range(4):
            P.stt(yg[:, i, :], xc[:, i, :], pc(l, PP_SSDD + i), ytT[:, i, :], ALU.mult, ALU.add)
        P.tt(yg, yg, PJ_Z[:, :, 0:T], ALU.mult)
        rmsnorm_fm(yg, 4, T, l, PP_SSDNW, YB[0][:, :, 0:T], 512.0)

    def s5_branch(l, T, nseq, Q, sample, last):
        wb = wnext()
        Bv = wb[:, 0:4096].rearrange("p (r c m) -> p r c m", r=2, c=4)
        stream(Bv[:, 0], bbd_d[l, 0])
        stream(Bv[:, 1], bbd_d[l, 1])
        if sample:
            nat = P.al(2048)
            for (src, dstt) in ((st_s5re, S5R_S), (st_s5im, S5I_S)):
                P.dma(nat[0:NSEQ_S], src[l])
                psx = P.ps()
                for st in range(16):
                    P.tr(psx[:, st * 16:(st + 1) * 16], nat[0:NSEQ_S, st * 128:(st + 1) * 128], ident[0:16, 0:16])
                P.copy(dstt[:].rearrange("p s b -> p (s b)"), psx[:, 0:256])
            subs = [(0, NSEQ_S, QS)]
            SR, SI = S5R_S[:], S5I_S[:]
        else:
            subs = [(0, 1, 64), (64, 1, 64)]
            SR, SI = S5R_P[:, l], S5I_P[:, l]
        HR = P.al(16, T)
        NHI = P.al(16, T)
        mark = P.aoff
        for (c0, ns, q) in subs:
            P.aoff = mark
            W = ns * q
            wre = P.al(16, W)
            wim = P.al(16, W)
            t1 = P.al(4, W)
            t2 = P.al(4, W)
            for ct in range(4):
                ps = P.ps()
                for sl_ in range(4):
                    P.mm(ps[:, sl_ * 128:sl_ * 128 + W], Bv[:, 0, ct, sl_ * 128:(sl_ + 1) * 128], PJ_U[:, ct, c0:c0 + W])
                    P.mm(ps[:, sl_ * 128 + 64:sl_ * 128 + 64 + W], Bv[:, 1, ct, sl_ * 128:(sl_ + 1) * 128],
                         PJ_U[:, ct, c0:c0 + W])
                p4 = ps[:, :].rearrange("p (s r w) -> p s r w", s=4, r=2)
                pr = p4[:, :, 0, 0:W]
                pi = p4[:, :, 1, 0:W]
                stv = slice(ct * 4, ct * 4 + 4)
                if ns == 1:
                    e2r = E2R[:, l, stv, 0:q]
                    e2i = E2I[:, l, stv, 0:q]
                    v = lambda a: a
                else:
                    e2r = E2R[:, l, stv, 0:q].unsqueeze(2).to_broadcast([128, 4, ns, q])
                    e2i = E2I[:, l, stv, 0:q].unsqueeze(2).to_broadcast([128, 4, ns, q])
                    v = lambda a: a.rearrange("p s (b q) -> p s b q", b=ns)
                P.tt(v(t1), v(pr), e2r, ALU.mult)
                P.tt(v(t2), v(pi), e2i, ALU.mult)
                P.tt(wre[:, stv, :], t1, t2, ALU.subtract)
                P.tt(v(t1), v(pi), e2r, ALU.mult)
                P.tt(v(t2), v(pr), e2i, ALU.mult)
                P.tt(wim[:, stv, :], t1, t2, ALU.add)
            gre = P.al(16, W)
            gim = P.al(16, W)
            lastsub = (c0, ns, q) == subs[-1]
            if lastsub and not sample:
                pump_merge(0, 2)
            if ns == 1:
                for st in range(16):
                    rb = RHO[:, l, st:st + 1].to_broadcast([128, W])
                    P.scan(gre[:, st, :], rb, wre[:, st, :], SR[:, st:st + 1])
                    P.scan(gim[:, st, :], rb, wim[:, st, :], SI[:, st:st + 1])
            else:
                tmp = P.al(16, ns)
                w4r = wre.rearrange("p s (b q) -> p s b q", b=ns)
                w4i = wim.rearrange("p s (b q) -> p s b q", b=ns)
                P.tt(tmp, SR[:], bc3(RHO[:, l, :], n_last=ns), ALU.mult)
                P.tt(w4r[:, :, :, 0], w4r[:, :, :, 0], tmp, ALU.add)
                P.tt(tmp, SI[:], bc3(RHO[:, l, :], n_last=ns), ALU.mult)
                P.tt(w4i[:, :, :, 0], w4i[:, :, :, 0], tmp, ALU.add)
                rm3 = nat[:, 0:1024].rearrange("p (s w) -> p s w", s=16)
                P.tt(rm3, RHO[:, l, :].unsqueeze(2).to_broadcast([128, 16, 64]),
                     CT[:, C_MASK4:C_MASK4 + 64].unsqueeze(1).to_broadcast([128, 16, 64]), ALU.mult)
                rm = nat[:, 0:1024]
                P.scan(gre.rearrange("p s w -> p (s w)"), rm, wre.rearrange("p s w -> p (s w)"), 0.0)
                P.scan(gim.rearrange("p s w -> p (s w)"), rm, wim.rearrange("p s w -> p (s w)"), 0.0)
            if lastsub and not sample:
                pump_merge(0, 2)
            if lastsub:
                wc = wnext()
                Cv = wc[:, 0:4096].rearrange("p (r s m) -> p r s m", r=2, s=16)
                stream(Cv[:, 0], cbd_d[l, 0])
                stream(Cv[:, 1], cbd_d[l, 1])
                wg = wnext()
                Gv = wg[:, 0:2048].rearrange("p (c m) -> p c m", c=4)
                stream(Gv, w_glu[l].rearrange("(c p) m -> p c m", p=128))
            if ns == 1:
                cs = COS[:, l, :, 0:q]
                sn = SIN[:, l, :, 0:q]
                v = lambda a: a
            else:
                cs = COS[:, l, :, 0:q].unsqueeze(2).to_broadcast([128, 16, ns, q])
                sn = SIN[:, l, :, 0:q].unsqueeze(2).to_broadcast([128, 16, ns, q])
                v = lambda a: a.rearrange("p s (b q) -> p s b q", b=ns)
            t1 = wre
            t2 = wim
            P.tt(v(t1), v(gre), cs, ALU.mult)
            P.tt(v(t2), v(gim), sn, ALU.mult)
            P.tt(HR[:, :, c0:c0 + W], t1, t2, ALU.subtract)
            P.tt(v(t1), v(gre), sn, ALU.mult)
            P.tt(v(t2), v(gim), cs, ALU.mult)
            P.stt(NHI[:, :, c0:c0 + W], t1, -1.0, t2, ALU.mult, ALU.subtract)
            if ns == 1:
                P.copy(SR, HR[:, :, c0 + W - 1], eng="act")
                P.ts(SI, NHI[:, :, c0 + W - 1], -1.0, ALU.mult)
            else:
                h4 = HR[:, :, c0:c0 + W].rearrange("p s (b q) -> p s b q", b=ns)
                n4 = NHI[:, :, c0:c0 + W].rearrange("p s (b q) -> p s b q", b=ns)
                P.copy(SR[:], h4[:, :, :, q - 1], eng="act")
                P.ts(SI[:], n4[:, :, :, q - 1], -1.0, ALU.mult)
        P.aoff = mark
        if sample or last:
            for (srct, dstd_s, dstd_p) in ((SR, o_s5re_s, o_s5re_p), (SI, o_s5im_s, o_s5im_p)):
                if sample:
                    so = P.al(2048)
                    for b_ in range(4):
                        pq = P.ps()
                        for st in range(4 * b_, 4 * b_ + 4):
                            P.tr(pq[0:NSEQ_S, (st % 4) * 128:(st % 4 + 1) * 128], srct[:, st, :], ident)
                        P.copy(so[0:NSEQ_S, b_ * 512:(b_ + 1) * 512], pq[0:NSEQ_S, 0:512])
                    P.dma(dstd_s[l], so[0:NSEQ_S, :], q="act")
                else:
                    pso = P.ps()
                    P.tr(pso[0:16, 0:128], srct, ident)
                    so = P.al(128)
                    P.copy(so[0:16], pso[0:16, 0:128])
                    P.dma(dstd_p[l], so[0:16], q="act")
        psY = P.ps()
        for ct in range(4):
            k = 0
            for sl_ in range(4):
                st = ct * 4 + sl_
                P.mm(psY[:, ct * T:(ct + 1) * T], Cv[:, 0, st, :], HR[:, st, :], start=(k == 0), stop=False)
                k += 1
                P.mm(psY[:, ct * T:(ct + 1) * T], Cv[:, 1, st, :], NHI[:, st, :], start=False, stop=(sl_ == 3))
        yb = P.al(4, T)
        for ct in range(4):
            P.stt(yb[:, ct, :], PJ_U[:, ct, 0:T], pc(l, PP_S5D + ct), psY[:, ct * T:(ct + 1) * T], ALU.mult, ALU.add)
        gg = P.al(4, T)
        P.act(gg, yb, AF.Gelu_apprx_tanh)
        psG = P.ps()
        for co in range(4):
            for ct in range(4):
                P.mm(psG[:, co * T:(co + 1) * T], Gv[:, ct, co * 128:(co + 1) * 128], gg[:, ct, :],
                     start=(ct == 0), stop=(ct == 3))
        sg = P.al(4, T)
        P.act(sg, psG[:, 0:4 * T].rearrange("p (c t) -> p c t", c=4), AF.Sigmoid)
        P.tt(sg, sg, gg, ALU.mult)
        P.tt(YB[1][:, :, 0:T], sg, PJ_GATE[:, :, 0:T], ALU.mult)

    def ml_branch(l, T, nseq, Q, sample, last):
        wb = wnext()
        Wq = wb[:, 0:512].rearrange("p (h k) -> p h k", h=4)
        Wk = wb[:, 512:1024].rearrange("p (h k) -> p h k", h=4)
        Wv = wb[:, 1024:1536].rearrange("p (h k) -> p h k", h=4)
        stream(Wq, w_mq[l].rearrange("h c k -> c h k"))
        stream(Wk, w_mk[l].rearrange("h c k -> c h k"))
        stream(Wv, w_mv[l].rearrange("h c k -> c h k"))
        xpm4 = XPM[:, :, 0:nseq * (Q + 3)].rearrange("p b (s q) -> p b s q", s=nseq)
        acc = P.al(4, nseq, Q)
        conv_fm(xpm4, [128] * 4, l, PP_MCONV, acc, Q)
        xcm = P.al(4, T)
        P.act(xcm, acc.rearrange("p b s q -> p b (s q)"), AF.Silu)
        if not sample:
            P.copy(HIST_M[:, l, :, :], xpm4[:, :, 0, Q:Q + 3], eng="act")
        e_ = P.al(T)
        P.act(e_[0:4], PJ_CF[0:4, 0:T], AF.Exp, bias=NFB[0:4, l:l + 1], scale=-1.0)
        sp = P.al(T)
        P.act(sp[0:4], e_[0:4], AF.Ln, bias=EPSB[0:4, 1:2])
        lf = P.al(T)
        P.ts(lf[0:4], sp[0:4], -1.0, ALU.mult)
        bT = P.al(T)
        if sample:
            P.scan(bT[0:4], CT[0:4, C_MASK4:C_MASK4 + T], lf[0:4], 0.0)
        else:
            P.scan(bT[0:4], CT[0:4, C_ONES:C_ONES + T], lf[0:4], 0.0)
        rT = P.al(T)
        P.tt(rT[0:4], PJ_CI[0:4, 0:T], bT[0:4], ALU.subtract)
        hnT = P.al(4, T)
        mark = P.aoff
        sel = CT[:, C_SEL4:C_SEL4 + 128] if sample else CT[:, C_SEL128:C_SEL128 + 128]
        for s in range(nseq):
            P.aoff = mark
            c0 = s * Q
            sl = slice(c0, c0 + Q)
            if sample:
                Cst, Nst, Mst = MC_S[:], MN_S[:], MM_S[:]
                P.dma(MC_S[:], st_mC[l, s].rearrange("h k v -> k h v"))
                nat = P.al(128)
                P.dma(nat[0:4], st_mn[l, s])
                psx = P.ps()
                P.tr(psx[:, 0:4], nat[0:4, :], ident[0:4, 0:4])
                P.copy(MN_S[:, :, 0], psx[:, 0:4])
                P.copy(MN_S[:, :, 1], psx[:, 0:4])
                P.dma(MM_S[:], st_mm[l, s:s + 1, :].partition_broadcast(128))
            else:
                Cst, Nst, Mst = MC_P[:, l], MN_P[:, l], MM_P[:, l]
            tok = P.al(12)
            psg = P.ps()
            P.tr(psg[0:Q, 0:4], bT[0:4, sl], ident[0:4, 0:4])
            P.tr(psg[0:Q, 4:8], rT[0:4, sl], ident[0:4, 0:4])
            P.copy(tok[0:Q, 0:8], psg[0:Q, 0:8], eng="act")
            b_tok = tok[0:Q, 0:4]
            r_tok = tok[0:Q, 4:8]
            m_t = tok[0:Q, 8:12]
            psq = P.ps()
            psk = P.ps()
            for h in range(4):
                P.mm(psq[:, h * Q:(h + 1) * Q], Wq[:, h, :], xcm[:, h, sl])
                P.mm(psk[:, h * Q:(h + 1) * Q], Wk[:, h, :], xcm[:, h, sl])
            qT = P.al(4, Q)
            kT = P.al(4, Q)
            P.copy(qT, psq[:, 0:4 * Q].rearrange("p (h q) -> p h q", h=4), eng="act")
            P.act(kT, psk[:, 0:4 * Q].rearrange("p (h q) -> p h q", h=4), AF.Copy, scale=128.0 ** -0.5)
            pskt = P.ps()
            psvt = P.ps()
            for h in range(4):
                P.mm(pskt[0:Q, h * 128:(h + 1) * 128], xcm[:, h, sl], Wk[:, h, :])
                P.mm(psvt[0:Q, h * 128:(h + 1) * 128], xpm4[:, h, s, 3:3 + Q], Wv[:, h, :])
            vtok = P.al(4, 128)
            P.copy(vtok[0:Q], psvt[0:Q, :].rearrange("p (h v) -> p h v", h=4), eng="act")
            ktok = P.al(4, 128)
            P.copy(ktok[0:Q], pskt[0:Q, :].rearrange("p (h v) -> p h v", h=4), eng="act")
            rdiag = P.al(4, Q)
            P.tt(rdiag[0:Q], bc3(ident[0:Q, 0:Q], n_mid=4), bc3(r_tok, n_last=Q), ALU.mult)
            psR = P.ps()
            P.mm(psR[:, 0:4 * Q].rearrange("p (h t) -> p h t", h=4), ones[0:Q, :], rdiag[0:Q])
            dl = P.al(4, Q)
            P.tt(dl[0:Q], psR[0:Q, 0:4 * Q].rearrange("p (h t) -> p h t", h=4), bc3(b_tok, n_last=Q), ALU.add)
            P.tt(dl[0:Q], dl[0:Q], bc3(CT[0:Q, C_NEGL:C_NEGL + Q], n_mid=4), ALU.add)
            mx = P.al(4)
            P.red(mx[0:Q], dl[0:Q], ALU.max)
            bm = P.al(4)
            P.tt(bm[0:Q], b_tok, Mst[0:Q], ALU.add)
            P.tt(m_t, bm[0:Q], mx[0:Q], ALU.max)
            negm = P.al(4)
            P.ts(negm[0:Q], m_t, -1.0, ALU.mult)
            Wt = P.al(4, Q)
            for h in range(4):
                P.act(Wt[0:Q, h, :], dl[0:Q, h, :], AF.Exp, bias=negm[0:Q, h:h + 1])
            psqk = P.ps()
            for h in range(4):
                P.mm(psqk[0:Q, h * Q:(h + 1) * Q], qT[:, h, :], kT[:, h, :])
            S = P.al(4, Q)
            P.tt(S[0:Q], Wt[0:Q], psqk[0:Q, 0:4 * Q].rearrange("p (h t) -> p h t", h=4), ALU.mult)
            dotin = P.al(4)
            P.red(dotin[0:Q], S[0:Q], ALU.add)
            psst = P.ps()
            for h in range(4):
                P.tr(psst[0:Q, h * Q:(h + 1) * Q], S[0:Q, h, :], ident[0:Q, 0:Q])
            ST = P.al(4, Q)
            P.copy(ST[0:Q], psst[0:Q, 0:4 * Q].rearrange("p (h t) -> p h t", h=4), eng="act")
            psn = P.ps()
            psi = P.ps()
            psqn = P.ps()
            for h in range(4):
                P.mm(psn[0:Q, h * 128:(h + 1) * 128], ST[0:Q, h, :], vtok[0:Q, h, :])
                P.mm(psi[0:Q, h * 128:(h + 1) * 128], qT[:, h, :], Cst[:, h, :])
                P.mm(psqn[0:Q, 2 * h:2 * h + 2], qT[:, h, :], Nst[:, h, :])
            scl = P.al(4)
            P.tt(scl[0:Q], bm[0:Q], m_t, ALU.subtract)
            P.act(scl[0:Q], scl[0:Q], AF.Exp)
            num = P.al(4, 128)
            P.tt(num[0:Q], psi[0:Q, :].rearrange("p (h v) -> p h v", h=4), bc3(scl[0:Q], n_last=128), ALU.mult)
            P.tt(num[0:Q], num[0:Q], psn[0:Q, :].rearrange("p (h v) -> p h v", h=4), ALU.add)
            dot = P.al(4)
            P.tt(dot[0:Q], psqn[0:Q, 0:8].rearrange("p (h t) -> p h t", h=4)[:, :, 0], scl[0:Q], ALU.mult)
            P.tt(dot[0:Q], dot[0:Q], dotin[0:Q], ALU.add)
            emn = P.al(4)
            emn2 = P.al(4)
            P.act(emn[0:Q], negm[0:Q], AF.Exp)
            P.ts(emn2[0:Q], dot[0:Q], -1.0, ALU.mult)
            P.tt(dot[0:Q], dot[0:Q], emn2[0:Q], ALU.max)
            P.tt(dot[0:Q], dot[0:Q], emn[0:Q], ALU.max)
            P.recip(dot[0:Q], dot[0:Q])
            P.tt(num[0:Q], num[0:Q], bc3(dot[0:Q], n_last=128), ALU.mult)
            pso = P.ps()
            for h in range(4):
                P.tr(pso[0:Q, h * 128:(h + 1) * 128], PJ_CO[:, h, sl], ident)
            P.tt(num[0:Q], num[0:Q], pso[0:Q, :].rearrange("p (h v) -> p h v", h=4), ALU.mult)
            mean = P.al(4)
            P.red(mean[0:Q], num[0:Q], ALU.add)
            P.ts(mean[0:Q], mean[0:Q], 1.0 / 128.0, ALU.mult)
            P.tt(num[0:Q], num[0:Q], bc3(mean[0:Q], n_last=128), ALU.subtract)
            sq = P.al(4, 128)
            P.tt(sq[0:Q], num[0:Q], num[0:Q], ALU.mult)
            var = P.al(4)
            P.red(var[0:Q], sq[0:Q], ALU.add)
            P.act(var[0:Q], var[0:Q], AF.Sqrt, bias=EPSB[0:Q, 0:1], scale=1.0 / 128.0)
            P.recip(var[0:Q], var[0:Q])
            P.tt(num[0:Q], num[0:Q], bc3(var[0:Q], n_last=128), ALU.mult)
            pst = P.ps()
            for h in range(4):
                P.tr(pst[:, h * Q:(h + 1) * Q], num[0:Q, h, :], ident[0:Q, 0:Q])
            P.copy(hnT[:, :, sl], pst[:, 0:4 * Q].rearrange("p (h q) -> p h q", h=4), eng="act")
            psl = P.ps()
            P.mm(psl[:, 0:12], sel[0:Q, :], tok[0:Q, 0:12])
            L12 = P.al(12)
            P.copy(L12, psl[:, 0:12])
            lm = P.al(4)
            P.tt(lm, L12[:, 0:4], L12[:, 8:12], ALU.subtract)
            wend = P.al(4)
            P.tt(wend[0:Q], r_tok, lm[0:Q], ALU.add)
            P.act(wend[0:Q], wend[0:Q], AF.Exp)
            P.ts(wend[0:Q], wend[0:Q], 128.0 ** -0.5, ALU.mult)
            carry = P.al(4)
            P.tt(carry, lm, Mst, ALU.add)
            P.act(carry, carry, AF.Exp)
            kw = P.al(4, 128)
            P.tt(kw[0:Q], ktok[0:Q], bc3(wend[0:Q], n_last=128), ALU.mult)
            psc = P.ps()
            psdn = P.ps()
            for h in range(4):
                P.mm(psc[:, h * 128:(h + 1) * 128], kw[0:Q, h, :], vtok[0:Q, h, :])
                P.mm(psdn[:, 2 * h:2 * h + 2], kw[0:Q, h, :], ones[0:Q, 0:2])
            csc = P.al(4, 128)
            nsc = P.al(4, 2)
            for h in range(4):
                P.act(csc[:, h, :], Cst[:, h, :], AF.Copy, scale=carry[:, h:h + 1])
                P.act(nsc[:, h, :], Nst[:, h, :], AF.Copy, scale=carry[:, h:h + 1])
            P.tt(Cst, csc, psc[:, :].rearrange("p (h v) -> p h v", h=4), ALU.add)
            P.tt(Nst, nsc, psdn[:, 0:8].rearrange("p (h t) -> p h t", h=4), ALU.add)
            P.copy(Mst, L12[:, 8:12], eng="act")
            if sample or last:
                P.dma((o_mC_s[l, s] if sample else o_mC_p[l]).rearrange("h k v -> k h v"), Cst, q="act")
                psx = P.ps()
                ncp = P.al(4)
                P.copy(ncp, Nst[:, :, 0], eng="act")
                P.tr(psx[0:4, 0:128], ncp, ident)
                no = P.al(128)
                P.copy(no[0:4], psx[0:4, 0:128])
                P.dma(o_mn_s[l, s] if sample else o_mn_p[l], no[0:4], q="act")
                mo = P.al(4)
                P.copy(mo[0:1], Mst[0:1])
                P.dma(o_mm_s[l, s:s + 1, :] if sample else o_mm_p[l], mo[0:1], q="act")
        P.aoff = mark
        if sample or last:
            cvi = P.al(4, nseq, 3)
            if sample:
                P.copy(cvi, xpm4[:, :, :, Q:Q + 3])
            else:
                P.copy(cvi[:, :, 0, :], HIST_M[:, l, :, :])
            n3 = nseq * 3
            psv = P.ps()
            for bi in range(4):
                P.tr(psv[0:n3, bi * 128:(bi + 1) * 128], cvi[:, bi].rearrange("p s j -> p (s j)"), ident)
            cvo = P.al(512)
            P.copy(cvo[0:n3], psv[0:n3, 0:512])
            P.dma(o_mconv_s[l] if sample else o_mconv_p[l], cvo[0:n3], q="act")
        yy = P.al(4, T)
        for h in range(4):
            P.ts(yy[:, h, :], hnT[:, h, :], pc(l, PP_MNW + h), ALU.mult)
            P.stt(yy[:, h, :], xcm[:, h, :], pc(l, PP_MSKIP + h), yy[:, h, :], ALU.mult, ALU.add)
        P.tt(YB[2][:, :, 0:T], yy, PJ_CZ[:, :, 0:T], ALU.mult)

    NFB = P.sb("nfb", [128, L])
    for l in range(L):
        P.ts(NFB[0:4, l:l + 1], pc(l, PP_MFB, 1, 4), -1.0, ALU.mult)

    def hg_branch(l, T, nseq, Q, sample, last):
        q = QS if sample else 64
        nsub = T // q
        fg = P.al(4, T)
        for ct in range(4):
            P.ts(fg[:, ct, :], PJ_DF[:, ct, 0:T], OML[:, l, ct:ct + 1], ALU.mult, LB[:, l, ct:ct + 1], ALU.add)
        lf = P.al(4, T)
        P.act(lf, fg, AF.Ln)
        kT = P.al(4, T)
        P.ts(kT, fg, -1.0, ALU.mult, 1.0, ALU.add)
        qT = P.al(4, T)
        P.ts(qT, PJ_DQ[:, :, 0:T], 128.0 ** -0.5, ALU.mult)
        gT = P.al(4, T)
        mk = CT[:, C_MASKH_S:C_MASKH_S + 4 * T] if sample else CT[:, C_MASKH_P:C_MASKH_P + 4 * T]
        P.scan(gT.rearrange("p c t -> p (c t)"), mk, lf.rearrange("p c t -> p (c t)"), 0.0)
        eg = P.al(4, T)
        P.act(eg, gT, AF.Exp)
        qe = P.al(4, T)
        P.tt(qe, qT, eg, ALU.mult)
        onT = P.al(4, T)
        mark = P.aoff
        for s in range(nsub):
            P.aoff = mark
            c0 = s * q
            sl = slice(c0, c0 + q)
            if sample:
                Sst = HG_S[:]
                P.dma(HG_S[:], st_hg[l, s].rearrange("h k v -> k h v"))
            else:
                Sst = HG_P[:, l]
            gref = gT[:, :, c0 + q // 2]
            glast = gT[:, :, c0 + q - 1]
            Dx = P.al(3, 4, q)
            P.tt(Dx[:, 0], gT[:, :, sl], bc3(gref, n_last=q), ALU.subtract)
            P.ts(Dx[:, 1], Dx[:, 0], -1.0, ALU.mult)
            P.tt(Dx[:, 2], bc3(glast, n_last=q), gT[:, :, sl], ALU.subtract)
            P.act(Dx, Dx, AF.Exp)
            qg = P.al(4, q)
            kg = P.al(4, q)
            kd = P.al(4, q)
            P.tt(qg, qT[:, :, sl], Dx[:, 0], ALU.mult)
            P.tt(kg, kT[:, :, sl], Dx[:, 1], ALU.mult)
            P.tt(kd, kT[:, :, sl], Dx[:, 2], ALU.mult)
            psa = P.ps()
            for h in range(4):
                P.mm(psa[0:q, h * q:(h + 1) * q], kg[:, h, :], qg[:, h, :])
            attT = P.al(4, q)
            P.tt(attT[0:q], psa[0:q, 0:4 * q].rearrange("p (h t) -> p h t", h=4),
                 bc3(CT[0:q, C_U01:C_U01 + q], n_mid=4), ALU.mult)
            psv = P.ps()
            psk = P.ps()
            for h in range(4):
                P.tr(psv[0:q, h * 128:(h + 1) * 128], PJ_DI[:, h, sl], ident)
                P.tr(psk[0:q, h * 128:(h + 1) * 128], kd[:, h, :], ident)
            vtok = P.al(4, 128)
            kdtok = P.al(4, 128)
            P.copy(vtok[0:q], psv[0:q, :].rearrange("p (h v) -> p h v", h=4), eng="act")
            P.copy(kdtok[0:q], psk[0:q, :].rearrange("p (h v) -> p h v", h=4), eng="act")
            pso = P.ps()
            for h in range(4):
                P.mm(pso[0:q, h * 128:(h + 1) * 128], attT[0:q, h, :], vtok[0:q, h, :], start=True, stop=False)
                P.mm(pso[0:q, h * 128:(h + 1) * 128], qe[:, h, sl], Sst[:, h, :], start=False, stop=True)
            o = P.al(4, 128)
            P.copy(o[0:q], pso[0:q, :].rearrange("p (h v) -> p h v", h=4), eng="act")
            sq = P.al(4, 128)
            P.tt(sq[0:q], o[0:q], o[0:q], ALU.mult)
            ssq = P.al(4)
            P.red(ssq[0:q], sq[0:q], ALU.add)
            P.act(ssq[0:q], ssq[0:q], AF.Sqrt, bias=EPSB[0:q, 0:1], scale=1.0 / 128.0)
            P.recip(ssq[0:q], ssq[0:q])
            P.tt(o[0:q], o[0:q], bc3(ssq[0:q], n_last=128), ALU.mult)
            pst = P.ps()
            for h in range(4):
                P.tr(pst[:, h * q:(h + 1) * q], o[0:q, h, :], ident[0:q, 0:q])
            P.copy(onT[:, :, sl], pst[:, 0:4 * q].rearrange("p (h t) -> p h t", h=4), eng="act")
            egl = P.al(4)
            P.act(egl, glast, AF.Exp)
            pss = P.ps()
            for h in range(4):
                P.mm(pss[:, h * 128:(h + 1) * 128], kdtok[0:q, h, :], vtok[0:q, h, :])
            ssc = P.al(4, 128)
            for h in range(4):
                P.act(ssc[:, h, :], Sst[:, h, :], AF.Copy, scale=egl[:, h:h + 1])
            P.tt(Sst, ssc, pss[:, :].rearrange("p (h v) -> p h v", h=4), ALU.add)
            if sample or (last and s == nsub - 1):
                P.dma((o_hg_s[l, s] if sample else o_hg_p[l]).rearrange("h k v -> k h v"), Sst, q="act")
        P.aoff = mark
        yy = P.al(4, T)
        for ct in range(4):
            P.ts(yy[:, ct, :], onT[:, ct, :], pc(l, PP_HNW + ct), ALU.mult)
        P.tt(YB[3][:, :, 0:T], yy, PJ_DG[:, :, 0:T], ALU.mult)

    def norm_in(l, T):
        if NORM_CUT == 0:
            return
        sq = P.al(D)
        P.act(sq[0:T], XTK[0:T, :], AF.Square)
        ss = P.al(1)
        P.red(ss[0:T], sq[0:T], ALU.add)
        P.act(ss[0:T], ss[0:T], AF.Sqrt, bias=EPSB[0:T, 0:1], scale=1.0 / D)
        P.recip(ss[0:T], ss[0:T])
        if NORM_CUT == 1:
            return
        P.ts(sq[0:T], XTK[0:T, :], ss[0:T, 0:1], ALU.mult)
        if NORM_CUT == 2:
            return
        for half in range(2):
            ps = P.ps()
            for j in range(4):
                kt = half * 4 + j
                P.tr(ps[:, j * T:(j + 1) * T], sq[0:T, kt * 128:(kt + 1) * 128], ident[0:T, 0:T])
            if NORM_CUT == 3:
                continue
            for j in range(4):
                kt = half * 4 + j
                if NORM_CUT == 15:
                    P.copy(XN[:, kt, 0:T], ps[:, j * T:(j + 1) * T])
                elif NORM_CUT == 16:
                    P.ts(XN[:, kt, 0:T], ps[:, j * T:(j + 1) * T], pc(l, PP_NORMW + kt), ALU.mult)
                elif NORM_CUT == 13:
                    P.copy(XN[:, kt, 0:T], ps[:, j * T:(j + 1) * T], eng=("act" if j % 2 == 0 else "dve"))
                elif NORM_CUT == 14:
                    P.act(XN[:, kt, 0:T], ps[:, j * T:(j + 1) * T], AF.Copy, scale=EPSB[:, 1:2])
                elif (NORM_CUT == 11) or (NORM_CUT not in (12,) and j % 2 == 0):
                    P.act(XN[:, kt, 0:T], ps[:, j * T:(j + 1) * T], AF.Copy, scale=pc(l, PP_NORMW + kt))
                else:
                    P.ts(XN[:, kt, 0:T], ps[:, j * T:(j + 1) * T], pc(l, PP_NORMW + kt), ALU.mult)

    mctx = {"gen": None, "todo": [], "l": 0, "T": 128}

    def merge_branch_steps(l, T, i):
        w_l = w_in[l].rearrange("(kt p) c -> p kt c", p=128)
        gt = []
        for half in range(2):
            wb = wnext()
            wv = wb[:, 0:4096].rearrange("p (k c) -> p k c", k=8)
            c0 = O_MERGE + i * 1024 + half * 512
            stream(wv, w_l[:, :, c0:c0 + 512])
            ps = P.ps()
            for kt in range(8):
                P.mm(ps[0:T, :], XN[:, kt, 0:T], wv[:, kt, :], start=(kt == 0), stop=(kt == 7))
            g = PTK[half]
            P.act(g[0:T, 0:512], ps[0:T, :], AF.Sigmoid)
            gt.append(g)
            yield
        wb = wnext()
        wv = wb[:, 0:4096].rearrange("p (c d) -> p c d", c=4)
        stream(wv, w_br[l, i].rearrange("(c p) d -> p c d", p=128))
        for half in range(2):
            hs = slice(half * 512, (half + 1) * 512)
            ps = P.ps()
            for ct in range(4):
                P.mm(ps[0:T, :], YB[i][:, ct, 0:T], wv[:, ct, hs], start=(ct == 0), stop=(ct == 3))
            g = gt[half]
            if i == 0:
                P.tt(MRGK[0:T, hs], ps[0:T, :], g[0:T, 0:512], ALU.mult)
            else:
                P.tt(g[0:T, 0:512], ps[0:T, :], g[0:T, 0:512], ALU.mult)
                P.tt(MRGK[0:T, hs], MRGK[0:T, hs], g[0:T, 0:512], ALU.add)
            if half == 0:
                yield

    def pump_merge(max_i, nsteps=1):
        for _ in range(nsteps):
            if mctx["gen"] is None:
                if not (mctx["todo"] and mctx["todo"][0] <= max_i):
                    return
                mctx["gen"] = merge_branch_steps(mctx["l"], mctx["T"], mctx["todo"].pop(0))
            try:
                next(mctx["gen"])
            except StopIteration:
                mctx["gen"] = None

    def merge_out(l, T):
        while mctx["gen"] is not None or mctx["todo"]:
            pump_merge(99)
        mt8 = P.al(8, T)
        for half in range(2):
            ps = P.ps()
            for j in range(4):
                kt = half * 4 + j
                P.tr(ps[:, j * T:(j + 1) * T], MRGK[0:T, kt * 128:(kt + 1) * 128], ident[0:T, 0:T])
            P.copy(mt8[:, half * 4:half * 4 + 4, :], ps[:, 0:4 * T].rearrange("p (j t) -> p j t", j=4),
                   eng=("act" if half == 0 else "dve"))
        for half in range(2):
            hs = slice(half * 512, (half + 1) * 512)
            wb = wnext()
            wv = wb[:, 0:4096].rearrange("p (k c) -> p k c", k=8)
            stream(wv, w_out[l].rearrange("(kt p) d -> p kt d", p=128)[:, :, hs])
            ps = P.ps()
            for kt in range(8):
                P.mm(ps[0:T, :], mt8[:, kt, :], wv[:, kt, :], start=(kt == 0), stop=(kt == 7))
            P.tt(XTK[0:T, hs], XTK[0:T, hs], ps[0:T, :], ALU.add)

    tiles = [("p", i) for i in range(n_ptiles)]
    if with_sample:
        tiles.append(("s", 0))
    for (kind, ti) in tiles:
        sample = kind == "s"
        T = NSEQ_S * QS if sample else 128
        nseq = NSEQ_S if sample else 1
        Q = QS if sample else 128
        first = (ti == 0)
        last = (ti == n_ptiles - 1)
        P.aoff = 0
        src = xs_d[:, :] if sample else xp_d[ti * 128:(ti + 1) * 128, :]
        P.dma(XTK[0:T, :], src)
        for l in range(L):
            P.aoff = 0
            xps4 = XPS[:, :, 0:nseq * (Q + 3)].rearrange("p b (s q) -> p b s q", s=nseq)
            xpm4 = XPM[:, :, 0:nseq * (Q + 3)].rearrange("p b (s q) -> p b s q", s=nseq)
            if sample:
                for (src_d, xp4, nb_, cols) in ((st_sconv, xps4, 8, [0, 128, 256, 384, 512, 576, 640, 704]),
                                                (st_mconv, xpm4, 4, [0, 128, 256, 384])):
                    nch = 768 if nb_ == 8 else 512
                    nat = P.al(nch)
                    P.dma(nat[0:48], src_d[l])
                    for bi in range(nb_):
                        rows = 128 if (nb_ == 4 or bi < 4) else 64
                        psx = P.ps()
                        P.tr(psx[0:rows, 0:48], nat[0:48, cols[bi]:cols[bi] + rows], ident[0:48, 0:48])
                        P.copy(xp4[0:rows, bi, :, 0:3], psx[0:rows, 0:48].rearrange("p (s j) -> p s j", s=NSEQ_S))
            else:
                P.copy(xps4[:, :, 0, 0:3], HIST_S[:, l, :, :], eng="act")
                P.copy(xpm4[:, :, 0, 0:3], HIST_M[:, l, :, :], eng="act")
            P.aoff = 0
            norm_in(l, T)
            mctx["todo"] = [0, 1, 2, 3] if "merge" in stages else []
            mctx["gen"] = None
            mctx["l"] = l
            mctx["T"] = T
            P.cur_tag = "%s%d_proj" % (kind, ti)
            pending[:] = proj_blocks(l, T, nseq, Q) if "proj" in stages else []
            for bi_, (nm_, fn_) in enumerate((("ssd", lambda: ssd_branch(l, T, nseq, Q, sample, first, last, None)),
                                              ("s5", lambda: s5_branch(l, T, nseq, Q, sample, last)),
                                              ("ml", lambda: ml_branch(l, T, nseq, Q, sample, last)),
                                              ("hg", lambda: hg_branch(l, T, nseq, Q, sample, last)))):
                P.aoff = 0
                flush_upto(bi_)
                P.cur_tag = "%s%d_%s" % (kind, ti, nm_)
                if nm_ in stages:
                    fn_()
                else:
                    P.memset(YB[bi_][:], 0.0)
            P.aoff = 0
            P.cur_tag = "%s%d_merge" % (kind, ti)
            flush_upto(99)
            if "merge" in stages:
                merge_out(l, T)
        P.aoff = 0
        sq = P.al(D)
        P.act(sq[0:T], XTK[0:T, :], AF.Square)
        ss = P.al(1)
        P.red(ss[0:T], sq[0:T], ALU.add)
        P.act(ss[0:T], ss[0:T], AF.Sqrt, bias=EPSB[0:T, 0:1], scale=1.0 / D)
        P.recip(ss[0:T], ss[0:T])
        xo = P.al(D)
        P.stt(xo[0:T], XTK[0:T, :], ss[0:T, 0:1], FNW[0:T, :], ALU.mult, ALU.mult)
        dst = y_s[:, :] if sample else y_p[ti * 128:(ti + 1) * 128, :]
        P.dma(dst, xo[0:T], q="act")

    P.emit()
    return nc, P


_CACHE = {}


def _prep_inputs(inp, n_ptiles=16):
    f = lambda a: np.ascontiguousarray(np.asarray(a, np.float32))
    consts = _host_consts()
    pp = _host_pp(inp)
    bbd, cbd = _host_s5_bd(inp)
    shared = {
        "w_in": f(inp["w_in"]), "w_br": f(inp["w_branch"]), "w_out": f(inp["w_out"]),
        "w_glu": f(inp["s5_glu_w"]), "w_mq": f(inp["ml_wq"]), "w_mk": f(inp["ml_wk"]), "w_mv": f(inp["ml_wv"]),
        "bbd": bbd, "cbd": cbd, "consts": consts, "pp": pp,
        "fnw": f(inp["final_norm_w"]).reshape(1, D),
    }
    maps = []
    for c in range(NCORE):
        b0 = c * NSEQ_S
        sl = slice(b0, b0 + NSEQ_S)
        m = dict(shared)
        m["xp"] = f(inp["x_prompt"][c, :n_ptiles * 128])
        m["xs"] = f(inp["x_sample"][sl]).reshape(NSEQ_S * QS, D)
        m["st_sconv"] = f(inp["state_ssd_conv"][:, sl]).reshape(L, NSEQ_S * 3, 768)
        m["st_ssd"] = f(inp["state_ssd"][:, sl])
        m["st_s5re"] = f(inp["state_s5_re"][:, sl]).reshape(L, NSEQ_S, 2048)
        m["st_s5im"] = f(inp["state_s5_im"][:, sl]).reshape(L, NSEQ_S, 2048)
        m["st_mconv"] = f(inp["state_mlstm_conv"][:, sl]).reshape(L, NSEQ_S * 3, 512)
        m["st_mC"] = f(inp["state_mlstm_C"][:, sl])
        m["st_mn"] = f(inp["state_mlstm_n"][:, sl])
        m["st_mm"] = f(inp["state_mlstm_m"][:, sl])
        m["st_hg"] = f(inp["state_hgrn"][:, sl])
        maps.append(m)
    return maps


def _gather(res, n_ptiles=16):
    R = res
    cat1 = lambda k, shp: np.stack([r[k] for r in R], axis=1).reshape(shp)
    cats = lambda k, shp: np.concatenate([r[k].reshape((L, NSEQ_S) + r[k].shape[2:]) if False else r[k] for r in R], axis=1)
    y_p = np.stack([r["y_p"] for r in R], axis=0)
    y_s = np.concatenate([r["y_s"].reshape(NSEQ_S, QS, D) for r in R], axis=0)
    B = NCORE

    def P_(k, tail):
        return np.stack([r[k].reshape((L,) + tail) for r in R], axis=1)

    def S_(k, tail):
        return np.concatenate([r[k].reshape((L, NSEQ_S) + tail) for r in R], axis=1)

    outs = (
        y_p, y_s,
        P_("o_sconv_p", (3, 768)), S_("o_sconv_s", (3, 768)),
        P_("o_ssd_p", (8, 64, 64)), S_("o_ssd_s", (8, 64, 64)),
        P_("o_s5re_p", (32, 64)), S_("o_s5re_s", (32, 64)),
        P_("o_s5im_p", (32, 64)), S_("o_s5im_s", (32, 64)),
        P_("o_mconv_p", (3, 512)), S_("o_mconv_s", (3, 512)),
        P_("o_mC_p", (4, 128, 128)), S_("o_mC_s", (4, 128, 128)),
        P_("o_mn_p", (4, 128)), S_("o_mn_s", (4, 128)),
        P_("o_mm_p", (4,)), S_("o_mm_s", (4,)),
        P_("o_hg_p", (4, 128, 128)), S_("o_hg_s", (4, 128, 128)),
    )
    return tuple(np.ascontiguousarray(o.astype(np.float32)) for o in outs)


def kernel(**inputs):
    n_ptiles = inputs["x_prompt"].shape[1] // 128
    nc, _ = build(n_ptiles=n_ptiles)
    maps = _prep_inputs(inputs, n_ptiles)
    res = run_bass_kernel_spmd(nc, maps, core_ids=list(range(NCORE)))
    return _gather(res.results, n_ptiles)
```

```python
import contextlib
import math
import numpy as np
import concourse.bass as bass
import concourse.mybir as mybir
from concourse.bass_utils import run_bass_kernel_spmd

F32 = mybir.dt.float32
I32 = mybir.dt.int32
AF = mybir.ActivationFunctionType
ALU = mybir.AluOpType
AX = mybir.AxisListType

L = 2
D = 1024
NCORE = 8
SEQ = 2048
NSEQ_S = 16
QS = 4
EPS = 1e-6
D_IN = 10000
NEG = -1.0e30
SSD_CUT = 99
NORM_CUT = 99
SSD_VAR = 0
TWO_PI = 2.0 * math.pi


class _Op:
    __slots__ = ("eng", "fn", "deps", "is_dma", "idx", "sig", "needed")

    def __init__(self, eng, fn, is_dma):
        self.eng = eng
        self.fn = fn
        self.deps = set()
        self.is_dma = is_dma
        self.sig = None
        self.needed = False


def _region(ap):
    t = ap.tensor
    tn = type(t).__name__
    if not (tn.startswith("SBTensor") or tn.startswith("PSum")):
        return None
    fs = 1
    for s in list(t.shape)[1:]:
        fs *= int(s)
    off = int(ap.offset)
    p0 = off // fs
    f0 = off % fs
    dims = list(ap.ap)
    pstep, pcnt = int(dims[0][0]), int(dims[0][1])
    if pstep == 0 or pcnt == 1:
        p1 = p0 + 1
    else:
        assert pstep == fs, (t.name, pstep, fs)
        p1 = p0 + pcnt
    ext = 1
    for st, cn in dims[1:]:
        ext += (int(cn) - 1) * abs(int(st))
    return (t.name, p0, p1, f0, f0 + ext)


def _isap(x):
    return x is not None and not isinstance(x, (int, float))


class Prog:
    def __init__(self, nc):
        self.nc = nc
        self.ops = []
        self.acc = {}
        self.stack = contextlib.ExitStack()
        self.psum_banks = []
        self.psum_i = 0
        self.arena = None
        self.aoff = 0
        self.tags = []

    def sb(self, name, shape, dtype=F32):
        return self.stack.enter_context(self.nc.sbuf_tensor(name, list(shape), dtype))

    def alloc_psum(self, n=8):
        for i in range(n):
            self.psum_banks.append(
                self.stack.enter_context(self.nc.psum_tensor("psb%d" % i, [128, 512], F32)))

    def ps(self):
        t = self.psum_banks[self.psum_i % len(self.psum_banks)]
        self.psum_i += 1
        return t

    def al(self, *shape, rows=128):
        n = 1
        for s in shape:
            n *= s
        off = self.aoff
        self.aoff += n
        assert self.aoff <= self.arena_n, ("arena overflow", self.aoff)
        v = self.arena[0:rows, off:off + n]
        if len(shape) == 2:
            v = v.rearrange("p (a b) -> p a b", a=shape[0])
        elif len(shape) == 3:
            v = v.rearrange("p (a b c) -> p a b c", a=shape[0], b=shape[1])
        return v

    def op(self, eng, fn, reads, writes, is_dma=False):
        o = _Op(eng, fn, is_dma)
        o.idx = len(self.ops)
        self.tags.append(getattr(self, "cur_tag", ""))
        engkey = ("dma", o.idx) if is_dma else eng
        rr = [r for r in (_region(a) for a in reads if _isap(a)) if r]
        ww = [r for r in (_region(a) for a in writes if _isap(a)) if r]
        if eng == "pe":
            ww = [(n, 0, 128, 0, 512) if n.startswith("psb") else (n, p0, p1, f0, f1) for (n, p0, p1, f0, f1) in ww]
        for (n, p0, p1, f0, f1) in rr:
            for e in self.acc.get(n, ()):
                if e[5] and e[0] < p1 and p0 < e[1] and e[2] < f1 and f0 < e[3]:
                    o.deps.add(e[4])
        for (n, p0, p1, f0, f1) in ww:
            for e in self.acc.get(n, ()):
                if e[0] < p1 and p0 < e[1] and e[2] < f1 and f0 < e[3]:
                    o.deps.add(e[4])
        for (n, p0, p1, f0, f1) in rr:
            if n.startswith("psb"):
                for e in self.acc.get(n, ()):
                    if (not e[5]) and e[6] != engkey:
                        o.deps.add(e[4])
        for (n, p0, p1, f0, f1) in ww:
            lst = self.acc.setdefault(n, [])
            lst[:] = [e for e in lst if not (p0 <= e[0] and e[1] <= p1 and f0 <= e[2] and e[3] <= f1)]
            lst.append([p0, p1, f0, f1, o.idx, True, engkey])
        for (n, p0, p1, f0, f1) in rr:
            lst = self.acc.setdefault(n, [])
            if not is_dma:
                lst[:] = [e for e in lst if not ((not e[5]) and e[6] == engkey and p0 <= e[0] and e[1] <= p1
                                                 and f0 <= e[2] and e[3] <= f1)]
            lst.append([p0, p1, f0, f1, o.idx, False, engkey])
        o.deps.discard(o.idx)
        self.ops.append(o)
        return o

    def mm(self, out, lhsT, rhs, start=True, stop=True):
        rd = [lhsT, rhs] + ([] if start else [out])
        return self.op("pe", lambda e: e.matmul(out, lhsT, rhs, start=start, stop=stop), rd, [out])

    def tr(self, out, in_, ident):
        return self.op("pe", lambda e: e.transpose(out, in_, ident), [in_, ident], [out])

    def act(self, out, in_, func, bias=0.0, scale=1.0):
        rd = [in_, bias, scale]
        return self.op("act", lambda e: e.activation(out, in_, func, bias=bias, scale=scale), rd, [out])

    def tt(self, out, in0, in1, op, eng="dve"):
        return self.op(eng, lambda e: e.tensor_tensor(out, in0, in1, op), [in0, in1], [out])

    def ts(self, out, in0, s1, op0, s2=None, op1=None, eng="dve"):
        rd = [in0, s1, s2]
        if op1 is None:
            return self.op(eng, lambda e: e.tensor_scalar(out, in0, s1, None, op0), rd, [out])
        return self.op(eng, lambda e: e.tensor_scalar(out, in0, s1, s2, op0, op1), rd, [out])

    def stt(self, out, in0, scalar, in1, op0, op1):
        rd = [in0, in1, scalar]
        return self.op("dve", lambda e: e.scalar_tensor_tensor(out, in0, scalar, in1, op0, op1), rd, [out])

    def copy(self, out, in_, eng="dve"):
        if eng == "act":
            return self.op("act", lambda e: e.activation(out, in_, AF.Copy), [in_], [out])
        return self.op(eng, lambda e: e.tensor_copy(out, in_), [in_], [out])

    def red(self, out, in_, op, axis=AX.X):
        return self.op("dve", lambda e: e.tensor_reduce(out, in_, axis, op), [in_], [out])

    def scan(self, out, d0, d1, init, op0=ALU.mult, op1=ALU.add):
        rd = [d0, d1, init]
        return self.op("dve", lambda e: e.tensor_tensor_scan(out, d0, d1, init, op0, op1), rd, [out])

    def recip(self, out, in_):
        return self.op("dve", lambda e: e.reciprocal(out, in_), [in_], [out])

    def memset(self, out, v, eng="dve"):
        return self.op(eng, lambda e: e.memset(out, v), [], [out])

    def dma(self, out, in_, q="sp"):
        return self.op(q, lambda e: e.dma_start(out=out, in_=in_), [in_], [out], is_dma=True)

    def emit(self, n_dma_sems=12):
        nc = self.nc
        ops = self.ops
        for o in ops:
            if o.is_dma:
                o.needed = True
        engs = ["pe", "act", "dve", "pool", "sp"]
        stack = self.stack
        csem = {e: stack.enter_context(nc.semaphore("cs_" + e)) for e in engs}
        dsem = {e: [stack.enter_context(nc.semaphore("ds_%s%d" % (e, i))) for i in range(n_dma_sems)]
                for e in ("sp", "pool", "act")}
        dcount = {e: [0] * n_dma_sems for e in dsem}
        dlast = {e: [None] * n_dma_sems for e in dsem}
        drr = {e: 0 for e in dsem}
        for o in ops:
            if o.is_dma:
                k = drr[o.eng] % n_dma_sems
                drr[o.eng] += 1
                if dlast[o.eng][k] is not None:
                    o.deps.add(dlast[o.eng][k])
                dlast[o.eng][k] = o.idx
        for o in ops:
            if o.eng == "pe":
                o.deps = {d for d in o.deps if ops[d].eng != "pe"}
        for o in ops:
            for d in o.deps:
                ops[d].needed = True
        ccount = {e: 0 for e in engs}
        drr = {e: 0 for e in dsem}
        for o in ops:
            if o.is_dma:
                k = drr[o.eng] % n_dma_sems
                drr[o.eng] += 1
                dcount[o.eng][k] += 16
                o.sig = (dsem[o.eng][k], dcount[o.eng][k], ("d", o.eng, k))
            elif o.needed:
                ccount[o.eng] += 1
                o.sig = (csem[o.eng], ccount[o.eng], ("c", o.eng))
        per = {e: [o for o in ops if o.eng == e] for e in engs}
        self.trace = {e: [] for e in engs}
        self.stats = {e: len(per[e]) for e in engs}

        def run(engname, eng):
            waited = {}
            for o in per[engname]:
                need = {}
                for d in o.deps:
                    s, v, key = ops[d].sig
                    if waited.get(key, 0) >= v:
                        continue
                    if key not in need or need[key][1] < v:
                        need[key] = (s, v)
                for key, (s, v) in need.items():
                    eng.wait_ge(s, v)
                    waited[key] = v
                self.trace[engname].append(([(k_, v_[1]) for k_, v_ in need.items()], o.sig[2] if o.sig else None, o.idx))
                ins = o.fn(eng)
                if o.sig is not None:
                    ins.then_inc(o.sig[0], 16 if o.is_dma else 1)
            if engname in dsem:
                for k in range(n_dma_sems):
                    if dcount[engname][k] > 0 and waited.get(("d", engname, k), 0) < dcount[engname][k]:
                        eng.wait_ge(dsem[engname][k], dcount[engname][k])

        with nc.Block() as block:
            @block.tensor
            def _(e):
                run("pe", e)

            @block.scalar
            def _(e):
                run("act", e)

            @block.vector
            def _(e):
                run("dve", e)

            @block.gpsimd
            def _(e):
                run("pool", e)

            @block.sync
            def _(e):
                run("sp", e)
        self.stack.close()

    def check_deadlock(self):
        sem = {}
        pos = {e: 0 for e in self.trace}
        total = sum(len(v) for v in self.trace.values())
        done = 0
        while done < total:
            prog = False
            for e, lst in self.trace.items():
                while pos[e] < len(lst):
                    waits, sig, idx = lst[pos[e]]
                    if all(sem.get(k, 0) >= v for k, v in waits):
                        if sig is not None:
                            sem[sig] = sem.get(sig, 0) + (16 if sig[0] == "d" else 1)
                        pos[e] += 1
                        done += 1
                        prog = True
                    else:
                        break
            if not prog:
                return {e: (pos[e], self.trace[e][pos[e]] if pos[e] < len(self.trace[e]) else None) for e in self.trace}
        return None


O_AZ = 0
O_XBC = 512
O_DT = 1280
O_U = 1288
O_GATE = 1800
O_CX = 2312
O_CZ = 2824
O_CO = 3336
O_CI = 3848
O_CF = 3852
O_DF = 3856
O_DI = 4368
O_DQ = 4880
O_DG = 5392
O_MERGE = 5904

C_ID = 0
C_ONES = 128
C_NEGU = 256
C_NEGL = 384
C_U01 = 512
C_SEL128 = 640
C_SEL4 = 768
C_TAU = 896
C_MASK4 = 960
C_MASKH_P = 1024
C_MASKH_S = 1536
NCONST = 1792

PP_NORMW = 0
PP_SCONV = 8
PP_DTB = 48
PP_ALOG = 49
PP_SSDD = 50
PP_SSDNW = 54
PP_S5D = 58
PP_ARE = 62
PP_AIM = 78
PP_LDT = 94
PP_MCONV = 110
PP_MIB = 130
PP_MFB = 131
PP_MNW = 132
PP_MSKIP = 136
PP_HL0 = 140
PP_HL1 = 144
PP_HNW = 148
NPP = 152

WB = 4096
NBUF = 3


def _host_consts():
    c = np.zeros((128, NCONST), np.float32)
    i = np.arange(128)[:, None]
    j = np.arange(128)[None, :]
    c[:, C_ID:C_ID + 128] = (i == j)
    c[:, C_ONES:C_ONES + 128] = 1.0
    c[:, C_NEGU:C_NEGU + 128] = np.where(j >= i, 0.0, NEG)
    c[:, C_NEGL:C_NEGL + 128] = np.where(j <= i, 0.0, NEG)
    c[:, C_U01:C_U01 + 128] = (j >= i)
    c[127, C_SEL128:C_SEL128 + 128] = 1.0
    c[3, C_SEL4:C_SEL4 + 128] = 1.0
    c[:, C_TAU:C_TAU + 64] = np.arange(1, 65)[None, :]
    c[:, C_MASK4:C_MASK4 + 64] = (np.arange(64) % 4 != 0)[None, :]
    c[:, C_MASKH_P:C_MASKH_P + 512] = (np.arange(512) % 64 != 0)[None, :]
    c[:, C_MASKH_S:C_MASKH_S + 256] = (np.arange(256) % 4 != 0)[None, :]
    return c


def _col(v, ntile):
    return np.ascontiguousarray(np.asarray(v, np.float32).reshape(ntile, 128).T)


def _host_pp(inp):
    pp = np.zeros((L, 128, NPP), np.float32)
    for l in range(L):
        p = pp[l]
        p[:, PP_NORMW:PP_NORMW + 8] = _col(inp["norm_w"][l], 8)
        cw = inp["ssd_conv_w"][l]
        cb = inp["ssd_conv_b"][l]
        blocks = [(0, 128), (128, 128), (256, 128), (384, 128), (512, 64), (576, 64), (640, 64), (704, 64)]
        for bi, (c0, n) in enumerate(blocks):
            for jj in range(4):
                p[0:n, PP_SCONV + bi * 5 + jj] = cw[jj, c0:c0 + n]
            p[0:n, PP_SCONV + bi * 5 + 4] = cb[c0:c0 + n]
        p[0:8, PP_DTB] = inp["ssd_dt_bias"][l]
        p[0:8, PP_ALOG] = inp["ssd_A_log"][l]
        p[:, PP_SSDD:PP_SSDD + 4] = _col(np.repeat(inp["ssd_D"][l], 64), 4)
        p[:, PP_SSDNW:PP_SSDNW + 4] = _col(inp["ssd_norm_w"][l], 4)
        p[:, PP_S5D:PP_S5D + 4] = _col(inp["s5_D"][l], 4)
        p[:, PP_ARE:PP_ARE + 16] = _col(inp["s5_A_re"][l].reshape(-1), 16)
        p[:, PP_AIM:PP_AIM + 16] = _col(inp["s5_A_im"][l].reshape(-1), 16)
        p[:, PP_LDT:PP_LDT + 16] = _col(np.repeat(inp["s5_log_dt"][l], 64), 16)
        mw = inp["ml_conv_w"][l]
        mb = inp["ml_conv_b"][l]
        for bi in range(4):
            for jj in range(4):
                p[:, PP_MCONV + bi * 5 + jj] = mw[jj, bi * 128:(bi + 1) * 128]
            p[:, PP_MCONV + bi * 5 + 4] = mb[bi * 128:(bi + 1) * 128]
        p[0:4, PP_MIB] = inp["ml_i_bias"][l]
        p[0:4, PP_MFB] = inp["ml_f_bias"][l]
        p[:, PP_MNW:PP_MNW + 4] = _col(inp["ml_norm_w"][l], 4)
        p[:, PP_MSKIP:PP_MSKIP + 4] = _col(inp["ml_skip"][l], 4)
        p[:, PP_HL0:PP_HL0 + 4] = _col(inp["hg_lb_logits"][0], 4)
        p[:, PP_HL1:PP_HL1 + 4] = _col(inp["hg_lb_logits"][1], 4)
        p[:, PP_HNW:PP_HNW + 4] = _col(inp["hg_norm_w"][l], 4)
    return pp


def _host_s5_bd(inp):
    bbd = np.zeros((L, 2, 128, 4, 512), np.float32)
    cbd = np.zeros((L, 2, 128, 16, 128), np.float32)
    for l in range(L):
        for ri, (bn, cn) in enumerate((("s5_B_re", "s5_C_re"), ("s5_B_im", "s5_C_im"))):
            B = inp[bn][l]
            C = inp[cn][l]
            for g in range(32):
                ct = g // 8
                gl8 = g % 8
                sl = gl8 // 2
                g2 = gl8 % 2
                st = ct * 4 + sl
                bbd[l, ri, gl8 * 16:(gl8 + 1) * 16, ct, sl * 128 + g2 * 64: sl * 128 + (g2 + 1) * 64] = B[g].T
                cbd[l, ri, g2 * 64:(g2 + 1) * 64, st, gl8 * 16:(gl8 + 1) * 16] = C[g].T
    return bbd, cbd


def build(n_ptiles=16, with_sample=True, stages=("ssd", "s5", "ml", "hg", "merge", "proj")):
    nc = bass.Bass("TRN2", target_bir_lowering=False)
    NT = n_ptiles * 128

    def din(name, shape):
        return nc.dram_tensor(name, list(shape), F32, kind="ExternalInput").ap()

    def dout(name, shape):
        return nc.dram_tensor(name, list(shape), F32, kind="ExternalOutput").ap()

    xp_d = din("xp", [NT, D])
    xs_d = din("xs", [NSEQ_S * QS, D])
    st_sconv = din("st_sconv", [L, NSEQ_S * 3, 768])
    st_ssd = din("st_ssd", [L, NSEQ_S, 8, 64, 64])
    st_s5re = din("st_s5re", [L, NSEQ_S, 2048])
    st_s5im = din("st_s5im", [L, NSEQ_S, 2048])
    st_mconv = din("st_mconv", [L, NSEQ_S * 3, 512])
    st_mC = din("st_mC", [L, NSEQ_S, 4, 128, 128])
    st_mn = din("st_mn", [L, NSEQ_S, 4, 128])
    st_mm = din("st_mm", [L, NSEQ_S, 4])
    st_hg = din("st_hg", [L, NSEQ_S, 4, 128, 128])
    w_in = din("w_in", [L, D, D_IN])
    w_br = din("w_br", [L, 4, 512, D])
    w_out = din("w_out", [L, D, D])
    w_glu = din("w_glu", [L, 512, 512])
    w_mq = din("w_mq", [L, 4, 128, 128])
    w_mk = din("w_mk", [L, 4, 128, 128])
    w_mv = din("w_mv", [L, 4, 128, 128])
    bbd_d = din("bbd", [L, 2, 128, 4, 512])
    cbd_d = din("cbd", [L, 2, 128, 16, 128])
    const_d = din("consts", [128, NCONST])
    pp_d = din("pp", [L, 128, NPP])
    fnw_d = din("fnw", [1, D])

    y_p = dout("y_p", [NT, D])
    y_s = dout("y_s", [NSEQ_S * QS, D])
    o_sconv_p = dout("o_sconv_p", [L, 3, 768])
    o_sconv_s = dout("o_sconv_s", [L, NSEQ_S * 3, 768])
    o_ssd_p = dout("o_ssd_p", [L, 8, 64, 64])
    o_ssd_s = dout("o_ssd_s", [L, NSEQ_S, 8, 64, 64])
    o_s5re_p = dout("o_s5re_p", [L, 16, 128])
    o_s5re_s = dout("o_s5re_s", [L, NSEQ_S, 2048])
    o_s5im_p = dout("o_s5im_p", [L, 16, 128])
    o_s5im_s = dout("o_s5im_s", [L, NSEQ_S, 2048])
    o_mconv_p = dout("o_mconv_p", [L, 3, 512])
    o_mconv_s = dout("o_mconv_s", [L, NSEQ_S * 3, 512])
    o_mC_p = dout("o_mC_p", [L, 4, 128, 128])
    o_mC_s = dout("o_mC_s", [L, NSEQ_S, 4, 128, 128])
    o_mn_p = dout("o_mn_p", [L, 4, 128])
    o_mn_s = dout("o_mn_s", [L, NSEQ_S, 4, 128])
    o_mm_p = dout("o_mm_p", [L, 1, 4])
    o_mm_s = dout("o_mm_s", [L, NSEQ_S, 4])
    o_hg_p = dout("o_hg_p", [L, 4, 128, 128])
    o_hg_s = dout("o_hg_s", [L, NSEQ_S, 4, 128, 128])

    P = Prog(nc)
    P.alloc_psum(8)
    ARENA_N = 9216 + 512
    P.arena = P.sb("arena", [128, ARENA_N])
    P.arena_n = ARENA_N

    HT_P = P.sb("hT_p", [128, L, 2, 256])
    CT = P.sb("consts_t", [128, NCONST])
    PPT = P.sb("pp_t", [128, L, NPP])
    FNW = P.sb("fnw_t", [128, D])
    WBUF = [P.sb("wbuf%d" % i, [128, WB]) for i in range(NBUF)]
    wctr = [0]

    def wnext():
        b = WBUF[wctr[0] % NBUF]
        wctr[0] += 1
        return b

    ident = CT[:, C_ID:C_ID + 128]
    ones = CT[:, C_ONES:C_ONES + 128]

    AH = P.sb("ssdA", [128, L])
    LB = P.sb("hg_lb", [128, L, 4])
    OML = P.sb("hg_oml", [128, L, 4])
    COS = P.sb("s5cos", [128, L, 16, 64])
    SIN = P.sb("s5sin", [128, L, 16, 64])
    RHO = P.sb("s5rho", [128, L, 16])
    CR = P.sb("s5cr", [128, L, 16])
    CI = P.sb("s5ci", [128, L, 16])
    E2R = P.sb("s5e2r", [128, L, 16, 64])
    E2I = P.sb("s5e2i", [128, L, 16, 64])

    XTK = P.sb("x_tok", [128, D])
    PTK = [P.sb("ptk%d" % i, [128, 520]) for i in range(2)]
    XN = P.sb("xnT", [128, 8, 128])
    PJ_Z = P.sb("pj_z", [128, 4, 128])
    XPS = P.sb("xp_ssd", [128, 8, 131])
    PJ_DT = P.sb("pj_dt", [128, 128])
    PJ_U = P.sb("pj_u", [128, 4, 128])
    PJ_GATE = P.sb("pj_gate", [128, 4, 128])
    XPM = P.sb("xp_ml", [128, 4, 131])
    PJ_CZ = P.sb("pj_cz", [128, 4, 128])
    PJ_CO = P.sb("pj_co", [128, 4, 128])
    PJ_CI = P.sb("pj_ci", [128, 128])
    PJ_CF = P.sb("pj_cf", [128, 128])
    PJ_DF = P.sb("pj_df", [128, 4, 128])
    PJ_DI = P.sb("pj_di", [128, 4, 128])
    PJ_DQ = P.sb("pj_dq", [128, 4, 128])
    PJ_DG = P.sb("pj_dg", [128, 4, 128])
    YB = [P.sb("ybr%d" % i, [128, 4, 128]) for i in range(4)]
    MRGK = P.sb("merged_tok", [128, D])

    HIST_S = P.sb("hist_s", [128, L, 8, 3])
    HIST_M = P.sb("hist_m", [128, L, 4, 3])
    S5R_P = P.sb("s5r_p", [128, L, 16])
    S5I_P = P.sb("s5i_p", [128, L, 16])
    MC_P = P.sb("mC_p", [128, L, 4, 128])
    MN_P = P.sb("mn_p", [128, L, 4, 2])
    MM_P = P.sb("mm_p", [128, L, 4])
    HG_P = P.sb("hg_p", [128, L, 4, 128])
    HT_S = P.sb("hT_s", [128, 2, 256])
    HNAT = P.sb("hnat", [128, 8, 64])
    S5R_S = P.sb("s5r_s", [128, 16, NSEQ_S])
    S5I_S = P.sb("s5i_s", [128, 16, NSEQ_S])
    MC_S = P.sb("mC_s", [128, 4, 128])
    MN_S = P.sb("mn_s", [128, 4, 2])
    MM_S = P.sb("mm_s", [128, 4])
    HG_S = P.sb("hg_s", [128, 4, 128])

    def pc(l, col, n=1, rows=128):
        return PPT[0:rows, l, col:col + n]

    P.dma(CT[:], const_d[:, :])
    P.dma(PPT[:], pp_d.rearrange("l p c -> p l c"))
    P.dma(FNW[:], fnw_d[0:1, :].partition_broadcast(128))
    for t_, v_ in ((HIST_S, 0.0), (HIST_M, 0.0), (S5R_P, 0.0), (S5I_P, 0.0), (MC_P, 0.0),
                   (MN_P, 0.0), (MM_P, 0.0), (HG_P, 0.0)):
        P.memset(t_[:], v_)
    P.memset(HT_P[:], 0.0)
    for t_ in (XPS, XPM, PJ_DT, PJ_CI, PJ_CF, P.arena):
        P.memset(t_[:], 0.0)

    def range_reduce(a, tmpf, tmpi):
        P.ts(tmpf, a, 1.0 / TWO_PI, ALU.mult)
        P.copy(tmpi, tmpf)
        P.copy(tmpf, tmpi)
        P.stt(a, tmpf, -TWO_PI, a, ALU.mult, ALU.add)
        P.ts(tmpf, a, math.pi, ALU.is_gt, TWO_PI, ALU.mult)
        P.tt(a, a, tmpf, ALU.subtract)
        P.ts(tmpf, a, -math.pi, ALU.is_lt, TWO_PI, ALU.mult)
        P.tt(a, a, tmpf, ALU.add)
        P.ts(a, a, math.pi, ALU.min, -math.pi, ALU.max)

    for l in range(L):
        P.aoff = 0
        P.act(AH[0:8, l:l + 1], pc(l, PP_ALOG, 1, 8), AF.Exp)
        P.ts(AH[0:8, l:l + 1], AH[0:8, l:l + 1], -1.0, ALU.mult)
        if l == 0:
            P.memset(LB[:, 0, :], 0.0)
        else:
            dl_ = P.al(4)
            P.tt(dl_, pc(l, PP_HL1, 4), pc(l, PP_HL0, 4), ALU.subtract)
            P.act(LB[:, l, :], dl_, AF.Sigmoid)
        P.ts(OML[:, l, :], LB[:, l, :], -1.0, ALU.mult, 1.0, ALU.add)
        dt = P.al(16)
        P.act(dt, pc(l, PP_LDT, 16), AF.Exp)
        lrdt = P.al(16)
        P.tt(lrdt, pc(l, PP_ARE, 16), dt, ALU.mult)
        P.act(RHO[:, l, :], lrdt, AF.Exp)
        th = P.al(16)
        P.tt(th, pc(l, PP_AIM, 16), dt, ALU.mult)
        ang = P.al(16, 64)
        tmpf = P.al(16, 64)
        tau = CT[:, C_TAU:C_TAU + 64]
        P.tt(ang, th.unsqueeze(2).to_broadcast([128, 16, 64]), tau.unsqueeze(1).to_broadcast([128, 16, 64]),
             ALU.mult)
        ang2 = P.al(16, 64)
        P.ts(ang2, ang, math.pi / 2.0, ALU.add)
        ti3 = P.al(16, 64).bitcast(I32)
        range_reduce(ang, tmpf, ti3)
        range_reduce(ang2, tmpf, ti3)
        P.act(SIN[:, l, :, :], ang, AF.Sin)
        P.act(COS[:, l, :, :], ang2, AF.Sin)
        abr = P.al(16)
        abi = P.al(16)
        P.tt(abr, RHO[:, l, :], COS[:, l, :, 0], ALU.mult)
        P.tt(abi, RHO[:, l, :], SIN[:, l, :, 0], ALU.mult)
        am1 = P.al(16)
        P.ts(am1, abr, -1.0, ALU.add)
        lr = pc(l, PP_ARE, 16)
        li = pc(l, PP_AIM, 16)
        den = P.al(16)
        t0 = P.al(16)
        P.tt(den, lr, lr, ALU.mult)
        P.tt(t0, li, li, ALU.mult)
        P.tt(den, den, t0, ALU.add)
        P.recip(den, den)
        t1 = P.al(16)
        P.tt(t0, am1, lr, ALU.mult)
        P.tt(t1, abi, li, ALU.mult)
        P.tt(t0, t0, t1, ALU.add)
        P.tt(CR[:, l, :], t0, den, ALU.mult)
        P.tt(t0, abi, lr, ALU.mult)
        P.tt(t1, am1, li, ALU.mult)
        P.tt(t0, t0, t1, ALU.subtract)
        P.tt(CI[:, l, :], t0, den, ALU.mult)
        crb_ = CR[:, l, :].unsqueeze(2).to_broadcast([128, 16, 64])
        cib_ = CI[:, l, :].unsqueeze(2).to_broadcast([128, 16, 64])
        P.tt(ang, COS[:, l, :, :], crb_, ALU.mult)
        P.tt(ang2, SIN[:, l, :, :], cib_, ALU.mult)
        P.tt(E2R[:, l, :, :], ang, ang2, ALU.add)
        P.tt(ang, COS[:, l, :, :], cib_, ALU.mult)
        P.tt(ang2, SIN[:, l, :, :], crb_, ALU.mult)
        P.tt(E2I[:, l, :, :], ang, ang2, ALU.subtract)

    def rmsnorm_fm(src, n, T, wcol_l, wcol, dst, dmodel):
        sq = P.al(n, T)
        P.act(sq, src, AF.Square)
        ps = P.ps()
        for k in range(n):
            P.mm(ps[:, 0:T], ones, sq[:, k, :], start=(k == 0), stop=(k == n - 1))
        rstd = P.al(T)
        P.act(rstd, ps[:, 0:T], AF.Sqrt, bias=EPSB[:, 0:1], scale=1.0 / dmodel)
        P.recip(rstd, rstd)
        for k in range(n):
            P.stt(dst[:, k, :], src[:, k, :], pc(wcol_l, wcol + k), rstd, ALU.mult, ALU.mult)

    EPSB = P.sb("epsb", [128, 2])
    P.memset(EPSB[:, 0:1], EPS)
    P.memset(EPSB[:, 1:2], 1.0)

    def bc3(ap2, n_mid=None, n_last=None):
        rows = ap2.shape[0]
        if n_last is not None:
            return ap2.unsqueeze(2).to_broadcast([rows, ap2.shape[1], n_last])
        return ap2.unsqueeze(1).to_broadcast([rows, n_mid, ap2.shape[1]])

    def stream(dst_view, src):
        P.dma(dst_view, src)

    def proj_blocks(l, T, nseq, Q):
        w_l = w_in[l].rearrange("(kt p) c -> p kt c", p=128)
        xps4 = XPS[:, :, 0:nseq * (Q + 3)].rearrange("p b (s q) -> p b s q", s=nseq)
        xpm4 = XPM[:, :, 0:nseq * (Q + 3)].rearrange("p b (s q) -> p b s q", s=nseq)

        def seqv(ps_ap):
            return ps_ap.rearrange("p (s q) -> p s q", s=nseq)

        groups = []

        def ev_act(dst_fn, func, bias=None, scale=1.0):
            def f(ps_ap, bi, rows):
                P.act(dst_fn(bi, rows), ps_ap, func, bias=(bias(rows) if bias else 0.0), scale=scale)
            return f

        def simple(col0, tile, func):
            blks = [(i * 128, 128, None) for i in range(4)]
            groups.append((col0, 512, blks,
                           (lambda ps2, t=tile, f=func: P.act(t[:, :, 0:T], ps2[:, 0:4 * T].rearrange("p (i t) -> p i t", i=4), f))))

        simple(O_AZ, PJ_Z, AF.Silu)
        groups.append((O_XBC, 512, [(i * 128, 128, None) for i in range(4)],
                       (lambda ps2: P.act(xps4[:, 0:4, :, 3:3 + Q],
                                          ps2[:, 0:4 * T].rearrange("p (i s q) -> p i s q", i=4, s=nseq), AF.Copy))))
        blks = [(j * 64, 64, (lambda ps_ap, bi, rows, j=j: P.act(xps4[0:rows, 4 + j, :, 3:3 + Q], seqv(ps_ap), AF.Copy)))
                for j in range(4)]
        blks.append((256, 8, (lambda ps_ap, bi, rows: P.act(PJ_DT[0:rows, 0:T], ps_ap, AF.Copy))))
        groups.append((O_XBC + 512, 264, blks))
        simple(O_U, PJ_U, AF.Copy)
        simple(O_GATE, PJ_GATE, AF.Silu)
        groups.append((O_CX, 512, [(i * 128, 128, None) for i in range(4)],
                       (lambda ps2: P.act(xpm4[:, 0:4, :, 3:3 + Q],
                                          ps2[:, 0:4 * T].rearrange("p (i s q) -> p i s q", i=4, s=nseq), AF.Copy))))
        simple(O_CZ, PJ_CZ, AF.Silu)
        simple(O_CO, PJ_CO, AF.Sigmoid)
        groups.append((O_CI, 8, [
            (0, 4, (lambda ps_ap, bi, rows: P.act(PJ_CI[0:rows, 0:T], ps_ap, AF.Identity, bias=pc(l, PP_MIB, 1, 4)))),
            (4, 4, (lambda ps_ap, bi, rows: P.act(PJ_CF[0:rows, 0:T], ps_ap, AF.Copy)))]))
        simple(O_DF, PJ_DF, AF.Sigmoid)
        simple(O_DI, PJ_DI, AF.Copy)
        simple(O_DQ, PJ_DQ, AF.Silu)
        simple(O_DG, PJ_DG, AF.Silu)

        tags = [0, 0, 0, 1, 1, 2, 2, 2, 2, 3, 3, 3, 3]
        assert len(tags) == len(groups)

        def run_group(grp, gi):
            col0, ncols, blks = grp[0], grp[1], grp[2]
            gevac = grp[3] if len(grp) > 3 else None
            wb = wnext()
            wv = wb[:, 0:8 * ncols].rearrange("p (k c) -> p k c", k=8)
            stream(wv, w_l[:, :, col0:col0 + ncols])
            ps = P.ps()
            for kt in range(8):
                P.mm(ps[0:T, 0:ncols], XN[:, kt, 0:T], wv[:, kt, 0:ncols], start=(kt == 0), stop=(kt == 7))
            tk = PTK[gi % 2]
            P.copy(tk[0:T, 0:ncols], ps[0:T, 0:ncols], eng=("act" if gi % 2 == 0 else "dve"))
            for b0 in range(0, len(blks), 4):
                sub = blks[b0:b0 + 4]
                ps2 = P.ps()
                for bj, (co, n, evac) in enumerate(sub):
                    P.tr(ps2[0:n, bj * T:(bj + 1) * T], tk[0:T, co:co + n], ident[0:T, 0:T])
                if gevac is not None:
                    gevac(ps2)
                else:
                    for bj, (co, n, evac) in enumerate(sub):
                        evac(ps2[0:n, bj * T:(bj + 1) * T], None, n)

        return [(tags[gi], (lambda g=grp, gi=gi: run_group(g, gi))) for gi, grp in enumerate(groups)]

    pending = []

    def pump(n=1):
        for _ in range(n):
            if pending:
                pending.pop(0)[1]()

    def flush_upto(tag):
        while pending and pending[0][0] <= tag:
            pending.pop(0)[1]()

    def conv_fm(xp4, nblk_rows, l, ppbase, acc4, Q):
        for bi, rows in enumerate(nblk_rows):
            c = ppbase + bi * 5
            P.ts(acc4[0:rows, bi], xp4[0:rows, bi, :, 0:Q], pc(l, c, 1, rows), ALU.mult,
                 pc(l, c + 4, 1, rows), ALU.add)
            for j in range(1, 4):
                P.stt(acc4[0:rows, bi], xp4[0:rows, bi, :, j:j + Q], pc(l, c + j, 1, rows), acc4[0:rows, bi],
                      ALU.mult, ALU.add)

    def ssd_branch(l, T, nseq, Q, sample, first, last, core_out):
        xps4 = XPS[:, :, 0:nseq * (Q + 3)].rearrange("p b (s q) -> p b s q", s=nseq)
        rows8 = [128] * 4 + [64] * 4
        acc = P.al(8, nseq, Q)
        conv_fm(xps4, rows8, l, PP_SCONV, acc, Q)
        xc = P.al(8, T)
        accf = acc.rearrange("p b s q -> p b (s q)")
        P.act(xc[:, 0:4, :], accf[:, 0:4, :], AF.Silu)
        P.act(xc[0:64, 4:8, :], accf[0:64, 4:8, :], AF.Silu)
        if SSD_CUT == 1:
            P.memset(YB[0][:], 0.0)
            return
        if not sample:
            P.copy(HIST_S[:, l, :, :], xps4[:, :, 0, Q:Q + 3], eng="act")
        dte = P.al(T)
        P.act(dte[0:8], PJ_DT[0:8, 0:T], AF.Exp, bias=pc(l, PP_DTB, 1, 8))
        dtT = P.al(T)
        P.act(dtT[0:8], dte[0:8], AF.Ln, bias=EPSB[0:8, 1:2])
        aT = P.al(T)
        P.ts(aT[0:8], dtT[0:8], AH[0:8, l:l + 1], ALU.mult)
        acT = P.al(T)
        if sample:
            P.scan(acT[0:8], CT[0:8, C_MASK4:C_MASK4 + T], aT[0:8], 0.0)
        else:
            P.scan(acT[0:8], CT[0:8, C_ONES:C_ONES + T], aT[0:8], 0.0)
        if SSD_CUT == 2:
            P.memset(YB[0][:], 0.0)
            return
        ytT = P.al(4, T)
        mark = P.aoff
        for s in range(nseq):
            P.aoff = mark
            c0 = s * Q
            sl = slice(c0, c0 + Q)
            if sample:
                hT = HT_S[0:64]
                P.dma(HNAT[0:64], st_ssd[l, s].rearrange("h p n -> p h n"))
                psx = P.ps()
                for g in range(2):
                    for r in range(4):
                        P.tr(psx[0:64, (g * 4 + r) * 64:(g * 4 + r + 1) * 64], HNAT[0:64, 4 * g + r, :], ident[0:64, 0:64])
                P.copy(HT_S[0:64].rearrange("p g c -> p (g c)"), psx[0:64, 0:512])
            else:
                hT = HT_P[0:64, l]
            psA = P.ps()
            for i in range(4):
                P.tr(psA[0:Q, i * 128:(i + 1) * 128], xc[:, i, sl], ident)
            psB = P.ps()
            for g in range(2):
                P.tr(psB[0:Q, g * 64:(g + 1) * 64], xc[0:64, 4 + g, sl], ident[0:64, 0:64])
            P.tr(psB[0:Q, 128:136], dtT[0:8, sl], ident[0:8, 0:8])
            P.tr(psB[0:Q, 136:144], acT[0:8, sl], ident[0:8, 0:8])
            btok = P.al(128)
            dtac = P.al(16)
            P.copy(btok[0:Q], psB[0:Q, 0:128], eng="act")
            P.copy(dtac[0:Q], psB[0:Q, 128:144], eng="act")
            dt_tok = dtac[0:Q, 0:8]
            ac_tok = dtac[0:Q, 8:16]
            xdt = P.al(8, 64)
            P.tt(xdt[0:Q], psA[0:Q, :].rearrange("p (h c) -> p h c", h=8), bc3(dt_tok, n_last=64), ALU.mult)
            if SSD_CUT == 3:
                break
            adiag = P.al(8, Q)
            P.tt(adiag[0:Q], bc3(ident[0:Q, 0:Q], n_mid=8), bc3(ac_tok, n_last=Q), ALU.mult)
            pump(3)
            nb = 2 if Q == 128 else 1
            hb = 8 // nb
            psR = [P.ps() for _ in range(nb)]
            for b_ in range(nb):
                P.mm(psR[b_][:, 0:hb * Q].rearrange("p (h t) -> p h t", h=hb), ones[0:Q, :],
                     adiag[0:Q, b_ * hb:(b_ + 1) * hb, :])
            alast = P.al(8)
            for b_ in range(nb):
                P.copy(alast[:, b_ * hb:(b_ + 1) * hb],
                       psR[b_][:, 0:hb * Q].rearrange("p (h t) -> p h t", h=hb)[:, :, Q - 1], eng="act")
            dec = P.al(8, Q)
            for b_ in range(nb):
                P.tt(dec[0:Q, b_ * hb:(b_ + 1) * hb, :],
                     psR[b_][0:Q, 0:hb * Q].rearrange("p (h t) -> p h t", h=hb),
                     bc3(ac_tok[:, b_ * hb:(b_ + 1) * hb], n_last=Q), ALU.subtract)
            P.tt(dec[0:Q], dec[0:Q], bc3(CT[0:Q, C_NEGU:C_NEGU + Q], n_mid=8), ALU.add)
            P.act(dec[0:Q], dec[0:Q], AF.Exp)
            if SSD_CUT == 4:
                break
            psC = P.ps()
            for g in range(2):
                P.mm(psC[0:Q, g * Q:(g + 1) * Q], xc[0:64, 4 + g, sl], xc[0:64, 6 + g, sl])
            MT = P.al(8, Q)
            for g in range(2):
                P.tt(MT[0:Q, 4 * g:4 * g + 4, :], dec[0:Q, 4 * g:4 * g + 4, :],
                     bc3(psC[0:Q, g * Q:(g + 1) * Q], n_mid=4), ALU.mult)
            pump(3)
            psY = P.ps()
            for h in range(8):
                P.mm(psY[0:Q, h * 64:(h + 1) * 64], MT[0:Q, h, :], xdt[0:Q, h, :])
            psS = P.ps()
            for g in range(2):
                P.mm(psS[0:Q, g * 256:(g + 1) * 256], xc[0:64, 6 + g, sl], hT[:, g, :])
            eac = P.al(8)
            P.act(eac[0:Q], ac_tok, AF.Exp)
            ytok = P.al(8, 64)
            P.tt(ytok[0:Q], psS[0:Q, :].rearrange("p (h c) -> p h c", h=8), bc3(eac[0:Q], n_last=64), ALU.mult)
            P.tt(ytok[0:Q], ytok[0:Q], psY[0:Q, :].rearrange("p (h c) -> p h c", h=8), ALU.add)
            if SSD_CUT == 6:
                break
            elast = P.al(8)
            P.act(elast, alast, AF.Exp)
            if SSD_CUT == 61:
                break
            toend = P.al(8)
            P.tt(toend[0:Q], alast[0:Q], ac_tok, ALU.subtract)
            P.act(toend[0:Q], toend[0:Q], AF.Exp)
            xe = P.al(8, 64)
            P.tt(xe[0:Q], xdt[0:Q], bc3(toend[0:Q], n_last=64), ALU.mult)
            if SSD_CUT == 62:
                break
            pump(3)
            psH = P.ps()
            for g in range(2):
                P.mm(psH[0:64, g * 256:(g + 1) * 256], btok[0:Q, g * 64:(g + 1) * 64],
                     xe[0:Q, 4 * g:4 * g + 4, :].rearrange("p h c -> p (h c)"))
            if SSD_CUT == 63:
                break
            hsc = P.al(2, 256)
            for g in range(2):
                for r in range(4):
                    P.act(hsc[0:64, g, r * 64:(r + 1) * 64], hT[:, g, r * 64:(r + 1) * 64], AF.Copy,
                          scale=elast[0:64, 4 * g + r:4 * g + r + 1])
            for g in range(2):
                P.tt(hT[:, g, :], hsc[0:64, g, :], psH[0:64, g * 256:(g + 1) * 256], ALU.add)
            pump(1)
            psT = P.ps()
            if SSD_CUT == 64:
                break
            yf = ytok[0:Q].rearrange("p h c -> p (h c)")
            for i in range(4):
                P.tr(psT[:, i * Q:(i + 1) * Q], yf[:, i * 128:(i + 1) * 128], ident[0:Q, 0:Q])
            P.copy(ytT[:, :, sl], psT[:, 0:4 * Q].rearrange("p (i q) -> p i q", i=4), eng="act")
            if SSD_CUT == 8:
                break
            if sample or last:
                pso = P.ps()
                for g in range(2):
                    for r in range(4):
                        P.tr(pso[0:64, (4 * g + r) * 64:(4 * g + r + 1) * 64], hT[:, g, r * 64:(r + 1) * 64],
                             ident[0:64, 0:64])
                hout = P.al(8, 64)
                P.copy(hout[0:64], pso[0:64, :].rearrange("p (h n) -> p h n", h=8))
                dst = o_ssd_s[l, s] if sample else o_ssd_p[l]
                P.dma(dst.rearrange("h p n -> p h n"), hout[0:64], q="act")
        P.aoff = mark
        if sample or last:
            cvi = P.al(8, nseq, 3)
            if sample:
                P.copy(cvi, xps4[:, :, :, Q:Q + 3])
            else:
                P.copy(cvi[:, :, 0, :], HIST_S[:, l, :, :])
            n3 = nseq * 3
            psv = [P.ps(), P.ps()]
            cols = [0, 128, 256, 384, 512, 576, 640, 704]
            for bi in range(8):
                rows = rows8[bi]
                pv = psv[0] if cols[bi] < 512 else psv[1]
                cc = cols[bi] % 512
                P.tr(pv[0:n3, cc:cc + rows], cvi[0:rows, bi].rearrange("p s j -> p (s j)"), ident[0:rows, 0:rows])
            cvo = P.al(768)
            P.copy(cvo[0:n3, 0:512], psv[0][0:n3, 0:512])
            P.copy(cvo[0:n3, 512:768], psv[1][0:n3, 0:256])
            dst = o_sconv_s[l] if sample else o_sconv_p[l]
            P.dma(dst, cvo[0:n3, :], q="act")
        yg = P.al(4, T)
        for i in range(4):
            P.stt(yg[:, i, :], xc[:, i, :], pc(l, PP_SSDD + i), ytT[:, i, :], ALU.mult, ALU.add)
        P.tt(yg, yg, PJ_Z[:, :, 0:T], ALU.mult)
        rmsnorm_fm(yg, 4, T, l, PP_SSDNW, YB[0][:, :, 0:T], 512.0)

    def s5_branch(l, T, nseq, Q, sample, last):
        wb = wnext()
        Bv = wb[:, 0:4096].rearrange("p (r c m) -> p r c m", r=2, c=4)
        stream(Bv[:, 0], bbd_d[l, 0])
        stream(Bv[:, 1], bbd_d[l, 1])
        if sample:
            nat = P.al(2048)
            for (src, dstt) in ((st_s5re, S5R_S), (st_s5im, S5I_S)):
                P.dma(nat[0:NSEQ_S], src[l])
                psx = P.ps()
                for st in range(16):
                    P.tr(psx[:, st * 16:(st + 1) * 16], nat[0:NSEQ_S, st * 128:(st + 1) * 128], ident[0:16, 0:16])
                P.copy(dstt[:].rearrange("p s b -> p (s b)"), psx[:, 0:256])
            subs = [(0, NSEQ_S, QS)]
            SR, SI = S5R_S[:], S5I_S[:]
        else:
            subs = [(0, 1, 64), (64, 1, 64)]
            SR, SI = S5R_P[:, l], S5I_P[:, l]
        HR = P.al(16, T)
        NHI = P.al(16, T)
        mark = P.aoff
        for (c0, ns, q) in subs:
            P.aoff = mark
            W = ns * q
            wre = P.al(16, W)
            wim = P.al(16, W)
            t1 = P.al(4, W)
            t2 = P.al(4, W)
            for ct in range(4):
                ps = P.ps()
                for sl_ in range(4):
                    P.mm(ps[:, sl_ * 128:sl_ * 128 + W], Bv[:, 0, ct, sl_ * 128:(sl_ + 1) * 128], PJ_U[:, ct, c0:c0 + W])
                    P.mm(ps[:, sl_ * 128 + 64:sl_ * 128 + 64 + W], Bv[:, 1, ct, sl_ * 128:(sl_ + 1) * 128],
                         PJ_U[:, ct, c0:c0 + W])
                p4 = ps[:, :].rearrange("p (s r w) -> p s r w", s=4, r=2)
                pr = p4[:, :, 0, 0:W]
                pi = p4[:, :, 1, 0:W]
                stv = slice(ct * 4, ct * 4 + 4)
                if ns == 1:
                    e2r = E2R[:, l, stv, 0:q]
                    e2i = E2I[:, l, stv, 0:q]
                    v = lambda a: a
                else:
                    e2r = E2R[:, l, stv, 0:q].unsqueeze(2).to_broadcast([128, 4, ns, q])
                    e2i = E2I[:, l, stv, 0:q].unsqueeze(2).to_broadcast([128, 4, ns, q])
                    v = lambda a: a.rearrange("p s (b q) -> p s b q", b=ns)
                P.tt(v(t1), v(pr), e2r, ALU.mult)
                P.tt(v(t2), v(pi), e2i, ALU.mult)
                P.tt(wre[:, stv, :], t1, t2, ALU.subtract)
                P.tt(v(t1), v(pi), e2r, ALU.mult)
                P.tt(v(t2), v(pr), e2i, ALU.mult)
                P.tt(wim[:, stv, :], t1, t2, ALU.add)
            gre = P.al(16, W)
            gim = P.al(16, W)
            lastsub = (c0, ns, q) == subs[-1]
            if lastsub and not sample:
                pump_merge(0, 2)
            if ns == 1:
                for st in range(16):
                    rb = RHO[:, l, st:st + 1].to_broadcast([128, W])
                    P.scan(gre[:, st, :], rb, wre[:, st, :], SR[:, st:st + 1])
                    P.scan(gim[:, st, :], rb, wim[:, st, :], SI[:, st:st + 1])
            else:
                tmp = P.al(16, ns)
                w4r = wre.rearrange("p s (b q) -> p s b q", b=ns)
                w4i = wim.rearrange("p s (b q) -> p s b q", b=ns)
                P.tt(tmp, SR[:], bc3(RHO[:, l, :], n_last=ns), ALU.mult)
                P.tt(w4r[:, :, :, 0], w4r[:, :, :, 0], tmp, ALU.add)
                P.tt(tmp, SI[:], bc3(RHO[:, l, :], n_last=ns), ALU.mult)
                P.tt(w4i[:, :, :, 0], w4i[:, :, :, 0], tmp, ALU.add)
                rm3 = nat[:, 0:1024].rearrange("p (s w) -> p s w", s=16)
                P.tt(rm3, RHO[:, l, :].unsqueeze(2).to_broadcast([128, 16, 64]),
                     CT[:, C_MASK4:C_MASK4 + 64].unsqueeze(1).to_broadcast([128, 16, 64]), ALU.mult)
                rm = nat[:, 0:1024]
                P.scan(gre.rearrange("p s w -> p (s w)"), rm, wre.rearrange("p s w -> p (s w)"), 0.0)
                P.scan(gim.rearrange("p s w -> p (s w)"), rm, wim.rearrange("p s w -> p (s w)"), 0.0)
            if lastsub and not sample:
                pump_merge(0, 2)
            if lastsub:
                wc = wnext()
                Cv = wc[:, 0:4096].rearrange("p (r s m) -> p r s m", r=2, s=16)
                stream(Cv[:, 0], cbd_d[l, 0])
                stream(Cv[:, 1], cbd_d[l, 1])
                wg = wnext()
                Gv = wg[:, 0:2048].rearrange("p (c m) -> p c m", c=4)
                stream(Gv, w_glu[l].rearrange("(c p) m -> p c m", p=128))
            if ns == 1:
                cs = COS[:, l, :, 0:q]
                sn = SIN[:, l, :, 0:q]
                v = lambda a: a
            else:
                cs = COS[:, l, :, 0:q].unsqueeze(2).to_broadcast([128, 16, ns, q])
                sn = SIN[:, l, :, 0:q].unsqueeze(2).to_broadcast([128, 16, ns, q])
                v = lambda a: a.rearrange("p s (b q) -> p s b q", b=ns)
            t1 = wre
            t2 = wim
            P.tt(v(t1), v(gre), cs, ALU.mult)
            P.tt(v(t2), v(gim), sn, ALU.mult)
            P.tt(HR[:, :, c0:c0 + W], t1, t2, ALU.subtract)
            P.tt(v(t1), v(gre), sn, ALU.mult)
            P.tt(v(t2), v(gim), cs, ALU.mult)
            P.stt(NHI[:, :, c0:c0 + W], t1, -1.0, t2, ALU.mult, ALU.subtract)
            if ns == 1:
                P.copy(SR, HR[:, :, c0 + W - 1], eng="act")
                P.ts(SI, NHI[:, :, c0 + W - 1], -1.0, ALU.mult)
            else:
                h4 = HR[:, :, c0:c0 + W].rearrange("p s (b q) -> p s b q", b=ns)
                n4 = NHI[:, :, c0:c0 + W].rearrange("p s (b q) -> p s b q", b=ns)
                P.copy(SR[:], h4[:, :, :, q - 1], eng="act")
                P.ts(SI[:], n4[:, :, :, q - 1], -1.0, ALU.mult)
        P.aoff = mark
        if sample or last:
            for (srct, dstd_s, dstd_p) in ((SR, o_s5re_s, o_s5re_p), (SI, o_s5im_s, o_s5im_p)):
                if sample:
                    so = P.al(2048)
                    for b_ in range(4):
                        pq = P.ps()
                        for st in range(4 * b_, 4 * b_ + 4):
                            P.tr(pq[0:NSEQ_S, (st % 4) * 128:(st % 4 + 1) * 128], srct[:, st, :], ident)
                        P.copy(so[0:NSEQ_S, b_ * 512:(b_ + 1) * 512], pq[0:NSEQ_S, 0:512])
                    P.dma(dstd_s[l], so[0:NSEQ_S, :], q="act")
                else:
                    pso = P.ps()
                    P.tr(pso[0:16, 0:128], srct, ident)
                    so = P.al(128)
                    P.copy(so[0:16], pso[0:16, 0:128])
                    P.dma(dstd_p[l], so[0:16], q="act")
        psY = P.ps()
        for ct in range(4):
            k = 0
            for sl_ in range(4):
                st = ct * 4 + sl_
                P.mm(psY[:, ct * T:(ct + 1) * T], Cv[:, 0, st, :], HR[:, st, :], start=(k == 0), stop=False)
                k += 1
                P.mm(psY[:, ct * T:(ct + 1) * T], Cv[:, 1, st, :], NHI[:, st, :], start=False, stop=(sl_ == 3))
        yb = P.al(4, T)
        for ct in range(4):
            P.stt(yb[:, ct, :], PJ_U[:, ct, 0:T], pc(l, PP_S5D + ct), psY[:, ct * T:(ct + 1) * T], ALU.mult, ALU.add)
        gg = P.al(4, T)
        P.act(gg, yb, AF.Gelu_apprx_tanh)
        psG = P.ps()
        for co in range(4):
            for ct in range(4):
                P.mm(psG[:, co * T:(co + 1) * T], Gv[:, ct, co * 128:(co + 1) * 128], gg[:, ct, :],
                     start=(ct == 0), stop=(ct == 3))
        sg = P.al(4, T)
        P.act(sg, psG[:, 0:4 * T].rearrange("p (c t) -> p c t", c=4), AF.Sigmoid)
        P.tt(sg, sg, gg, ALU.mult)
        P.tt(YB[1][:, :, 0:T], sg, PJ_GATE[:, :, 0:T], ALU.mult)

    def ml_branch(l, T, nseq, Q, sample, last):
        wb = wnext()
        Wq = wb[:, 0:512].rearrange("p (h k) -> p h k", h=4)
        Wk = wb[:, 512:1024].rearrange("p (h k) -> p h k", h=4)
        Wv = wb[:, 1024:1536].rearrange("p (h k) -> p h k", h=4)
        stream(Wq, w_mq[l].rearrange("h c k -> c h k"))
        stream(Wk, w_mk[l].rearrange("h c k -> c h k"))
        stream(Wv, w_mv[l].rearrange("h c k -> c h k"))
        xpm4 = XPM[:, :, 0:nseq * (Q + 3)].rearrange("p b (s q) -> p b s q", s=nseq)
        acc = P.al(4, nseq, Q)
        conv_fm(xpm4, [128] * 4, l, PP_MCONV, acc, Q)
        xcm = P.al(4, T)
        P.act(xcm, acc.rearrange("p b s q -> p b (s q)"), AF.Silu)
        if not sample:
            P.copy(HIST_M[:, l, :, :], xpm4[:, :, 0, Q:Q + 3], eng="act")
        e_ = P.al(T)
        P.act(e_[0:4], PJ_CF[0:4, 0:T], AF.Exp, bias=NFB[0:4, l:l + 1], scale=-1.0)
        sp = P.al(T)
        P.act(sp[0:4], e_[0:4], AF.Ln, bias=EPSB[0:4, 1:2])
        lf = P.al(T)
        P.ts(lf[0:4], sp[0:4], -1.0, ALU.mult)
        bT = P.al(T)
        if sample:
            P.scan(bT[0:4], CT[0:4, C_MASK4:C_MASK4 + T], lf[0:4], 0.0)
        else:
            P.scan(bT[0:4], CT[0:4, C_ONES:C_ONES + T], lf[0:4], 0.0)
        rT = P.al(T)
        P.tt(rT[0:4], PJ_CI[0:4, 0:T], bT[0:4], ALU.subtract)
        hnT = P.al(4, T)
        mark = P.aoff
        sel = CT[:, C_SEL4:C_SEL4 + 128] if sample else CT[:, C_SEL128:C_SEL128 + 128]
        for s in range(nseq):
            P.aoff = mark
            c0 = s * Q
            sl = slice(c0, c0 + Q)
            if sample:
                Cst, Nst, Mst = MC_S[:], MN_S[:], MM_S[:]
                P.dma(MC_S[:], st_mC[l, s].rearrange("h k v -> k h v"))
                nat = P.al(128)
                P.dma(nat[0:4], st_mn[l, s])
                psx = P.ps()
                P.tr(psx[:, 0:4], nat[0:4, :], ident[0:4, 0:4])
                P.copy(MN_S[:, :, 0], psx[:, 0:4])
                P.copy(MN_S[:, :, 1], psx[:, 0:4])
                P.dma(MM_S[:], st_mm[l, s:s + 1, :].partition_broadcast(128))
            else:
                Cst, Nst, Mst = MC_P[:, l], MN_P[:, l], MM_P[:, l]
            tok = P.al(12)
            psg = P.ps()
            P.tr(psg[0:Q, 0:4], bT[0:4, sl], ident[0:4, 0:4])
            P.tr(psg[0:Q, 4:8], rT[0:4, sl], ident[0:4, 0:4])
            P.copy(tok[0:Q, 0:8], psg[0:Q, 0:8], eng="act")
            b_tok = tok[0:Q, 0:4]
            r_tok = tok[0:Q, 4:8]
            m_t = tok[0:Q, 8:12]
            psq = P.ps()
            psk = P.ps()
            for h in range(4):
                P.mm(psq[:, h * Q:(h + 1) * Q], Wq[:, h, :], xcm[:, h, sl])
                P.mm(psk[:, h * Q:(h + 1) * Q], Wk[:, h, :], xcm[:, h, sl])
            qT = P.al(4, Q)
            kT = P.al(4, Q)
            P.copy(qT, psq[:, 0:4 * Q].rearrange("p (h q) -> p h q", h=4), eng="act")
            P.act(kT, psk[:, 0:4 * Q].rearrange("p (h q) -> p h q", h=4), AF.Copy, scale=128.0 ** -0.5)
            pskt = P.ps()
            psvt = P.ps()
            for h in range(4):
                P.mm(pskt[0:Q, h * 128:(h + 1) * 128], xcm[:, h, sl], Wk[:, h, :])
                P.mm(psvt[0:Q, h * 128:(h + 1) * 128], xpm4[:, h, s, 3:3 + Q], Wv[:, h, :])
            vtok = P.al(4, 128)
            P.copy(vtok[0:Q], psvt[0:Q, :].rearrange("p (h v) -> p h v", h=4), eng="act")
            ktok = P.al(4, 128)
            P.copy(ktok[0:Q], pskt[0:Q, :].rearrange("p (h v) -> p h v", h=4), eng="act")
            rdiag = P.al(4, Q)
            P.tt(rdiag[0:Q], bc3(ident[0:Q, 0:Q], n_mid=4), bc3(r_tok, n_last=Q), ALU.mult)
            psR = P.ps()
            P.mm(psR[:, 0:4 * Q].rearrange("p (h t) -> p h t", h=4), ones[0:Q, :], rdiag[0:Q])
            dl = P.al(4, Q)
            P.tt(dl[0:Q], psR[0:Q, 0:4 * Q].rearrange("p (h t) -> p h t", h=4), bc3(b_tok, n_last=Q), ALU.add)
            P.tt(dl[0:Q], dl[0:Q], bc3(CT[0:Q, C_NEGL:C_NEGL + Q], n_mid=4), ALU.add)
            mx = P.al(4)
            P.red(mx[0:Q], dl[0:Q], ALU.max)
            bm = P.al(4)
            P.tt(bm[0:Q], b_tok, Mst[0:Q], ALU.add)
            P.tt(m_t, bm[0:Q], mx[0:Q], ALU.max)
            negm = P.al(4)
            P.ts(negm[0:Q], m_t, -1.0, ALU.mult)
            Wt = P.al(4, Q)
            for h in range(4):
                P.act(Wt[0:Q, h, :], dl[0:Q, h, :], AF.Exp, bias=negm[0:Q, h:h + 1])
            if not sample:
                pump_merge(1)
            psqk = P.ps()
            for h in range(4):
                P.mm(psqk[0:Q, h * Q:(h + 1) * Q], qT[:, h, :], kT[:, h, :])
            S = P.al(4, Q)
            P.tt(S[0:Q], Wt[0:Q], psqk[0:Q, 0:4 * Q].rearrange("p (h t) -> p h t", h=4), ALU.mult)
            dotin = P.al(4)
            P.red(dotin[0:Q], S[0:Q], ALU.add)
            psst = P.ps()
            for h in range(4):
                P.tr(psst[0:Q, h * Q:(h + 1) * Q], S[0:Q, h, :], ident[0:Q, 0:Q])
            ST = P.al(4, Q)
            P.copy(ST[0:Q], psst[0:Q, 0:4 * Q].rearrange("p (h t) -> p h t", h=4), eng="act")
            psn = P.ps()
            psi = P.ps()
            psqn = P.ps()
            for h in range(4):
                P.mm(psn[0:Q, h * 128:(h + 1) * 128], ST[0:Q, h, :], vtok[0:Q, h, :])
                P.mm(psi[0:Q, h * 128:(h + 1) * 128], qT[:, h, :], Cst[:, h, :])
                P.mm(psqn[0:Q, 2 * h:2 * h + 2], qT[:, h, :], Nst[:, h, :])
            scl = P.al(4)
            P.tt(scl[0:Q], bm[0:Q], m_t, ALU.subtract)
            P.act(scl[0:Q], scl[0:Q], AF.Exp)
            num = P.al(4, 128)
            P.tt(num[0:Q], psi[0:Q, :].rearrange("p (h v) -> p h v", h=4), bc3(scl[0:Q], n_last=128), ALU.mult)
            P.tt(num[0:Q], num[0:Q], psn[0:Q, :].rearrange("p (h v) -> p h v", h=4), ALU.add)
            dot = P.al(4)
            P.tt(dot[0:Q], psqn[0:Q, 0:8].rearrange("p (h t) -> p h t", h=4)[:, :, 0], scl[0:Q], ALU.mult)
            P.tt(dot[0:Q], dot[0:Q], dotin[0:Q], ALU.add)
            emn = P.al(4)
            emn2 = P.al(4)
            P.act(emn[0:Q], negm[0:Q], AF.Exp)
            P.ts(emn2[0:Q], dot[0:Q], -1.0, ALU.mult)
            P.tt(dot[0:Q], dot[0:Q], emn2[0:Q], ALU.max)
            P.tt(dot[0:Q], dot[0:Q], emn[0:Q], ALU.max)
            P.recip(dot[0:Q], dot[0:Q])
            P.tt(num[0:Q], num[0:Q], bc3(dot[0:Q], n_last=128), ALU.mult)
            if not sample:
                pump_merge(1)
            pso = P.ps()
            for h in range(4):
                P.tr(pso[0:Q, h * 128:(h + 1) * 128], PJ_CO[:, h, sl], ident)
            P.tt(num[0:Q], num[0:Q], pso[0:Q, :].rearrange("p (h v) -> p h v", h=4), ALU.mult)
            mean = P.al(4)
            P.red(mean[0:Q], num[0:Q], ALU.add)
            P.ts(mean[0:Q], mean[0:Q], 1.0 / 128.0, ALU.mult)
            P.tt(num[0:Q], num[0:Q], bc3(mean[0:Q], n_last=128), ALU.subtract)
            sq = P.al(4, 128)
            P.tt(sq[0:Q], num[0:Q], num[0:Q], ALU.mult)
            var = P.al(4)
            P.red(var[0:Q], sq[0:Q], ALU.add)
            P.act(var[0:Q], var[0:Q], AF.Sqrt, bias=EPSB[0:Q, 0:1], scale=1.0 / 128.0)
            P.recip(var[0:Q], var[0:Q])
            P.tt(num[0:Q], num[0:Q], bc3(var[0:Q], n_last=128), ALU.mult)
            if not sample:
                pump_merge(1)
            pst = P.ps()
            for h in range(4):
                P.tr(pst[:, h * Q:(h + 1) * Q], num[0:Q, h, :], ident[0:Q, 0:Q])
            P.copy(hnT[:, :, sl], pst[:, 0:4 * Q].rearrange("p (h q) -> p h q", h=4), eng="act")
            psl = P.ps()
            P.mm(psl[:, 0:12], sel[0:Q, :], tok[0:Q, 0:12])
            L12 = P.al(12)
            P.copy(L12, psl[:, 0:12])
            lm = P.al(4)
            P.tt(lm, L12[:, 0:4], L12[:, 8:12], ALU.subtract)
            wend = P.al(4)
            P.tt(wend[0:Q], r_tok, lm[0:Q], ALU.add)
            P.act(wend[0:Q], wend[0:Q], AF.Exp)
            P.ts(wend[0:Q], wend[0:Q], 128.0 ** -0.5, ALU.mult)
            carry = P.al(4)
            P.tt(carry, lm, Mst, ALU.add)
            P.act(carry, carry, AF.Exp)
            kw = P.al(4, 128)
            P.tt(kw[0:Q], ktok[0:Q], bc3(wend[0:Q], n_last=128), ALU.mult)
            if not sample:
                pump_merge(1)
            psc = P.ps()
            psdn = P.ps()
            for h in range(4):
                P.mm(psc[:, h * 128:(h + 1) * 128], kw[0:Q, h, :], vtok[0:Q, h, :])
                P.mm(psdn[:, 2 * h:2 * h + 2], kw[0:Q, h, :], ones[0:Q, 0:2])
            csc = P.al(4, 128)
            nsc = P.al(4, 2)
            for h in range(4):
                P.act(csc[:, h, :], Cst[:, h, :], AF.Copy, scale=carry[:, h:h + 1])
                P.act(nsc[:, h, :], Nst[:, h, :], AF.Copy, scale=carry[:, h:h + 1])
            P.tt(Cst, csc, psc[:, :].rearrange("p (h v) -> p h v", h=4), ALU.add)
            P.tt(Nst, nsc, psdn[:, 0:8].rearrange("p (h t) -> p h t", h=4), ALU.add)
            P.copy(Mst, L12[:, 8:12], eng="act")
            if sample or last:
                P.dma((o_mC_s[l, s] if sample else o_mC_p[l]).rearrange("h k v -> k h v"), Cst, q="act")
                psx = P.ps()
                ncp = P.al(4)
                P.copy(ncp, Nst[:, :, 0], eng="act")
                P.tr(psx[0:4, 0:128], ncp, ident)
                no = P.al(128)
                P.copy(no[0:4], psx[0:4, 0:128])
                P.dma(o_mn_s[l, s] if sample else o_mn_p[l], no[0:4], q="act")
                mo = P.al(4)
                P.copy(mo[0:1], Mst[0:1])
                P.dma(o_mm_s[l, s:s + 1, :] if sample else o_mm_p[l], mo[0:1], q="act")
        P.aoff = mark
        if sample or last:
            cvi = P.al(4, nseq, 3)
            if sample:
                P.copy(cvi, xpm4[:, :, :, Q:Q + 3])
            else:
                P.copy(cvi[:, :, 0, :], HIST_M[:, l, :, :])
            n3 = nseq * 3
            psv = P.ps()
            for bi in range(4):
                P.tr(psv[0:n3, bi * 128:(bi + 1) * 128], cvi[:, bi].rearrange("p s j -> p (s j)"), ident)
            cvo = P.al(512)
            P.copy(cvo[0:n3], psv[0:n3, 0:512])
            P.dma(o_mconv_s[l] if sample else o_mconv_p[l], cvo[0:n3], q="act")
        yy = P.al(4, T)
        for h in range(4):
            P.ts(yy[:, h, :], hnT[:, h, :], pc(l, PP_MNW + h), ALU.mult)
            P.stt(yy[:, h, :], xcm[:, h, :], pc(l, PP_MSKIP + h), yy[:, h, :], ALU.mult, ALU.add)
        P.tt(YB[2][:, :, 0:T], yy, PJ_CZ[:, :, 0:T], ALU.mult)

    NFB = P.sb("nfb", [128, L])
    for l in range(L):
        P.ts(NFB[0:4, l:l + 1], pc(l, PP_MFB, 1, 4), -1.0, ALU.mult)

    def hg_branch(l, T, nseq, Q, sample, last):
        q = QS if sample else 64
        nsub = T // q
        fg = P.al(4, T)
        for ct in range(4):
            P.ts(fg[:, ct, :], PJ_DF[:, ct, 0:T], OML[:, l, ct:ct + 1], ALU.mult, LB[:, l, ct:ct + 1], ALU.add)
        lf = P.al(4, T)
        P.act(lf, fg, AF.Ln)
        kT = P.al(4, T)
        P.ts(kT, fg, -1.0, ALU.mult, 1.0, ALU.add)
        qT = P.al(4, T)
        P.ts(qT, PJ_DQ[:, :, 0:T], 128.0 ** -0.5, ALU.mult)
        gT = P.al(4, T)
        mk = CT[:, C_MASKH_S:C_MASKH_S + 4 * T] if sample else CT[:, C_MASKH_P:C_MASKH_P + 4 * T]
        P.scan(gT.rearrange("p c t -> p (c t)"), mk, lf.rearrange("p c t -> p (c t)"), 0.0)
        eg = P.al(4, T)
        P.act(eg, gT, AF.Exp)
        qe = P.al(4, T)
        P.tt(qe, qT, eg, ALU.mult)
        onT = P.al(4, T)
        mark = P.aoff
        for s in range(nsub):
            P.aoff = mark
            c0 = s * q
            sl = slice(c0, c0 + q)
            if sample:
                Sst = HG_S[:]
                P.dma(HG_S[:], st_hg[l, s].rearrange("h k v -> k h v"))
            else:
                Sst = HG_P[:, l]
            gref = gT[:, :, c0 + q // 2]
            glast = gT[:, :, c0 + q - 1]
            Dx = P.al(3, 4, q)
            P.tt(Dx[:, 0], gT[:, :, sl], bc3(gref, n_last=q), ALU.subtract)
            P.ts(Dx[:, 1], Dx[:, 0], -1.0, ALU.mult)
            P.tt(Dx[:, 2], bc3(glast, n_last=q), gT[:, :, sl], ALU.subtract)
            P.act(Dx, Dx, AF.Exp)
            qg = P.al(4, q)
            kg = P.al(4, q)
            kd = P.al(4, q)
            P.tt(qg, qT[:, :, sl], Dx[:, 0], ALU.mult)
            P.tt(kg, kT[:, :, sl], Dx[:, 1], ALU.mult)
            P.tt(kd, kT[:, :, sl], Dx[:, 2], ALU.mult)
            pump_merge(2)
            psa = P.ps()
            for h in range(4):
                P.mm(psa[0:q, h * q:(h + 1) * q], kg[:, h, :], qg[:, h, :])
            attT = P.al(4, q)
            P.tt(attT[0:q], psa[0:q, 0:4 * q].rearrange("p (h t) -> p h t", h=4),
                 bc3(CT[0:q, C_U01:C_U01 + q], n_mid=4), ALU.mult)
            psv = P.ps()
            psk = P.ps()
            for h in range(4):
                P.tr(psv[0:q, h * 128:(h + 1) * 128], PJ_DI[:, h, sl], ident)
                P.tr(psk[0:q, h * 128:(h + 1) * 128], kd[:, h, :], ident)
            vtok = P.al(4, 128)
            kdtok = P.al(4, 128)
            P.copy(vtok[0:q], psv[0:q, :].rearrange("p (h v) -> p h v", h=4), eng="act")
            P.copy(kdtok[0:q], psk[0:q, :].rearrange("p (h v) -> p h v", h=4), eng="act")
            pso = P.ps()
            for h in range(4):
                P.mm(pso[0:q, h * 128:(h + 1) * 128], attT[0:q, h, :], vtok[0:q, h, :], start=True, stop=False)
                P.mm(pso[0:q, h * 128:(h + 1) * 128], qe[:, h, sl], Sst[:, h, :], start=False, stop=True)
            o = P.al(4, 128)
            P.copy(o[0:q], pso[0:q, :].rearrange("p (h v) -> p h v", h=4), eng="act")
            sq = P.al(4, 128)
            P.tt(sq[0:q], o[0:q], o[0:q], ALU.mult)
            ssq = P.al(4)
            P.red(ssq[0:q], sq[0:q], ALU.add)
            P.act(ssq[0:q], ssq[0:q], AF.Sqrt, bias=EPSB[0:q, 0:1], scale=1.0 / 128.0)
            P.recip(ssq[0:q], ssq[0:q])
            P.tt(o[0:q], o[0:q], bc3(ssq[0:q], n_last=128), ALU.mult)
            pump_merge(2)
            pst = P.ps()
            for h in range(4):
                P.tr(pst[:, h * q:(h + 1) * q], o[0:q, h, :], ident[0:q, 0:q])
            P.copy(onT[:, :, sl], pst[:, 0:4 * q].rearrange("p (h t) -> p h t", h=4), eng="act")
            egl = P.al(4)
            P.act(egl, glast, AF.Exp)
            pss = P.ps()
            for h in range(4):
                P.mm(pss[:, h * 128:(h + 1) * 128], kdtok[0:q, h, :], vtok[0:q, h, :])
            ssc = P.al(4, 128)
            for h in range(4):
                P.act(ssc[:, h, :], Sst[:, h, :], AF.Copy, scale=egl[:, h:h + 1])
            P.tt(Sst, ssc, pss[:, :].rearrange("p (h v) -> p h v", h=4), ALU.add)
            if sample or (last and s == nsub - 1):
                P.dma((o_hg_s[l, s] if sample else o_hg_p[l]).rearrange("h k v -> k h v"), Sst, q="act")
        P.aoff = mark
        yy = P.al(4, T)
        for ct in range(4):
            P.ts(yy[:, ct, :], onT[:, ct, :], pc(l, PP_HNW + ct), ALU.mult)
        P.tt(YB[3][:, :, 0:T], yy, PJ_DG[:, :, 0:T], ALU.mult)

    def norm_in(l, T):
        if NORM_CUT == 0:
            return
        sq = P.al(D)
        P.act(sq[0:T], XTK[0:T, :], AF.Square)
        ss = P.al(1)
        P.red(ss[0:T], sq[0:T], ALU.add)
        P.act(ss[0:T], ss[0:T], AF.Sqrt, bias=EPSB[0:T, 0:1], scale=1.0 / D)
        P.recip(ss[0:T], ss[0:T])
        if NORM_CUT == 1:
            return
        P.ts(sq[0:T], XTK[0:T, :], ss[0:T, 0:1], ALU.mult)
        if NORM_CUT == 2:
            return
        for half in range(2):
            ps = P.ps()
            for j in range(4):
                kt = half * 4 + j
                P.tr(ps[:, j * T:(j + 1) * T], sq[0:T, kt * 128:(kt + 1) * 128], ident[0:T, 0:T])
            if NORM_CUT == 3:
                continue
            for j in range(4):
                kt = half * 4 + j
                if NORM_CUT == 15:
                    P.copy(XN[:, kt, 0:T], ps[:, j * T:(j + 1) * T])
                elif NORM_CUT == 16:
                    P.ts(XN[:, kt, 0:T], ps[:, j * T:(j + 1) * T], pc(l, PP_NORMW + kt), ALU.mult)
                elif NORM_CUT == 13:
                    P.copy(XN[:, kt, 0:T], ps[:, j * T:(j + 1) * T], eng=("act" if j % 2 == 0 else "dve"))
                elif NORM_CUT == 14:
                    P.act(XN[:, kt, 0:T], ps[:, j * T:(j + 1) * T], AF.Copy, scale=EPSB[:, 1:2])
                elif (NORM_CUT == 11) or (NORM_CUT not in (12,) and j % 2 == 0):
                    P.act(XN[:, kt, 0:T], ps[:, j * T:(j + 1) * T], AF.Copy, scale=pc(l, PP_NORMW + kt))
                else:
                    P.ts(XN[:, kt, 0:T], ps[:, j * T:(j + 1) * T], pc(l, PP_NORMW + kt), ALU.mult)

    mctx = {"gen": None, "todo": [], "l": 0, "T": 128}

    def merge_branch_steps(l, T, i):
        w_l = w_in[l].rearrange("(kt p) c -> p kt c", p=128)
        gt = []
        for half in range(2):
            wb = wnext()
            wv = wb[:, 0:4096].rearrange("p (k c) -> p k c", k=8)
            c0 = O_MERGE + i * 1024 + half * 512
            stream(wv, w_l[:, :, c0:c0 + 512])
            ps = P.ps()
            for kt in range(8):
                P.mm(ps[0:T, :], XN[:, kt, 0:T], wv[:, kt, :], start=(kt == 0), stop=(kt == 7))
            g = PTK[half]
            P.act(g[0:T, 0:512], ps[0:T, :], AF.Sigmoid)
            gt.append(g)
            yield
        wb = wnext()
        wv = wb[:, 0:4096].rearrange("p (c d) -> p c d", c=4)
        stream(wv, w_br[l, i].rearrange("(c p) d -> p c d", p=128))
        for half in range(2):
            hs = slice(half * 512, (half + 1) * 512)
            ps = P.ps()
            for ct in range(4):
                P.mm(ps[0:T, :], YB[i][:, ct, 0:T], wv[:, ct, hs], start=(ct == 0), stop=(ct == 3))
            g = gt[half]
            if i == 0:
                P.tt(MRGK[0:T, hs], ps[0:T, :], g[0:T, 0:512], ALU.mult)
            else:
                P.tt(g[0:T, 0:512], ps[0:T, :], g[0:T, 0:512], ALU.mult)
                P.tt(MRGK[0:T, hs], MRGK[0:T, hs], g[0:T, 0:512], ALU.add)
            if half == 0:
                yield

    def pump_merge(max_i, nsteps=1):
        for _ in range(nsteps):
            if mctx["gen"] is None:
                if not (mctx["todo"] and mctx["todo"][0] <= max_i):
                    return
                mctx["gen"] = merge_branch_steps(mctx["l"], mctx["T"], mctx["todo"].pop(0))
            try:
                next(mctx["gen"])
            except StopIteration:
                mctx["gen"] = None

    def merge_out(l, T):
        while mctx["gen"] is not None or mctx["todo"]:
            pump_merge(99)
        mt8 = P.al(8, T)
        for half in range(2):
            ps = P.ps()
            for j in range(4):
                kt = half * 4 + j
                P.tr(ps[:, j * T:(j + 1) * T], MRGK[0:T, kt * 128:(kt + 1) * 128], ident[0:T, 0:T])
            P.copy(mt8[:, half * 4:half * 4 + 4, :], ps[:, 0:4 * T].rearrange("p (j t) -> p j t", j=4),
                   eng=("act" if half == 0 else "dve"))
        for half in range(2):
            hs = slice(half * 512, (half + 1) * 512)
            wb = wnext()
            wv = wb[:, 0:4096].rearrange("p (k c) -> p k c", k=8)
            stream(wv, w_out[l].rearrange("(kt p) d -> p kt d", p=128)[:, :, hs])
            ps = P.ps()
            for kt in range(8):
                P.mm(ps[0:T, :], mt8[:, kt, :], wv[:, kt, :], start=(kt == 0), stop=(kt == 7))
            P.tt(XTK[0:T, hs], XTK[0:T, hs], ps[0:T, :], ALU.add)

    tiles = [("p", i) for i in range(n_ptiles)]
    if with_sample:
        tiles.append(("s", 0))
    for (kind, ti) in tiles:
        sample = kind == "s"
        T = NSEQ_S * QS if sample else 128
        nseq = NSEQ_S if sample else 1
        Q = QS if sample else 128
        first = (ti == 0)
        last = (ti == n_ptiles - 1)
        P.aoff = 0
        src = xs_d[:, :] if sample else xp_d[ti * 128:(ti + 1) * 128, :]
        P.dma(XTK[0:T, :], src)
        for l in range(L):
            P.aoff = 0
            xps4 = XPS[:, :, 0:nseq * (Q + 3)].rearrange("p b (s q) -> p b s q", s=nseq)
            xpm4 = XPM[:, :, 0:nseq * (Q + 3)].rearrange("p b (s q) -> p b s q", s=nseq)
            if sample:
                for (src_d, xp4, nb_, cols) in ((st_sconv, xps4, 8, [0, 128, 256, 384, 512, 576, 640, 704]),
                                                (st_mconv, xpm4, 4, [0, 128, 256, 384])):
                    nch = 768 if nb_ == 8 else 512
                    nat = P.al(nch)
                    P.dma(nat[0:48], src_d[l])
                    for bi in range(nb_):
                        rows = 128 if (nb_ == 4 or bi < 4) else 64
                        psx = P.ps()
                        P.tr(psx[0:rows, 0:48], nat[0:48, cols[bi]:cols[bi] + rows], ident[0:48, 0:48])
                        P.copy(xp4[0:rows, bi, :, 0:3], psx[0:rows, 0:48].rearrange("p (s j) -> p s j", s=NSEQ_S))
            else:
                P.copy(xps4[:, :, 0, 0:3], HIST_S[:, l, :, :], eng="act")
                P.copy(xpm4[:, :, 0, 0:3], HIST_M[:, l, :, :], eng="act")
            P.aoff = 0
            norm_in(l, T)
            mctx["todo"] = [0, 1, 2, 3] if "merge" in stages else []
            mctx["gen"] = None
            mctx["l"] = l
            mctx["T"] = T
            P.cur_tag = "%s%d_proj" % (kind, ti)
            pending[:] = proj_blocks(l, T, nseq, Q) if "proj" in stages else []
            for bi_, (nm_, fn_) in enumerate((("ssd", lambda: ssd_branch(l, T, nseq, Q, sample, first, last, None)),
                                              ("s5", lambda: s5_branch(l, T, nseq, Q, sample, last)),
                                              ("ml", lambda: ml_branch(l, T, nseq, Q, sample, last)),
                                              ("hg", lambda: hg_branch(l, T, nseq, Q, sample, last)))):
                P.aoff = 0
                flush_upto(bi_)
                P.cur_tag = "%s%d_%s" % (kind, ti, nm_)
                if nm_ in stages:
                    fn_()
                else:
                    P.memset(YB[bi_][:], 0.0)
            P.aoff = 0
            P.cur_tag = "%s%d_merge" % (kind, ti)
            flush_upto(99)
            if "merge" in stages:
                merge_out(l, T)
        P.aoff = 0
        sq = P.al(D)
        P.act(sq[0:T], XTK[0:T, :], AF.Square)
        ss = P.al(1)
        P.red(ss[0:T], sq[0:T], ALU.add)
        P.act(ss[0:T], ss[0:T], AF.Sqrt, bias=EPSB[0:T, 0:1], scale=1.0 / D)
        P.recip(ss[0:T], ss[0:T])
        xo = P.al(D)
        P.stt(xo[0:T], XTK[0:T, :], ss[0:T, 0:1], FNW[0:T, :], ALU.mult, ALU.mult)
        dst = y_s[:, :] if sample else y_p[ti * 128:(ti + 1) * 128, :]
        P.dma(dst, xo[0:T], q="act")

    P.emit()
    return nc, P


_CACHE = {}


def _prep_inputs(inp, n_ptiles=16):
    f = lambda a: np.ascontiguousarray(np.asarray(a, np.float32))
    consts = _host_consts()
    pp = _host_pp(inp)
    bbd, cbd = _host_s5_bd(inp)
    shared = {
        "w_in": f(inp["w_in"]), "w_br": f(inp["w_branch"]), "w_out": f(inp["w_out"]),
        "w_glu": f(inp["s5_glu_w"]), "w_mq": f(inp["ml_wq"]), "w_mk": f(inp["ml_wk"]), "w_mv": f(inp["ml_wv"]),
        "bbd": bbd, "cbd": cbd, "consts": consts, "pp": pp,
        "fnw": f(inp["final_norm_w"]).reshape(1, D),
    }
    maps = []
    for c in range(NCORE):
        b0 = c * NSEQ_S
        sl = slice(b0, b0 + NSEQ_S)
        m = dict(shared)
        m["xp"] = f(inp["x_prompt"][c, :n_ptiles * 128])
        m["xs"] = f(inp["x_sample"][sl]).reshape(NSEQ_S * QS, D)
        m["st_sconv"] = f(inp["state_ssd_conv"][:, sl]).reshape(L, NSEQ_S * 3, 768)
        m["st_ssd"] = f(inp["state_ssd"][:, sl])
        m["st_s5re"] = f(inp["state_s5_re"][:, sl]).reshape(L, NSEQ_S, 2048)
        m["st_s5im"] = f(inp["state_s5_im"][:, sl]).reshape(L, NSEQ_S, 2048)
        m["st_mconv"] = f(inp["state_mlstm_conv"][:, sl]).reshape(L, NSEQ_S * 3, 512)
        m["st_mC"] = f(inp["state_mlstm_C"][:, sl])
        m["st_mn"] = f(inp["state_mlstm_n"][:, sl])
        m["st_mm"] = f(inp["state_mlstm_m"][:, sl])
        m["st_hg"] = f(inp["state_hgrn"][:, sl])
        maps.append(m)
    return maps


def _gather(res, n_ptiles=16):
    R = res
    cat1 = lambda k, shp: np.stack([r[k] for r in R], axis=1).reshape(shp)
    cats = lambda k, shp: np.concatenate([r[k].reshape((L, NSEQ_S) + r[k].shape[2:]) if False else r[k] for r in R], axis=1)
    y_p = np.stack([r["y_p"] for r in R], axis=0)
    y_s = np.concatenate([r["y_s"].reshape(NSEQ_S, QS, D) for r in R], axis=0)
    B = NCORE

    def P_(k, tail):
        return np.stack([r[k].reshape((L,) + tail) for r in R], axis=1)

    def S_(k, tail):
        return np.concatenate([r[k].reshape((L, NSEQ_S) + tail) for r in R], axis=1)

    outs = (
        y_p, y_s,
        P_("o_sconv_p", (3, 768)), S_("o_sconv_s", (3, 768)),
        P_("o_ssd_p", (8, 64, 64)), S_("o_ssd_s", (8, 64, 64)),
        P_("o_s5re_p", (32, 64)), S_("o_s5re_s", (32, 64)),
        P_("o_s5im_p", (32, 64)), S_("o_s5im_s", (32, 64)),
        P_("o_mconv_p", (3, 512)), S_("o_mconv_s", (3, 512)),
        P_("o_mC_p", (4, 128, 128)), S_("o_mC_s", (4, 128, 128)),
        P_("o_mn_p", (4, 128)), S_("o_mn_s", (4, 128)),
        P_("o_mm_p", (4,)), S_("o_mm_s", (4,)),
        P_("o_hg_p", (4, 128, 128)), S_("o_hg_s", (4, 128, 128)),
    )
    return tuple(np.ascontiguousarray(o.astype(np.float32)) for o in outs)


def kernel(**inputs):
    n_ptiles = inputs["x_prompt"].shape[1] // 128
    nc, _ = build(n_ptiles=n_ptiles)
    maps = _prep_inputs(inputs, n_ptiles)
    res = run_bass_kernel_spmd(nc, maps, core_ids=list(range(NCORE)))
    return _gather(res.results, n_ptiles)
```

```python
import contextlib
import math
import numpy as np
import concourse.bass as bass
import concourse.mybir as mybir
from concourse.bass_utils import run_bass_kernel_spmd

F32 = mybir.dt.float32
I32 = mybir.dt.int32
AF = mybir.ActivationFunctionType
ALU = mybir.AluOpType
AX = mybir.AxisListType

L = 2
D = 1024
NCORE = 8
SEQ = 2048
NSEQ_S = 16
QS = 4
EPS = 1e-6
D_IN = 10000
NEG = -1.0e30
SSD_CUT = 99
NORM_CUT = 99
SSD_VAR = 0
TWO_PI = 2.0 * math.pi


class _Op:
    __slots__ = ("eng", "fn", "deps", "is_dma", "idx", "sig", "needed")

    def __init__(self, eng, fn, is_dma):
        self.eng = eng
        self.fn = fn
        self.deps = set()
        self.is_dma = is_dma
        self.sig = None
        self.needed = False


def _region(ap):
    t = ap.tensor
    tn = type(t).__name__
    if not (tn.startswith("SBTensor") or tn.startswith("PSum")):
        return None
    fs = 1
    for s in list(t.shape)[1:]:
        fs *= int(s)
    off = int(ap.offset)
    p0 = off // fs
    f0 = off % fs
    dims = list(ap.ap)
    pstep, pcnt = int(dims[0][0]), int(dims[0][1])
    if pstep == 0 or pcnt == 1:
        p1 = p0 + 1
    else:
        assert pstep == fs, (t.name, pstep, fs)
        p1 = p0 + pcnt
    ext = 1
    for st, cn in dims[1:]:
        ext += (int(cn) - 1) * abs(int(st))
    return (t.name, p0, p1, f0, f0 + ext)


def _isap(x):
    return x is not None and not isinstance(x, (int, float))


class Prog:
    def __init__(self, nc):
        self.nc = nc
        self.ops = []
        self.acc = {}
        self.stack = contextlib.ExitStack()
        self.psum_banks = []
        self.psum_i = 0
        self.arena = None
        self.aoff = 0
        self.tags = []

    def sb(self, name, shape, dtype=F32):
        return self.stack.enter_context(self.nc.sbuf_tensor(name, list(shape), dtype))

    def alloc_psum(self, n=8):
        for i in range(n):
            self.psum_banks.append(
                self.stack.enter_context(self.nc.psum_tensor("psb%d" % i, [128, 512], F32)))

    def ps(self):
        t = self.psum_banks[self.psum_i % len(self.psum_banks)]
        self.psum_i += 1
        return t

    def al(self, *shape, rows=128):
        n = 1
        for s in shape:
            n *= s
        off = self.aoff
        self.aoff += n
        assert self.aoff <= self.arena_n, ("arena overflow", self.aoff)
        v = self.arena[0:rows, off:off + n]
        if len(shape) == 2:
            v = v.rearrange("p (a b) -> p a b", a=shape[0])
        elif len(shape) == 3:
            v = v.rearrange("p (a b c) -> p a b c", a=shape[0], b=shape[1])
        return v

    def op(self, eng, fn, reads, writes, is_dma=False):
        o = _Op(eng, fn, is_dma)
        o.idx = len(self.ops)
        self.tags.append(getattr(self, "cur_tag", ""))
        engkey = ("dma", o.idx) if is_dma else eng
        rr = [r for r in (_region(a) for a in reads if _isap(a)) if r]
        ww = [r for r in (_region(a) for a in writes if _isap(a)) if r]
        if eng == "pe":
            ww = [(n, 0, 128, 0, 512) if n.startswith("psb") else (n, p0, p1, f0, f1) for (n, p0, p1, f0, f1) in ww]
        for (n, p0, p1, f0, f1) in rr:
            for e in self.acc.get(n, ()):
                if e[5] and e[0] < p1 and p0 < e[1] and e[2] < f1 and f0 < e[3]:
                    o.deps.add(e[4])
        for (n, p0, p1, f0, f1) in ww:
            for e in self.acc.get(n, ()):
                if e[0] < p1 and p0 < e[1] and e[2] < f1 and f0 < e[3]:
                    o.deps.add(e[4])
        for (n, p0, p1, f0, f1) in rr:
            if n.startswith("psb"):
                for e in self.acc.get(n, ()):
                    if (not e[5]) and e[6] != engkey:
                        o.deps.add(e[4])
        for (n, p0, p1, f0, f1) in ww:
            lst = self.acc.setdefault(n, [])
            lst[:] = [e for e in lst if not (p0 <= e[0] and e[1] <= p1 and f0 <= e[2] and e[3] <= f1)]
            lst.append([p0, p1, f0, f1, o.idx, True, engkey])
        for (n, p0, p1, f0, f1) in rr:
            lst = self.acc.setdefault(n, [])
            if not is_dma:
                lst[:] = [e for e in lst if not ((not e[5]) and e[6] == engkey and p0 <= e[0] and e[1] <= p1
                                                 and f0 <= e[2] and e[3] <= f1)]
            lst.append([p0, p1, f0, f1, o.idx, False, engkey])
        o.deps.discard(o.idx)
        self.ops.append(o)
        return o

    def mm(self, out, lhsT, rhs, start=True, stop=True):
        rd = [lhsT, rhs] + ([] if start else [out])
        return self.op("pe", lambda e: e.matmul(out, lhsT, rhs, start=start, stop=stop), rd, [out])

    def tr(self, out, in_, ident):
        return self.op("pe", lambda e: e.transpose(out, in_, ident), [in_, ident], [out])

    def act(self, out, in_, func, bias=0.0, scale=1.0):
        rd = [in_, bias, scale]
        return self.op("act", lambda e: e.activation(out, in_, func, bias=bias, scale=scale), rd, [out])

    def tt(self, out, in0, in1, op, eng="dve"):
        return self.op(eng, lambda e: e.tensor_tensor(out, in0, in1, op), [in0, in1], [out])

    def ts(self, out, in0, s1, op0, s2=None, op1=None, eng="dve"):
        rd = [in0, s1, s2]
        if op1 is None:
            return self.op(eng, lambda e: e.tensor_scalar(out, in0, s1, None, op0), rd, [out])
        return self.op(eng, lambda e: e.tensor_scalar(out, in0, s1, s2, op0, op1), rd, [out])

    def stt(self, out, in0, scalar, in1, op0, op1):
        rd = [in0, in1, scalar]
        return self.op("dve", lambda e: e.scalar_tensor_tensor(out, in0, scalar, in1, op0, op1), rd, [out])

    def copy(self, out, in_, eng="dve"):
        if eng == "act":
            return self.op("act", lambda e: e.activation(out, in_, AF.Copy), [in_], [out])
        return self.op(eng, lambda e: e.tensor_copy(out, in_), [in_], [out])

    def red(self, out, in_, op, axis=AX.X):
        return self.op("dve", lambda e: e.tensor_reduce(out, in_, axis, op), [in_], [out])

    def scan(self, out, d0, d1, init, op0=ALU.mult, op1=ALU.add):
        rd = [d0, d1, init]
        return self.op("dve", lambda e: e.tensor_tensor_scan(out, d0, d1, init, op0, op1), rd, [out])

    def recip(self, out, in_):
        return self.op("dve", lambda e: e.reciprocal(out, in_), [in_], [out])

    def memset(self, out, v, eng="dve"):
        return self.op(eng, lambda e: e.memset(out, v), [], [out])

    def dma(self, out, in_, q="sp"):
        return self.op(q, lambda e: e.dma_start(out=out, in_=in_), [in_], [out], is_dma=True)

    def emit(self, n_dma_sems=12):
        nc = self.nc
        ops = self.ops
        for o in ops:
            if o.is_dma:
                o.needed = True
        engs = ["pe", "act", "dve", "pool", "sp"]
        stack = self.stack
        csem = {e: stack.enter_context(nc.semaphore("cs_" + e)) for e in engs}
        dsem = {e: [stack.enter_context(nc.semaphore("ds_%s%d" % (e, i))) for i in range(n_dma_sems)]
                for e in ("sp", "pool", "act")}
        dcount = {e: [0] * n_dma_sems for e in dsem}
        dlast = {e: [None] * n_dma_sems for e in dsem}
        drr = {e: 0 for e in dsem}
        for o in ops:
            if o.is_dma:
                k = drr[o.eng] % n_dma_sems
                drr[o.eng] += 1
                if dlast[o.eng][k] is not None:
                    o.deps.add(dlast[o.eng][k])
                dlast[o.eng][k] = o.idx
        for o in ops:
            if o.eng == "pe":
                o.deps = {d for d in o.deps if ops[d].eng != "pe"}
        for o in ops:
            for d in o.deps:
                ops[d].needed = True
        ccount = {e: 0 for e in engs}
        drr = {e: 0 for e in dsem}
        for o in ops:
            if o.is_dma:
                k = drr[o.eng] % n_dma_sems
                drr[o.eng] += 1
                dcount[o.eng][k] += 16
                o.sig = (dsem[o.eng][k], dcount[o.eng][k], ("d", o.eng, k))
            elif o.needed:
                ccount[o.eng] += 1
                o.sig = (csem[o.eng], ccount[o.eng], ("c", o.eng))
        per = {e: [o for o in ops if o.eng == e] for e in engs}
        self.trace = {e: [] for e in engs}
        self.stats = {e: len(per[e]) for e in engs}

        def run(engname, eng):
            waited = {}
            for o in per[engname]:
                need = {}
                for d in o.deps:
                    s, v, key = ops[d].sig
                    if waited.get(key, 0) >= v:
                        continue
                    if key not in need or need[key][1] < v:
                        need[key] = (s, v)
                for key, (s, v) in need.items():
                    eng.wait_ge(s, v)
                    waited[key] = v
                self.trace[engname].append(([(k_, v_[1]) for k_, v_ in need.items()], o.sig[2] if o.sig else None, o.idx))
                ins = o.fn(eng)
                if o.sig is not None:
                    ins.then_inc(o.sig[0], 16 if o.is_dma else 1)
            if engname in dsem:
                for k in range(n_dma_sems):
                    if dcount[engname][k] > 0 and waited.get(("d", engname, k), 0) < dcount[engname][k]:
                        eng.wait_ge(dsem[engname][k], dcount[engname][k])

        with nc.Block() as block:
            @block.tensor
            def _(e):
                run("pe", e)

            @block.scalar
            def _(e):
                run("act", e)

            @block.vector
            def _(e):
                run("dve", e)

            @block.gpsimd
            def _(e):
                run("pool", e)

            @block.sync
            def _(e):
                run("sp", e)
        self.stack.close()

    def check_deadlock(self):
        sem = {}
        pos = {e: 0 for e in self.trace}
        total = sum(len(v) for v in self.trace.values())
        done = 0
        while done < total:
            prog = False
            for e, lst in self.trace.items():
                while pos[e] < len(lst):
                    waits, sig, idx = lst[pos[e]]
                    if all(sem.get(k, 0) >= v for k, v in waits):
                        if sig is not None:
                            sem[sig] = sem.get(sig, 0) + (16 if sig[0] == "d" else 1)
                        pos[e] += 1
                        done += 1
                        prog = True
                    else:
                        break
            if not prog:
                return {e: (pos[e], self.trace[e][pos[e]] if pos[e] < len(self.trace[e]) else None) for e in self.trace}
        return None


O_AZ = 0
O_XBC = 512
O_DT = 1280
O_U = 1288
O_GATE = 1800
O_CX = 2312
O_CZ = 2824
O_CO = 3336
O_CI = 3848
O_CF = 3852
O_DF = 3856
O_DI = 4368
O_DQ = 4880
O_DG = 5392
O_MERGE = 5904

C_ID = 0
C_ONES = 128
C_NEGU = 256
C_NEGL = 384
C_U01 = 512
C_SEL128 = 640
C_SEL4 = 768
C_TAU = 896
C_MASK4 = 960
C_MASKH_P = 1024
C_MASKH_S = 1536
NCONST = 1792

PP_NORMW = 0
PP_SCONV = 8
PP_DTB = 48
PP_ALOG = 49
PP_SSDD = 50
PP_SSDNW = 54
PP_S5D = 58
PP_ARE = 62
PP_AIM = 78
PP_LDT = 94
PP_MCONV = 110
PP_MIB = 130
PP_MFB = 131
PP_MNW = 132
PP_MSKIP = 136
PP_HL0 = 140
PP_HL1 = 144
PP_HNW = 148
NPP = 152

WB = 4096
NBUF = 3


def _host_consts():
    c = np.zeros((128, NCONST), np.float32)
    i = np.arange(128)[:, None]
    j = np.arange(128)[None, :]
    c[:, C_ID:C_ID + 128] = (i == j)
    c[:, C_ONES:C_ONES + 128] = 1.0
    c[:, C_NEGU:C_NEGU + 128] = np.where(j >= i, 0.0, NEG)
    c[:, C_NEGL:C_NEGL + 128] = np.where(j <= i, 0.0, NEG)
    c[:, C_U01:C_U01 + 128] = (j >= i)
    c[127, C_SEL128:C_SEL128 + 128] = 1.0
    c[3, C_SEL4:C_SEL4 + 128] = 1.0
    c[:, C_TAU:C_TAU + 64] = np.arange(1, 65)[None, :]
    c[:, C_MASK4:C_MASK4 + 64] = (np.arange(64) % 4 != 0)[None, :]
    c[:, C_MASKH_P:C_MASKH_P + 512] = (np.arange(512) % 64 != 0)[None, :]
    c[:, C_MASKH_S:C_MASKH_S + 256] = (np.arange(256) % 4 != 0)[None, :]
    return c


def _col(v, ntile):
    return np.ascontiguousarray(np.asarray(v, np.float32).reshape(ntile, 128).T)


def _host_pp(inp):
    pp = np.zeros((L, 128, NPP), np.float32)
    for l in range(L):
        p = pp[l]
        p[:, PP_NORMW:PP_NORMW + 8] = _col(inp["norm_w"][l], 8)
        cw = inp["ssd_conv_w"][l]
        cb = inp["ssd_conv_b"][l]
        blocks = [(0, 128), (128, 128), (256, 128), (384, 128), (512, 64), (576, 64), (640, 64), (704, 64)]
        for bi, (c0, n) in enumerate(blocks):
            for jj in range(4):
                p[0:n, PP_SCONV + bi * 5 + jj] = cw[jj, c0:c0 + n]
            p[0:n, PP_SCONV + bi * 5 + 4] = cb[c0:c0 + n]
        p[0:8, PP_DTB] = inp["ssd_dt_bias"][l]
        p[0:8, PP_ALOG] = inp["ssd_A_log"][l]
        p[:, PP_SSDD:PP_SSDD + 4] = _col(np.repeat(inp["ssd_D"][l], 64), 4)
        p[:, PP_SSDNW:PP_SSDNW + 4] = _col(inp["ssd_norm_w"][l], 4)
        p[:, PP_S5D:PP_S5D + 4] = _col(inp["s5_D"][l], 4)
        p[:, PP_ARE:PP_ARE + 16] = _col(inp["s5_A_re"][l].reshape(-1), 16)
        p[:, PP_AIM:PP_AIM + 16] = _col(inp["s5_A_im"][l].reshape(-1), 16)
        p[:, PP_LDT:PP_LDT + 16] = _col(np.repeat(inp["s5_log_dt"][l], 64), 16)
        mw = inp["ml_conv_w"][l]
        mb = inp["ml_conv_b"][l]
        for bi in range(4):
            for jj in range(4):
                p[:, PP_MCONV + bi * 5 + jj] = mw[jj, bi * 128:(bi + 1) * 128]
            p[:, PP_MCONV + bi * 5 + 4] = mb[bi * 128:(bi + 1) * 128]
        p[0:4, PP_MIB] = inp["ml_i_bias"][l]
        p[0:4, PP_MFB] = inp["ml_f_bias"][l]
        p[:, PP_MNW:PP_MNW + 4] = _col(inp["ml_norm_w"][l], 4)
        p[:, PP_MSKIP:PP_MSKIP + 4] = _col(inp["ml_skip"][l], 4)
        p[:, PP_HL0:PP_HL0 + 4] = _col(inp["hg_lb_logits"][0], 4)
        p[:, PP_HL1:PP_HL1 + 4] = _col(inp["hg_lb_logits"][1], 4)
        p[:, PP_HNW:PP_HNW + 4] = _col(inp["hg_norm_w"][l], 4)
    return pp


def _host_s5_bd(inp):
    bbd = np.zeros((L, 2, 128, 4, 512), np.float32)
    cbd = np.zeros((L, 2, 128, 16, 128), np.float32)
    for l in range(L):
        for ri, (bn, cn) in enumerate((("s5_B_re", "s5_C_re"), ("s5_B_im", "s5_C_im"))):
            B = inp[bn][l]
            C = inp[cn][l]
            for g in range(32):
                ct = g // 8
                gl8 = g % 8
                sl = gl8 // 2
                g2 = gl8 % 2
                st = ct * 4 + sl
                bbd[l, ri, gl8 * 16:(gl8 + 1) * 16, ct, sl * 128 + g2 * 64: sl * 128 + (g2 + 1) * 64] = B[g].T
                cbd[l, ri, g2 * 64:(g2 + 1) * 64, st, gl8 * 16:(gl8 + 1) * 16] = C[g].T
    return bbd, cbd


def build(n_ptiles=16, with_sample=True, stages=("ssd", "s5", "ml", "hg", "merge", "proj")):
    nc = bass.Bass("TRN2", target_bir_lowering=False)
    NT = n_ptiles * 128

    def din(name, shape):
        return nc.dram_tensor(name, list(shape), F32, kind="ExternalInput").ap()

    def dout(name, shape):
        return nc.dram_tensor(name, list(shape), F32, kind="ExternalOutput").ap()

    xp_d = din("xp", [NT, D])
    xs_d = din("xs", [NSEQ_S * QS, D])
    st_sconv = din("st_sconv", [L, NSEQ_S * 3, 768])
    st_ssd = din("st_ssd", [L, NSEQ_S, 8, 64, 64])
    st_s5re = din("st_s5re", [L, NSEQ_S, 2048])
    st_s5im = din("st_s5im", [L, NSEQ_S, 2048])
    st_mconv = din("st_mconv", [L, NSEQ_S * 3, 512])
    st_mC = din("st_mC", [L, NSEQ_S, 4, 128, 128])
    st_mn = din("st_mn", [L, NSEQ_S, 4, 128])
    st_mm = din("st_mm", [L, NSEQ_S, 4])
    st_hg = din("st_hg", [L, NSEQ_S, 4, 128, 128])
    w_in = din("w_in", [L, D, D_IN])
    w_br = din("w_br", [L, 4, 512, D])
    w_out = din("w_out", [L, D, D])
    w_glu = din("w_glu", [L, 512, 512])
    w_mq = din("w_mq", [L, 4, 128, 128])
    w_mk = din("w_mk", [L, 4, 128, 128])
    w_mv = din("w_mv", [L, 4, 128, 128])
    bbd_d = din("bbd", [L, 2, 128, 4, 512])
    cbd_d = din("cbd", [L, 2, 128, 16, 128])
    const_d = din("consts", [128, NCONST])
    pp_d = din("pp", [L, 128, NPP])
    fnw_d = din("fnw", [1, D])

    y_p = dout("y_p", [NT, D])
    y_s = dout("y_s", [NSEQ_S * QS, D])
    o_sconv_p = dout("o_sconv_p", [L, 3, 768])
    o_sconv_s = dout("o_sconv_s", [L, NSEQ_S * 3, 768])
    o_ssd_p = dout("o_ssd_p", [L, 8, 64, 64])
    o_ssd_s = dout("o_ssd_s", [L, NSEQ_S, 8, 64, 64])
    o_s5re_p = dout("o_s5re_p", [L, 16, 128])
    o_s5re_s = dout("o_s5re_s", [L, NSEQ_S, 2048])
    o_s5im_p = dout("o_s5im_p", [L, 16, 128])
    o_s5im_s = dout("o_s5im_s", [L, NSEQ_S, 2048])
    o_mconv_p = dout("o_mconv_p", [L, 3, 512])
    o_mconv_s = dout("o_mconv_s", [L, NSEQ_S * 3, 512])
    o_mC_p = dout("o_mC_p", [L, 4, 128, 128])
    o_mC_s = dout("o_mC_s", [L, NSEQ_S, 4, 128, 128])
    o_mn_p = dout("o_mn_p", [L, 4, 128])
    o_mn_s = dout("o_mn_s", [L, NSEQ_S, 4, 128])
    o_mm_p = dout("o_mm_p", [L, 1, 4])
    o_mm_s = dout("o_mm_s", [L, NSEQ_S, 4])
    o_hg_p = dout("o_hg_p", [L, 4, 128, 128])
    o_hg_s = dout("o_hg_s", [L, NSEQ_S, 4, 128, 128])

    P = Prog(nc)
    P.alloc_psum(8)
    ARENA_N = 9216 + 512
    P.arena = P.sb("arena", [128, ARENA_N])
    P.arena_n = ARENA_N

    HT_P = P.sb("hT_p", [128, L, 2, 256])
    CT = P.sb("consts_t", [128, NCONST])
    PPT = P.sb("pp_t", [128, L, NPP])
    FNW = P.sb("fnw_t", [128, D])
    WBUF = [P.sb("wbuf%d" % i, [128, WB]) for i in range(NBUF)]
    wctr = [0]

    def wnext():
        b = WBUF[wctr[0] % NBUF]
        wctr[0] += 1
        return b

    ident = CT[:, C_ID:C_ID + 128]
    ones = CT[:, C_ONES:C_ONES + 128]

    AH = P.sb("ssdA", [128, L])
    LB = P.sb("hg_lb", [128, L, 4])
    OML = P.sb("hg_oml", [128, L, 4])
    COS = P.sb("s5cos", [128, L, 16, 64])
    SIN = P.sb("s5sin", [128, L, 16, 64])
    RHO = P.sb("s5rho", [128, L, 16])
    CR = P.sb("s5cr", [128, L, 16])
    CI = P.sb("s5ci", [128, L, 16])
    E2R = P.sb("s5e2r", [128, L, 16, 64])
    E2I = P.sb("s5e2i", [128, L, 16, 64])

    XTK = P.sb("x_tok", [128, D])
    PTK = [P.sb("ptk%d" % i, [128, 520]) for i in range(2)]
    XN = P.sb("xnT", [128, 8, 128])
    PJ_Z = P.sb("pj_z", [128, 4, 128])
    XPS = P.sb("xp_ssd", [128, 8, 131])
    PJ_DT = P.sb("pj_dt", [128, 128])
    PJ_U = P.sb("pj_u", [128, 4, 128])
    PJ_GATE = P.sb("pj_gate", [128, 4, 128])
    XPM = P.sb("xp_ml", [128, 4, 131])
    PJ_CZ = P.sb("pj_cz", [128, 4, 128])
    PJ_CO = P.sb("pj_co", [128, 4, 128])
    PJ_CI = P.sb("pj_ci", [128, 128])
    PJ_CF = P.sb("pj_cf", [128, 128])
    PJ_DF = P.sb("pj_df", [128, 4, 128])
    PJ_DI = P.sb("pj_di", [128, 4, 128])
    PJ_DQ = P.sb("pj_dq", [128, 4, 128])
    PJ_DG = P.sb("pj_dg", [128, 4, 128])
    YB = [P.sb("ybr%d" % i, [128, 4, 128]) for i in range(4)]
    MRGK = P.sb("merged_tok", [128, D])

    HIST_S = P.sb("hist_s", [128, L, 8, 3])
    HIST_M = P.sb("hist_m", [128, L, 4, 3])
    S5R_P = P.sb("s5r_p", [128, L, 16])
    S5I_P = P.sb("s5i_p", [128, L, 16])
    MC_P = P.sb("mC_p", [128, L, 4, 128])
    MN_P = P.sb("mn_p", [128, L, 4, 2])
    MM_P = P.sb("mm_p", [128, L, 4])
    HG_P = P.sb("hg_p", [128, L, 4, 128])
    HT_S = P.sb("hT_s", [128, 2, 256])
    HNAT = P.sb("hnat", [128, 8, 64])
    S5R_S = P.sb("s5r_s", [128, 16, NSEQ_S])
    S5I_S = P.sb("s5i_s", [128, 16, NSEQ_S])
    MC_S = P.sb("mC_s", [128, 4, 128])
    MN_S = P.sb("mn_s", [128, 4, 2])
    MM_S = P.sb("mm_s", [128, 4])
    HG_S = P.sb("hg_s", [128, 4, 128])

    def pc(l, col, n=1, rows=128):
        return PPT[0:rows, l, col:col + n]

    P.dma(CT[:], const_d[:, :])
    P.dma(PPT[:], pp_d.rearrange("l p c -> p l c"))
    P.dma(FNW[:], fnw_d[0:1, :].partition_broadcast(128))
    for t_, v_ in ((HIST_S, 0.0), (HIST_M, 0.0), (S5R_P, 0.0), (S5I_P, 0.0), (MC_P, 0.0),
                   (MN_P, 0.0), (MM_P, 0.0), (HG_P, 0.0)):
        P.memset(t_[:], v_)
    P.memset(HT_P[:], 0.0)
    for t_ in (XPS, XPM, PJ_DT, PJ_CI, PJ_CF, P.arena):
        P.memset(t_[:], 0.0)

    def range_reduce(a, tmpf, tmpi):
        P.ts(tmpf, a, 1.0 / TWO_PI, ALU.mult)
        P.copy(tmpi, tmpf)
        P.copy(tmpf, tmpi)
        P.stt(a, tmpf, -TWO_PI, a, ALU.mult, ALU.add)
        P.ts(tmpf, a, math.pi, ALU.is_gt, TWO_PI, ALU.mult)
        P.tt(a, a, tmpf, ALU.subtract)
        P.ts(tmpf, a, -math.pi, ALU.is_lt, TWO_PI, ALU.mult)
        P.tt(a, a, tmpf, ALU.add)
        P.ts(a, a, math.pi, ALU.min, -math.pi, ALU.max)

    for l in range(L):
        P.aoff = 0
        P.act(AH[0:8, l:l + 1], pc(l, PP_ALOG, 1, 8), AF.Exp)
        P.ts(AH[0:8, l:l + 1], AH[0:8, l:l + 1], -1.0, ALU.mult)
        if l == 0:
            P.memset(LB[:, 0, :], 0.0)
        else:
            dl_ = P.al(4)
            P.tt(dl_, pc(l, PP_HL1, 4), pc(l, PP_HL0, 4), ALU.subtract)
            P.act(LB[:, l, :], dl_, AF.Sigmoid)
        P.ts(OML[:, l, :], LB[:, l, :], -1.0, ALU.mult, 1.0, ALU.add)
        dt = P.al(16)
        P.act(dt, pc(l, PP_LDT, 16), AF.Exp)
        lrdt = P.al(16)
        P.tt(lrdt, pc(l, PP_ARE, 16), dt, ALU.mult)
        P.act(RHO[:, l, :], lrdt, AF.Exp)
        th = P.al(16)
        P.tt(th, pc(l, PP_AIM, 16), dt, ALU.mult)
        ang = P.al(16, 64)
        tmpf = P.al(16, 64)
        tau = CT[:, C_TAU:C_TAU + 64]
        P.tt(ang, th.unsqueeze(2).to_broadcast([128, 16, 64]), tau.unsqueeze(1).to_broadcast([128, 16, 64]),
             ALU.mult)
        ang2 = P.al(16, 64)
        P.ts(ang2, ang, math.pi / 2.0, ALU.add)
        ti3 = P.al(16, 64).bitcast(I32)
        range_reduce(ang, tmpf, ti3)
        range_reduce(ang2, tmpf, ti3)
        P.act(SIN[:, l, :, :], ang, AF.Sin)
        P.act(COS[:, l, :, :], ang2, AF.Sin)
        abr = P.al(16)
        abi = P.al(16)
        P.tt(abr, RHO[:, l, :], COS[:, l, :, 0], ALU.mult)
        P.tt(abi, RHO[:, l, :], SIN[:, l, :, 0], ALU.mult)
        am1 = P.al(16)
        P.ts(am1, abr, -1.0, ALU.add)
        lr = pc(l, PP_ARE, 16)
        li = pc(l, PP_AIM, 16)
        den = P.al(16)
        t0 = P.al(16)
        P.tt(den, lr, lr, ALU.mult)
        P.tt(t0, li, li, ALU.mult)
        P.tt(den, den, t0, ALU.add)
        P.recip(den, den)
        t1 = P.al(16)
        P.tt(t0, am1, lr, ALU.mult)
        P.tt(t1, abi, li, ALU.mult)
        P.tt(t0, t0, t1, ALU.add)
        P.tt(CR[:, l, :], t0, den, ALU.mult)
        P.tt(t0, abi, lr, ALU.mult)
        P.tt(t1, am1, li, ALU.mult)
        P.tt(t0, t0, t1, ALU.subtract)
        P.tt(CI[:, l, :], t0, den, ALU.mult)
        crb_ = CR[:, l, :].unsqueeze(2).to_broadcast([128, 16, 64])
        cib_ = CI[:, l, :].unsqueeze(2).to_broadcast([128, 16, 64])
        P.tt(ang, COS[:, l, :, :], crb_, ALU.mult)
        P.tt(ang2, SIN[:, l, :, :], cib_, ALU.mult)
        P.tt(E2R[:, l, :, :], ang, ang2, ALU.add)
        P.tt(ang, COS[:, l, :, :], cib_, ALU.mult)
        P.tt(ang2, SIN[:, l, :, :], crb_, ALU.mult)
        P.tt(E2I[:, l, :, :], ang, ang2, ALU.subtract)

    def rmsnorm_fm(src, n, T, wcol_l, wcol, dst, dmodel):
        sq = P.al(n, T)
        P.act(sq, src, AF.Square)
        ps = P.ps()
        for k in range(n):
            P.mm(ps[:, 0:T], ones, sq[:, k, :], start=(k == 0), stop=(k == n - 1))
        rstd = P.al(T)
        P.act(rstd, ps[:, 0:T], AF.Sqrt, bias=EPSB[:, 0:1], scale=1.0 / dmodel)
        P.recip(rstd, rstd)
        for k in range(n):
            P.stt(dst[:, k, :], src[:, k, :], pc(wcol_l, wcol + k), rstd, ALU.mult, ALU.mult)

    EPSB = P.sb("epsb", [128, 2])
    P.memset(EPSB[:, 0:1], EPS)
    P.memset(EPSB[:, 1:2], 1.0)

    def bc3(ap2, n_mid=None, n_last=None):
        rows = ap2.shape[0]
        if n_last is not None:
            return ap2.unsqueeze(2).to_broadcast([rows, ap2.shape[1], n_last])
        return ap2.unsqueeze(1).to_broadcast([rows, n_mid, ap2.shape[1]])

    def stream(dst_view, src):
        P.dma(dst_view, src)

    def proj_blocks(l, T, nseq, Q):
        w_l = w_in[l].rearrange("(kt p) c -> p kt c", p=128)
        xps4 = XPS[:, :, 0:nseq * (Q + 3)].rearrange("p b (s q) -> p b s q", s=nseq)
        xpm4 = XPM[:, :, 0:nseq * (Q + 3)].rearrange("p b (s q) -> p b s q", s=nseq)

        def seqv(ps_ap):
            return ps_ap.rearrange("p (s q) -> p s q", s=nseq)

        groups = []

        def ev_act(dst_fn, func, bias=None, scale=1.0):
            def f(ps_ap, bi, rows):
                P.act(dst_fn(bi, rows), ps_ap, func, bias=(bias(rows) if bias else 0.0), scale=scale)
            return f

        def simple(col0, tile, func):
            blks = [(i * 128, 128, None) for i in range(4)]
            groups.append((col0, 512, blks,
                           (lambda ps2, t=tile, f=func: P.act(t[:, :, 0:T], ps2[:, 0:4 * T].rearrange("p (i t) -> p i t", i=4), f))))

        simple(O_AZ, PJ_Z, AF.Silu)
        groups.append((O_XBC, 512, [(i * 128, 128, None) for i in range(4)],
                       (lambda ps2: P.act(xps4[:, 0:4, :, 3:3 + Q],
                                          ps2[:, 0:4 * T].rearrange("p (i s q) -> p i s q", i=4, s=nseq), AF.Copy))))
        blks = [(j * 64, 64, (lambda ps_ap, bi, rows, j=j: P.act(xps4[0:rows, 4 + j, :, 3:3 + Q], seqv(ps_ap), AF.Copy)))
                for j in range(4)]
        blks.append((256, 8, (lambda ps_ap, bi, rows: P.act(PJ_DT[0:rows, 0:T], ps_ap, AF.Copy))))
        groups.append((O_XBC + 512, 264, blks))
        simple(O_U, PJ_U, AF.Copy)
        simple(O_GATE, PJ_GATE, AF.Silu)
        groups.append((O_CX, 512, [(i * 128, 128, None) for i in range(4)],
                       (lambda ps2: P.act(xpm4[:, 0:4, :, 3:3 + Q],
                                          ps2[:, 0:4 * T].rearrange("p (i s q) -> p i s q", i=4, s=nseq), AF.Copy))))
        simple(O_CZ, PJ_CZ, AF.Silu)
        simple(O_CO, PJ_CO, AF.Sigmoid)
        groups.append((O_CI, 8, [
            (0, 4, (lambda ps_ap, bi, rows: P.act(PJ_CI[0:rows, 0:T], ps_ap, AF.Identity, bias=pc(l, PP_MIB, 1, 4)))),
            (4, 4, (lambda ps_ap, bi, rows: P.act(PJ_CF[0:rows, 0:T], ps_ap, AF.Copy)))]))
        simple(O_DF, PJ_DF, AF.Sigmoid)
        simple(O_DI, PJ_DI, AF.Copy)
        simple(O_DQ, PJ_DQ, AF.Silu)
        simple(O_DG, PJ_DG, AF.Silu)

        tags = [0, 0, 0, 1, 1, 2, 2, 2, 2, 3, 3, 3, 3]
        assert len(tags) == len(groups)

        def run_group(grp, gi):
            col0, ncols, blks = grp[0], grp[1], grp[2]
            gevac = grp[3] if len(grp) > 3 else None
            wb = wnext()
            wv = wb[:, 0:8 * ncols].rearrange("p (k c) -> p k c", k=8)
            stream(wv, w_l[:, :, col0:col0 + ncols])
            ps = P.ps()
            for kt in range(8):
                P.mm(ps[0:T, 0:ncols], XN[:, kt, 0:T], wv[:, kt, 0:ncols], start=(kt == 0), stop=(kt == 7))
            tk = PTK[gi % 2]
            P.copy(tk[0:T, 0:ncols], ps[0:T, 0:ncols], eng=("act" if gi % 2 == 0 else "dve"))
            for b0 in range(0, len(blks), 4):
                sub = blks[b0:b0 + 4]
                ps2 = P.ps()
                for bj, (co, n, evac) in enumerate(sub):
                    P.tr(ps2[0:n, bj * T:(bj + 1) * T], tk[0:T, co:co + n], ident[0:T, 0:T])
                if gevac is not None:
                    gevac(ps2)
                else:
                    for bj, (co, n, evac) in enumerate(sub):
                        evac(ps2[0:n, bj * T:(bj + 1) * T], None, n)

        return [(tags[gi], (lambda g=grp, gi=gi: run_group(g, gi))) for gi, grp in enumerate(groups)]

    pending = []

    def pump(n=1, maxtag=99):
        for _ in range(n):
            if pending and pending[0][0] <= maxtag:
                pending.pop(0)[1]()

    def flush_upto(tag):
        while pending and pending[0][0] <= tag:
            pending.pop(0)[1]()

    def conv_fm(xp4, nblk_rows, l, ppbase, acc4, Q):
        for bi, rows in enumerate(nblk_rows):
            c = ppbase + bi * 5
            P.ts(acc4[0:rows, bi], xp4[0:rows, bi, :, 0:Q], pc(l, c, 1, rows), ALU.mult,
                 pc(l, c + 4, 1, rows), ALU.add)
            for j in range(1, 4):
                P.stt(acc4[0:rows, bi], xp4[0:rows, bi, :, j:j + Q], pc(l, c + j, 1, rows), acc4[0:rows, bi],
                      ALU.mult, ALU.add)

    def ssd_branch(l, T, nseq, Q, sample, first, last, core_out):
        xps4 = XPS[:, :, 0:nseq * (Q + 3)].rearrange("p b (s q) -> p b s q", s=nseq)
        rows8 = [128] * 4 + [64] * 4
        acc = P.al(8, nseq, Q)
        conv_fm(xps4, rows8, l, PP_SCONV, acc, Q)
        xc = P.al(8, T)
        accf = acc.rearrange("p b s q -> p b (s q)")
        P.act(xc[:, 0:4, :], accf[:, 0:4, :], AF.Silu)
        P.act(xc[0:64, 4:8, :], accf[0:64, 4:8, :], AF.Silu)
        if SSD_CUT == 1:
            P.memset(YB[0][:], 0.0)
            return
        if not sample:
            P.copy(HIST_S[:, l, :, :], xps4[:, :, 0, Q:Q + 3], eng="act")
        dte = P.al(T)
        P.act(dte[0:8], PJ_DT[0:8, 0:T], AF.Exp, bias=pc(l, PP_DTB, 1, 8))
        dtT = P.al(T)
        P.act(dtT[0:8], dte[0:8], AF.Ln, bias=EPSB[0:8, 1:2])
        aT = P.al(T)
        P.ts(aT[0:8], dtT[0:8], AH[0:8, l:l + 1], ALU.mult)
        acT = P.al(T)
        if sample:
            P.scan(acT[0:8], CT[0:8, C_MASK4:C_MASK4 + T], aT[0:8], 0.0)
        else:
            P.scan(acT[0:8], CT[0:8, C_ONES:C_ONES + T], aT[0:8], 0.0)
        if SSD_CUT == 2:
            P.memset(YB[0][:], 0.0)
            return
        ytT = P.al(4, T)
        mark = P.aoff
        for s in range(nseq):
            P.aoff = mark
            c0 = s * Q
            sl = slice(c0, c0 + Q)
            if sample:
                hT = HT_S[0:64]
                P.dma(HNAT[0:64], st_ssd[l, s].rearrange("h p n -> p h n"))
                psx = P.ps()
                for g in range(2):
                    for r in range(4):
                        P.tr(psx[0:64, (g * 4 + r) * 64:(g * 4 + r + 1) * 64], HNAT[0:64, 4 * g + r, :], ident[0:64, 0:64])
                P.copy(HT_S[0:64].rearrange("p g c -> p (g c)"), psx[0:64, 0:512])
            else:
                hT = HT_P[0:64, l]
            psA = P.ps()
            for i in range(4):
                P.tr(psA[0:Q, i * 128:(i + 1) * 128], xc[:, i, sl], ident)
            psB = P.ps()
            for g in range(2):
                P.tr(psB[0:Q, g * 64:(g + 1) * 64], xc[0:64, 4 + g, sl], ident[0:64, 0:64])
            P.tr(psB[0:Q, 128:136], dtT[0:8, sl], ident[0:8, 0:8])
            P.tr(psB[0:Q, 136:144], acT[0:8, sl], ident[0:8, 0:8])
            btok = P.al(128)
            dtac = P.al(16)
            P.copy(btok[0:Q], psB[0:Q, 0:128], eng="act")
            P.copy(dtac[0:Q], psB[0:Q, 128:144], eng="act")
            dt_tok = dtac[0:Q, 0:8]
            ac_tok = dtac[0:Q, 8:16]
            xdt = P.al(8, 64)
            P.tt(xdt[0:Q], psA[0:Q, :].rearrange("p (h c) -> p h c", h=8), bc3(dt_tok, n_last=64), ALU.mult)
            if SSD_CUT == 3:
                break
            adiag = P.al(8, Q)
            P.tt(adiag[0:Q], bc3(ident[0:Q, 0:Q], n_mid=8), bc3(ac_tok, n_last=Q), ALU.mult)
            pump(2)
            nb = 2 if Q == 128 else 1
            hb = 8 // nb
            psR = [P.ps() for _ in range(nb)]
            for b_ in range(nb):
                P.mm(psR[b_][:, 0:hb * Q].rearrange("p (h t) -> p h t", h=hb), ones[0:Q, :],
                     adiag[0:Q, b_ * hb:(b_ + 1) * hb, :])
            alast = P.al(8)
            for b_ in range(nb):
                P.copy(alast[:, b_ * hb:(b_ + 1) * hb],
                       psR[b_][:, 0:hb * Q].rearrange("p (h t) -> p h t", h=hb)[:, :, Q - 1], eng="act")
            dec = P.al(8, Q)
            for b_ in range(nb):
                P.tt(dec[0:Q, b_ * hb:(b_ + 1) * hb, :],
                     psR[b_][0:Q, 0:hb * Q].rearrange("p (h t) -> p h t", h=hb),
                     bc3(ac_tok[:, b_ * hb:(b_ + 1) * hb], n_last=Q), ALU.subtract)
            P.tt(dec[0:Q], dec[0:Q], bc3(CT[0:Q, C_NEGU:C_NEGU + Q], n_mid=8), ALU.add)
            P.act(dec[0:Q], dec[0:Q], AF.Exp)
            if SSD_CUT == 4:
                break
            psC = P.ps()
            for g in range(2):
                P.mm(psC[0:Q, g * Q:(g + 1) * Q], xc[0:64, 4 + g, sl], xc[0:64, 6 + g, sl])
            MT = P.al(8, Q)
            for g in range(2):
                P.tt(MT[0:Q, 4 * g:4 * g + 4, :], dec[0:Q, 4 * g:4 * g + 4, :],
                     bc3(psC[0:Q, g * Q:(g + 1) * Q], n_mid=4), ALU.mult)
            pump(2)
            psY = P.ps()
            for h in range(8):
                P.mm(psY[0:Q, h * 64:(h + 1) * 64], MT[0:Q, h, :], xdt[0:Q, h, :])
            psS = P.ps()
            for g in range(2):
                P.mm(psS[0:Q, g * 256:(g + 1) * 256], xc[0:64, 6 + g, sl], hT[:, g, :])
            eac = P.al(8)
            P.act(eac[0:Q], ac_tok, AF.Exp)
            ytok = P.al(8, 64)
            P.tt(ytok[0:Q], psS[0:Q, :].rearrange("p (h c) -> p h c", h=8), bc3(eac[0:Q], n_last=64), ALU.mult)
            P.tt(ytok[0:Q], ytok[0:Q], psY[0:Q, :].rearrange("p (h c) -> p h c", h=8), ALU.add)
            if SSD_CUT == 6:
                break
            elast = P.al(8)
            P.act(elast, alast, AF.Exp)
            if SSD_CUT == 61:
                break
            toend = P.al(8)
            P.tt(toend[0:Q], alast[0:Q], ac_tok, ALU.subtract)
            P.act(toend[0:Q], toend[0:Q], AF.Exp)
            xe = P.al(8, 64)
            P.tt(xe[0:Q], xdt[0:Q], bc3(toend[0:Q], n_last=64), ALU.mult)
            if SSD_CUT == 62:
                break
            pump(2)
            psH = P.ps()
            for g in range(2):
                P.mm(psH[0:64, g * 256:(g + 1) * 256], btok[0:Q, g * 64:(g + 1) * 64],
                     xe[0:Q, 4 * g:4 * g + 4, :].rearrange("p h c -> p (h c)"))
            if SSD_CUT == 63:
                break
            hsc = P.al(2, 256)
            for g in range(2):
                for r in range(4):
                    P.act(hsc[0:64, g, r * 64:(r + 1) * 64], hT[:, g, r * 64:(r + 1) * 64], AF.Copy,
                          scale=elast[0:64, 4 * g + r:4 * g + r + 1])
            for g in range(2):
                P.tt(hT[:, g, :], hsc[0:64, g, :], psH[0:64, g * 256:(g + 1) * 256], ALU.add)
            pump(2)
            psT = P.ps()
            if SSD_CUT == 64:
                break
            yf = ytok[0:Q].rearrange("p h c -> p (h c)")
            for i in range(4):
                P.tr(psT[:, i * Q:(i + 1) * Q], yf[:, i * 128:(i + 1) * 128], ident[0:Q, 0:Q])
            P.copy(ytT[:, :, sl], psT[:, 0:4 * Q].rearrange("p (i q) -> p i q", i=4), eng="act")
            if SSD_CUT == 8:
                break
            if sample or last:
                pso = P.ps()
                for g in range(2):
                    for r in range(4):
                        P.tr(pso[0:64, (4 * g + r) * 64:(4 * g + r + 1) * 64], hT[:, g, r * 64:(r + 1) * 64],
                             ident[0:64, 0:64])
                hout = P.al(8, 64)
                P.copy(hout[0:64], pso[0:64, :].rearrange("p (h n) -> p h n", h=8))
                dst = o_ssd_s[l, s] if sample else o_ssd_p[l]
                P.dma(dst.rearrange("h p n -> p h n"), hout[0:64], q="act")
        P.aoff = mark
        if sample or last:
            cvi = P.al(8, nseq, 3)
            if sample:
                P.copy(cvi, xps4[:, :, :, Q:Q + 3])
            else:
                P.copy(cvi[:, :, 0, :], HIST_S[:, l, :, :])
            n3 = nseq * 3
            psv = [P.ps(), P.ps()]
            cols = [0, 128, 256, 384, 512, 576, 640, 704]
            for bi in range(8):
                rows = rows8[bi]
                pv = psv[0] if cols[bi] < 512 else psv[1]
                cc = cols[bi] % 512
                P.tr(pv[0:n3, cc:cc + rows], cvi[0:rows, bi].rearrange("p s j -> p (s j)"), ident[0:rows, 0:rows])
            cvo = P.al(768)
            P.copy(cvo[0:n3, 0:512], psv[0][0:n3, 0:512])
            P.copy(cvo[0:n3, 512:768], psv[1][0:n3, 0:256])
            dst = o_sconv_s[l] if sample else o_sconv_p[l]
            P.dma(dst, cvo[0:n3, :], q="act")
        yg = P.al(4, T)
        for i in range(4):
            P.stt(yg[:, i, :], xc[:, i, :], pc(l, PP_SSDD + i), ytT[:, i, :], ALU.mult, ALU.add)
        P.tt(yg, yg, PJ_Z[:, :, 0:T], ALU.mult)
        rmsnorm_fm(yg, 4, T, l, PP_SSDNW, YB[0][:, :, 0:T], 512.0)

    def s5_branch(l, T, nseq, Q, sample, last):
        wb = wnext()
        Bv = wb[:, 0:4096].rearrange("p (r c m) -> p r c m", r=2, c=4)
        stream(Bv[:, 0], bbd_d[l, 0])
        stream(Bv[:, 1], bbd_d[l, 1])
        if sample:
            nat = P.al(2048)
            for (src, dstt) in ((st_s5re, S5R_S), (st_s5im, S5I_S)):
                P.dma(nat[0:NSEQ_S], src[l])
                psx = P.ps()
                for st in range(16):
                    P.tr(psx[:, st * 16:(st + 1) * 16], nat[0:NSEQ_S, st * 128:(st + 1) * 128], ident[0:16, 0:16])
                P.copy(dstt[:].rearrange("p s b -> p (s b)"), psx[:, 0:256])
            subs = [(0, NSEQ_S, QS)]
            SR, SI = S5R_S[:], S5I_S[:]
        else:
            subs = [(0, 1, 64), (64, 1, 64)]
            SR, SI = S5R_P[:, l], S5I_P[:, l]
        HR = P.al(16, T)
        NHI = P.al(16, T)
        mark = P.aoff
        for (c0, ns, q) in subs:
            P.aoff = mark
            W = ns * q
            wre = P.al(16, W)
            wim = P.al(16, W)
            t1 = P.al(4, W)
            t2 = P.al(4, W)
            for ct in range(4):
                ps = P.ps()
                for sl_ in range(4):
                    P.mm(ps[:, sl_ * 128:sl_ * 128 + W], Bv[:, 0, ct, sl_ * 128:(sl_ + 1) * 128], PJ_U[:, ct, c0:c0 + W])
                    P.mm(ps[:, sl_ * 128 + 64:sl_ * 128 + 64 + W], Bv[:, 1, ct, sl_ * 128:(sl_ + 1) * 128],
                         PJ_U[:, ct, c0:c0 + W])
                p4 = ps[:, :].rearrange("p (s r w) -> p s r w", s=4, r=2)
                pr = p4[:, :, 0, 0:W]
                pi = p4[:, :, 1, 0:W]
                stv = slice(ct * 4, ct * 4 + 4)
                if ns == 1:
                    e2r = E2R[:, l, stv, 0:q]
                    e2i = E2I[:, l, stv, 0:q]
                    v = lambda a: a
                else:
                    e2r = E2R[:, l, stv, 0:q].unsqueeze(2).to_broadcast([128, 4, ns, q])
                    e2i = E2I[:, l, stv, 0:q].unsqueeze(2).to_broadcast([128, 4, ns, q])
                    v = lambda a: a.rearrange("p s (b q) -> p s b q", b=ns)
                P.tt(v(t1), v(pr), e2r, ALU.mult)
                P.tt(v(t2), v(pi), e2i, ALU.mult)
                P.tt(wre[:, stv, :], t1, t2, ALU.subtract)
                P.tt(v(t1), v(pi), e2r, ALU.mult)
                P.tt(v(t2), v(pr), e2i, ALU.mult)
                P.tt(wim[:, stv, :], t1, t2, ALU.add)
            gre = P.al(16, W)
            gim = P.al(16, W)
            lastsub = (c0, ns, q) == subs[-1]
            if (not lastsub) and (not sample):
                pump(2)
            if lastsub and not sample:
                pump_merge(0, 2)
            if ns == 1:
                for st in range(16):
                    rb = RHO[:, l, st:st + 1].to_broadcast([128, W])
                    P.scan(gre[:, st, :], rb, wre[:, st, :], SR[:, st:st + 1])
                    P.scan(gim[:, st, :], rb, wim[:, st, :], SI[:, st:st + 1])
            else:
                tmp = P.al(16, ns)
                w4r = wre.rearrange("p s (b q) -> p s b q", b=ns)
                w4i = wim.rearrange("p s (b q) -> p s b q", b=ns)
                P.tt(tmp, SR[:], bc3(RHO[:, l, :], n_last=ns), ALU.mult)
                P.tt(w4r[:, :, :, 0], w4r[:, :, :, 0], tmp, ALU.add)
                P.tt(tmp, SI[:], bc3(RHO[:, l, :], n_last=ns), ALU.mult)
                P.tt(w4i[:, :, :, 0], w4i[:, :, :, 0], tmp, ALU.add)
                rm3 = nat[:, 0:1024].rearrange("p (s w) -> p s w", s=16)
                P.tt(rm3, RHO[:, l, :].unsqueeze(2).to_broadcast([128, 16, 64]),
                     CT[:, C_MASK4:C_MASK4 + 64].unsqueeze(1).to_broadcast([128, 16, 64]), ALU.mult)
                rm = nat[:, 0:1024]
                P.scan(gre.rearrange("p s w -> p (s w)"), rm, wre.rearrange("p s w -> p (s w)"), 0.0)
                P.scan(gim.rearrange("p s w -> p (s w)"), rm, wim.rearrange("p s w -> p (s w)"), 0.0)
            if lastsub and not sample:
                pump_merge(0, 2)
            if lastsub:
                wc = wnext()
                Cv = wc[:, 0:4096].rearrange("p (r s m) -> p r s m", r=2, s=16)
                stream(Cv[:, 0], cbd_d[l, 0])
                stream(Cv[:, 1], cbd_d[l, 1])
                wg = wnext()
                Gv = wg[:, 0:2048].rearrange("p (c m) -> p c m", c=4)
                stream(Gv, w_glu[l].rearrange("(c p) m -> p c m", p=128))
            if ns == 1:
                cs = COS[:, l, :, 0:q]
                sn = SIN[:, l, :, 0:q]
                v = lambda a: a
            else:
                cs = COS[:, l, :, 0:q].unsqueeze(2).to_broadcast([128, 16, ns, q])
                sn = SIN[:, l, :, 0:q].unsqueeze(2).to_broadcast([128, 16, ns, q])
                v = lambda a: a.rearrange("p s (b q) -> p s b q", b=ns)
            t1 = wre
            t2 = wim
            P.tt(v(t1), v(gre), cs, ALU.mult)
            P.tt(v(t2), v(gim), sn, ALU.mult)
            P.tt(HR[:, :, c0:c0 + W], t1, t2, ALU.subtract)
            P.tt(v(t1), v(gre), sn, ALU.mult)
            P.tt(v(t2), v(gim), cs, ALU.mult)
            P.stt(NHI[:, :, c0:c0 + W], t1, -1.0, t2, ALU.mult, ALU.subtract)
            if ns == 1:
                P.copy(SR, HR[:, :, c0 + W - 1], eng="act")
                P.ts(SI, NHI[:, :, c0 + W - 1], -1.0, ALU.mult)
            else:
                h4 = HR[:, :, c0:c0 + W].rearrange("p s (b q) -> p s b q", b=ns)
                n4 = NHI[:, :, c0:c0 + W].rearrange("p s (b q) -> p s b q", b=ns)
                P.copy(SR[:], h4[:, :, :, q - 1], eng="act")
                P.ts(SI[:], n4[:, :, :, q - 1], -1.0, ALU.mult)
        P.aoff = mark
        if sample or last:
            for (srct, dstd_s, dstd_p) in ((SR, o_s5re_s, o_s5re_p), (SI, o_s5im_s, o_s5im_p)):
                if sample:
                    so = P.al(2048)
                    for b_ in range(4):
                        pq = P.ps()
                        for st in range(4 * b_, 4 * b_ + 4):
                            P.tr(pq[0:NSEQ_S, (st % 4) * 128:(st % 4 + 1) * 128], srct[:, st, :], ident)
                        P.copy(so[0:NSEQ_S, b_ * 512:(b_ + 1) * 512], pq[0:NSEQ_S, 0:512])
                    P.dma(dstd_s[l], so[0:NSEQ_S, :], q="act")
                else:
                    pso = P.ps()
                    P.tr(pso[0:16, 0:128], srct, ident)
                    so = P.al(128)
                    P.copy(so[0:16], pso[0:16, 0:128])
                    P.dma(dstd_p[l], so[0:16], q="act")
        psY = P.ps()
        for ct in range(4):
            k = 0
            for sl_ in range(4):
                st = ct * 4 + sl_
                P.mm(psY[:, ct * T:(ct + 1) * T], Cv[:, 0, st, :], HR[:, st, :], start=(k == 0), stop=False)
                k += 1
                P.mm(psY[:, ct * T:(ct + 1) * T], Cv[:, 1, st, :], NHI[:, st, :], start=False, stop=(sl_ == 3))
        yb = P.al(4, T)
        for ct in range(4):
            P.stt(yb[:, ct, :], PJ_U[:, ct, 0:T], pc(l, PP_S5D + ct), psY[:, ct * T:(ct + 1) * T], ALU.mult, ALU.add)
        gg = P.al(4, T)
        P.act(gg, yb, AF.Gelu_apprx_tanh)
        psG = P.ps()
        for co in range(4):
            for ct in range(4):
                P.mm(psG[:, co * T:(co + 1) * T], Gv[:, ct, co * 128:(co + 1) * 128], gg[:, ct, :],
                     start=(ct == 0), stop=(ct == 3))
        sg = P.al(4, T)
        P.act(sg, psG[:, 0:4 * T].rearrange("p (c t) -> p c t", c=4), AF.Sigmoid)
        P.tt(sg, sg, gg, ALU.mult)
        P.tt(YB[1][:, :, 0:T], sg, PJ_GATE[:, :, 0:T], ALU.mult)

    def ml_branch(l, T, nseq, Q, sample, last):
        wb = wnext()
        Wq = wb[:, 0:512].rearrange("p (h k) -> p h k", h=4)
        Wk = wb[:, 512:1024].rearrange("p (h k) -> p h k", h=4)
        Wv = wb[:, 1024:1536].rearrange("p (h k) -> p h k", h=4)
        stream(Wq, w_mq[l].rearrange("h c k -> c h k"))
        stream(Wk, w_mk[l].rearrange("h c k -> c h k"))
        stream(Wv, w_mv[l].rearrange("h c k -> c h k"))
        xpm4 = XPM[:, :, 0:nseq * (Q + 3)].rearrange("p b (s q) -> p b s q", s=nseq)
        acc = P.al(4, nseq, Q)
        conv_fm(xpm4, [128] * 4, l, PP_MCONV, acc, Q)
        xcm = P.al(4, T)
        P.act(xcm, acc.rearrange("p b s q -> p b (s q)"), AF.Silu)
        if not sample:
            P.copy(HIST_M[:, l, :, :], xpm4[:, :, 0, Q:Q + 3], eng="act")
        e_ = P.al(T)
        P.act(e_[0:4], PJ_CF[0:4, 0:T], AF.Exp, bias=NFB[0:4, l:l + 1], scale=-1.0)
        sp = P.al(T)
        P.act(sp[0:4], e_[0:4], AF.Ln, bias=EPSB[0:4, 1:2])
        lf = P.al(T)
        P.ts(lf[0:4], sp[0:4], -1.0, ALU.mult)
        bT = P.al(T)
        if sample:
            P.scan(bT[0:4], CT[0:4, C_MASK4:C_MASK4 + T], lf[0:4], 0.0)
        else:
            P.scan(bT[0:4], CT[0:4, C_ONES:C_ONES + T], lf[0:4], 0.0)
        rT = P.al(T)
        P.tt(rT[0:4], PJ_CI[0:4, 0:T], bT[0:4], ALU.subtract)
        hnT = P.al(4, T)
        mark = P.aoff
        sel = CT[:, C_SEL4:C_SEL4 + 128] if sample else CT[:, C_SEL128:C_SEL128 + 128]
        for s in range(nseq):
            P.aoff = mark
            c0 = s * Q
            sl = slice(c0, c0 + Q)
            if sample:
                Cst, Nst, Mst = MC_S[:], MN_S[:], MM_S[:]
                P.dma(MC_S[:], st_mC[l, s].rearrange("h k v -> k h v"))
                nat = P.al(128)
                P.dma(nat[0:4], st_mn[l, s])
                psx = P.ps()
                P.tr(psx[:, 0:4], nat[0:4, :], ident[0:4, 0:4])
                P.copy(MN_S[:, :, 0], psx[:, 0:4])
                P.copy(MN_S[:, :, 1], psx[:, 0:4])
                P.dma(MM_S[:], st_mm[l, s:s + 1, :].partition_broadcast(128))
            else:
                Cst, Nst, Mst = MC_P[:, l], MN_P[:, l], MM_P[:, l]
            tok = P.al(12)
            psg = P.ps()
            P.tr(psg[0:Q, 0:4], bT[0:4, sl], ident[0:4, 0:4])
            P.tr(psg[0:Q, 4:8], rT[0:4, sl], ident[0:4, 0:4])
            P.copy(tok[0:Q, 0:8], psg[0:Q, 0:8], eng="act")
            b_tok = tok[0:Q, 0:4]
            r_tok = tok[0:Q, 4:8]
            m_t = tok[0:Q, 8:12]
            psq = P.ps()
            psk = P.ps()
            for h in range(4):
                P.mm(psq[:, h * Q:(h + 1) * Q], Wq[:, h, :], xcm[:, h, sl])
                P.mm(psk[:, h * Q:(h + 1) * Q], Wk[:, h, :], xcm[:, h, sl])
            qT = P.al(4, Q)
            kT = P.al(4, Q)
            P.copy(qT, psq[:, 0:4 * Q].rearrange("p (h q) -> p h q", h=4), eng="act")
            P.act(kT, psk[:, 0:4 * Q].rearrange("p (h q) -> p h q", h=4), AF.Copy, scale=128.0 ** -0.5)
            pskt = P.ps()
            psvt = P.ps()
            for h in range(4):
                P.mm(pskt[0:Q, h * 128:(h + 1) * 128], xcm[:, h, sl], Wk[:, h, :])
                P.mm(psvt[0:Q, h * 128:(h + 1) * 128], xpm4[:, h, s, 3:3 + Q], Wv[:, h, :])
            vtok = P.al(4, 128)
            P.copy(vtok[0:Q], psvt[0:Q, :].rearrange("p (h v) -> p h v", h=4), eng="act")
            ktok = P.al(4, 128)
            P.copy(ktok[0:Q], pskt[0:Q, :].rearrange("p (h v) -> p h v", h=4), eng="act")
            rdiag = P.al(4, Q)
            P.tt(rdiag[0:Q], bc3(ident[0:Q, 0:Q], n_mid=4), bc3(r_tok, n_last=Q), ALU.mult)
            psR = P.ps()
            P.mm(psR[:, 0:4 * Q].rearrange("p (h t) -> p h t", h=4), ones[0:Q, :], rdiag[0:Q])
            dl = P.al(4, Q)
            P.tt(dl[0:Q], psR[0:Q, 0:4 * Q].rearrange("p (h t) -> p h t", h=4), bc3(b_tok, n_last=Q), ALU.add)
            P.tt(dl[0:Q], dl[0:Q], bc3(CT[0:Q, C_NEGL:C_NEGL + Q], n_mid=4), ALU.add)
            mx = P.al(4)
            P.red(mx[0:Q], dl[0:Q], ALU.max)
            bm = P.al(4)
            P.tt(bm[0:Q], b_tok, Mst[0:Q], ALU.add)
            P.tt(m_t, bm[0:Q], mx[0:Q], ALU.max)
            negm = P.al(4)
            P.ts(negm[0:Q], m_t, -1.0, ALU.mult)
            Wt = P.al(4, Q)
            for h in range(4):
                P.act(Wt[0:Q, h, :], dl[0:Q, h, :], AF.Exp, bias=negm[0:Q, h:h + 1])
            if not sample:
                pump_merge(1)
            psqk = P.ps()
            for h in range(4):
                P.mm(psqk[0:Q, h * Q:(h + 1) * Q], qT[:, h, :], kT[:, h, :])
            S = P.al(4, Q)
            P.tt(S[0:Q], Wt[0:Q], psqk[0:Q, 0:4 * Q].rearrange("p (h t) -> p h t", h=4), ALU.mult)
            dotin = P.al(4)
            P.red(dotin[0:Q], S[0:Q], ALU.add)
            psst = P.ps()
            for h in range(4):
                P.tr(psst[0:Q, h * Q:(h + 1) * Q], S[0:Q, h, :], ident[0:Q, 0:Q])
            ST = P.al(4, Q)
            P.copy(ST[0:Q], psst[0:Q, 0:4 * Q].rearrange("p (h t) -> p h t", h=4), eng="act")
            psn = P.ps()
            psi = P.ps()
            psqn = P.ps()
            for h in range(4):
                P.mm(psn[0:Q, h * 128:(h + 1) * 128], ST[0:Q, h, :], vtok[0:Q, h, :])
                P.mm(psi[0:Q, h * 128:(h + 1) * 128], qT[:, h, :], Cst[:, h, :])
                P.mm(psqn[0:Q, 2 * h:2 * h + 2], qT[:, h, :], Nst[:, h, :])
            scl = P.al(4)
            P.tt(scl[0:Q], bm[0:Q], m_t, ALU.subtract)
            P.act(scl[0:Q], scl[0:Q], AF.Exp)
            num = P.al(4, 128)
            P.tt(num[0:Q], psi[0:Q, :].rearrange("p (h v) -> p h v", h=4), bc3(scl[0:Q], n_last=128), ALU.mult)
            P.tt(num[0:Q], num[0:Q], psn[0:Q, :].rearrange("p (h v) -> p h v", h=4), ALU.add)
            dot = P.al(4)
            P.tt(dot[0:Q], psqn[0:Q, 0:8].rearrange("p (h t) -> p h t", h=4)[:, :, 0], scl[0:Q], ALU.mult)
            P.tt(dot[0:Q], dot[0:Q], dotin[0:Q], ALU.add)
            emn = P.al(4)
            emn2 = P.al(4)
            P.act(emn[0:Q], negm[0:Q], AF.Exp)
            P.ts(emn2[0:Q], dot[0:Q], -1.0, ALU.mult)
            P.tt(dot[0:Q], dot[0:Q], emn2[0:Q], ALU.max)
            P.tt(dot[0:Q], dot[0:Q], emn[0:Q], ALU.max)
            P.recip(dot[0:Q], dot[0:Q])
            P.tt(num[0:Q], num[0:Q], bc3(dot[0:Q], n_last=128), ALU.mult)
            if not sample:
                pump_merge(1)
            pso = P.ps()
            for h in range(4):
                P.tr(pso[0:Q, h * 128:(h + 1) * 128], PJ_CO[:, h, sl], ident)
            P.tt(num[0:Q], num[0:Q], pso[0:Q, :].rearrange("p (h v) -> p h v", h=4), ALU.mult)
            mean = P.al(4)
            P.red(mean[0:Q], num[0:Q], ALU.add)
            P.ts(mean[0:Q], mean[0:Q], 1.0 / 128.0, ALU.mult)
            P.tt(num[0:Q], num[0:Q], bc3(mean[0:Q], n_last=128), ALU.subtract)
            sq = P.al(4, 128)
            P.tt(sq[0:Q], num[0:Q], num[0:Q], ALU.mult)
            var = P.al(4)
            P.red(var[0:Q], sq[0:Q], ALU.add)
            P.act(var[0:Q], var[0:Q], AF.Sqrt, bias=EPSB[0:Q, 0:1], scale=1.0 / 128.0)
            P.recip(var[0:Q], var[0:Q])
            P.tt(num[0:Q], num[0:Q], bc3(var[0:Q], n_last=128), ALU.mult)
            if not sample:
                pump_merge(1)
            pst = P.ps()
            for h in range(4):
                P.tr(pst[:, h * Q:(h + 1) * Q], num[0:Q, h, :], ident[0:Q, 0:Q])
            P.copy(hnT[:, :, sl], pst[:, 0:4 * Q].rearrange("p (h q) -> p h q", h=4), eng="act")
            psl = P.ps()
            P.mm(psl[:, 0:12], sel[0:Q, :], tok[0:Q, 0:12])
            L12 = P.al(12)
            P.copy(L12, psl[:, 0:12])
            lm = P.al(4)
            P.tt(lm, L12[:, 0:4], L12[:, 8:12], ALU.subtract)
            wend = P.al(4)
            P.tt(wend[0:Q], r_tok, lm[0:Q], ALU.add)
            P.act(wend[0:Q], wend[0:Q], AF.Exp)
            P.ts(wend[0:Q], wend[0:Q], 128.0 ** -0.5, ALU.mult)
            carry = P.al(4)
            P.tt(carry, lm, Mst, ALU.add)
            P.act(carry, carry, AF.Exp)
            kw = P.al(4, 128)
            P.tt(kw[0:Q], ktok[0:Q], bc3(wend[0:Q], n_last=128), ALU.mult)
            if not sample:
                pump_merge(1)
            psc = P.ps()
            psdn = P.ps()
            for h in range(4):
                P.mm(psc[:, h * 128:(h + 1) * 128], kw[0:Q, h, :], vtok[0:Q, h, :])
                P.mm(psdn[:, 2 * h:2 * h + 2], kw[0:Q, h, :], ones[0:Q, 0:2])
            csc = P.al(4, 128)
            nsc = P.al(4, 2)
            for h in range(4):
                P.act(csc[:, h, :], Cst[:, h, :], AF.Copy, scale=carry[:, h:h + 1])
                P.act(nsc[:, h, :], Nst[:, h, :], AF.Copy, scale=carry[:, h:h + 1])
            P.tt(Cst, csc, psc[:, :].rearrange("p (h v) -> p h v", h=4), ALU.add)
            P.tt(Nst, nsc, psdn[:, 0:8].rearrange("p (h t) -> p h t", h=4), ALU.add)
            P.copy(Mst, L12[:, 8:12], eng="act")
            if sample or last:
                P.dma((o_mC_s[l, s] if sample else o_mC_p[l]).rearrange("h k v -> k h v"), Cst, q="act")
                psx = P.ps()
                ncp = P.al(4)
                P.copy(ncp, Nst[:, :, 0], eng="act")
                P.tr(psx[0:4, 0:128], ncp, ident)
                no = P.al(128)
                P.copy(no[0:4], psx[0:4, 0:128])
                P.dma(o_mn_s[l, s] if sample else o_mn_p[l], no[0:4], q="act")
                mo = P.al(4)
                P.copy(mo[0:1], Mst[0:1])
                P.dma(o_mm_s[l, s:s + 1, :] if sample else o_mm_p[l], mo[0:1], q="act")
        P.aoff = mark
        if sample or last:
            cvi = P.al(4, nseq, 3)
            if sample:
                P.copy(cvi, xpm4[:, :, :, Q:Q + 3])
            else:
                P.copy(cvi[:, :, 0, :], HIST_M[:, l, :, :])
            n3 = nseq * 3
            psv = P.ps()
            for bi in range(4):
                P.tr(psv[0:n3, bi * 128:(bi + 1) * 128], cvi[:, bi].rearrange("p s j -> p (s j)"), ident)
            cvo = P.al(512)
            P.copy(cvo[0:n3], psv[0:n3, 0:512])
            P.dma(o_mconv_s[l] if sample else o_mconv_p[l], cvo[0:n3], q="act")
        yy = P.al(4, T)
        for h in range(4):
            P.ts(yy[:, h, :], hnT[:, h, :], pc(l, PP_MNW + h), ALU.mult)
            P.stt(yy[:, h, :], xcm[:, h, :], pc(l, PP_MSKIP + h), yy[:, h, :], ALU.mult, ALU.add)
        P.tt(YB[2][:, :, 0:T], yy, PJ_CZ[:, :, 0:T], ALU.mult)

    NFB = P.sb("nfb", [128, L])
    for l in range(L):
        P.ts(NFB[0:4, l:l + 1], pc(l, PP_MFB, 1, 4), -1.0, ALU.mult)

    def hg_branch(l, T, nseq, Q, sample, last):
        q = QS if sample else 64
        nsub = T // q
        fg = P.al(4, T)
        for ct in range(4):
            P.ts(fg[:, ct, :], PJ_DF[:, ct, 0:T], OML[:, l, ct:ct + 1], ALU.mult, LB[:, l, ct:ct + 1], ALU.add)
        lf = P.al(4, T)
        P.act(lf, fg, AF.Ln)
        kT = P.al(4, T)
        P.ts(kT, fg, -1.0, ALU.mult, 1.0, ALU.add)
        qT = P.al(4, T)
        P.ts(qT, PJ_DQ[:, :, 0:T], 128.0 ** -0.5, ALU.mult)
        gT = P.al(4, T)
        mk = CT[:, C_MASKH_S:C_MASKH_S + 4 * T] if sample else CT[:, C_MASKH_P:C_MASKH_P + 4 * T]
        P.scan(gT.rearrange("p c t -> p (c t)"), mk, lf.rearrange("p c t -> p (c t)"), 0.0)
        eg = P.al(4, T)
        P.act(eg, gT, AF.Exp)
        qe = P.al(4, T)
        P.tt(qe, qT, eg, ALU.mult)
        onT = P.al(4, T)
        mark = P.aoff
        for s in range(nsub):
            P.aoff = mark
            c0 = s * q
            sl = slice(c0, c0 + q)
            if sample:
                Sst = HG_S[:]
                P.dma(HG_S[:], st_hg[l, s].rearrange("h k v -> k h v"))
            else:
                Sst = HG_P[:, l]
            gref = gT[:, :, c0 + q // 2]
            glast = gT[:, :, c0 + q - 1]
            Dx = P.al(3, 4, q)
            P.tt(Dx[:, 0], gT[:, :, sl], bc3(gref, n_last=q), ALU.subtract)
            P.ts(Dx[:, 1], Dx[:, 0], -1.0, ALU.mult)
            P.tt(Dx[:, 2], bc3(glast, n_last=q), gT[:, :, sl], ALU.subtract)
            P.act(Dx, Dx, AF.Exp)
            qg = P.al(4, q)
            kg = P.al(4, q)
            kd = P.al(4, q)
            P.tt(qg, qT[:, :, sl], Dx[:, 0], ALU.mult)
            P.tt(kg, kT[:, :, sl], Dx[:, 1], ALU.mult)
            P.tt(kd, kT[:, :, sl], Dx[:, 2], ALU.mult)
            pump_merge(2)
            psa = P.ps()
            for h in range(4):
                P.mm(psa[0:q, h * q:(h + 1) * q], kg[:, h, :], qg[:, h, :])
            attT = P.al(4, q)
            P.tt(attT[0:q], psa[0:q, 0:4 * q].rearrange("p (h t) -> p h t", h=4),
                 bc3(CT[0:q, C_U01:C_U01 + q], n_mid=4), ALU.mult)
            psv = P.ps()
            psk = P.ps()
            for h in range(4):
                P.tr(psv[0:q, h * 128:(h + 1) * 128], PJ_DI[:, h, sl], ident)
                P.tr(psk[0:q, h * 128:(h + 1) * 128], kd[:, h, :], ident)
            vtok = P.al(4, 128)
            kdtok = P.al(4, 128)
            P.copy(vtok[0:q], psv[0:q, :].rearrange("p (h v) -> p h v", h=4), eng="act")
            P.copy(kdtok[0:q], psk[0:q, :].rearrange("p (h v) -> p h v", h=4), eng="act")
            pso = P.ps()
            for h in range(4):
                P.mm(pso[0:q, h * 128:(h + 1) * 128], attT[0:q, h, :], vtok[0:q, h, :], start=True, stop=False)
                P.mm(pso[0:q, h * 128:(h + 1) * 128], qe[:, h, sl], Sst[:, h, :], start=False, stop=True)
            o = P.al(4, 128)
            P.copy(o[0:q], pso[0:q, :].rearrange("p (h v) -> p h v", h=4), eng="act")
            sq = P.al(4, 128)
            P.tt(sq[0:q], o[0:q], o[0:q], ALU.mult)
            ssq = P.al(4)
            P.red(ssq[0:q], sq[0:q], ALU.add)
            P.act(ssq[0:q], ssq[0:q], AF.Sqrt, bias=EPSB[0:q, 0:1], scale=1.0 / 128.0)
            P.recip(ssq[0:q], ssq[0:q])
            P.tt(o[0:q], o[0:q], bc3(ssq[0:q], n_last=128), ALU.mult)
            pump_merge(2)
            pst = P.ps()
            for h in range(4):
                P.tr(pst[:, h * q:(h + 1) * q], o[0:q, h, :], ident[0:q, 0:q])
            P.copy(onT[:, :, sl], pst[:, 0:4 * q].rearrange("p (h t) -> p h t", h=4), eng="act")
            egl = P.al(4)
            P.act(egl, glast, AF.Exp)
            pss = P.ps()
            for h in range(4):
                P.mm(pss[:, h * 128:(h + 1) * 128], kdtok[0:q, h, :], vtok[0:q, h, :])
            ssc = P.al(4, 128)
            for h in range(4):
                P.act(ssc[:, h, :], Sst[:, h, :], AF.Copy, scale=egl[:, h:h + 1])
            P.tt(Sst, ssc, pss[:, :].rearrange("p (h v) -> p h v", h=4), ALU.add)
            if sample or (last and s == nsub - 1):
                P.dma((o_hg_s[l, s] if sample else o_hg_p[l]).rearrange("h k v -> k h v"), Sst, q="act")
        P.aoff = mark
        yy = P.al(4, T)
        for ct in range(4):
            P.ts(yy[:, ct, :], onT[:, ct, :], pc(l, PP_HNW + ct), ALU.mult)
        P.tt(YB[3][:, :, 0:T], yy, PJ_DG[:, :, 0:T], ALU.mult)

    def norm_in(l, T):
        if NORM_CUT == 0:
            return
        sq = P.al(D)
        P.act(sq[0:T], XTK[0:T, :], AF.Square)
        ss = P.al(1)
        P.red(ss[0:T], sq[0:T], ALU.add)
        P.act(ss[0:T], ss[0:T], AF.Sqrt, bias=EPSB[0:T, 0:1], scale=1.0 / D)
        P.recip(ss[0:T], ss[0:T])
        if NORM_CUT == 1:
            return
        P.ts(sq[0:T], XTK[0:T, :], ss[0:T, 0:1], ALU.mult)
        if NORM_CUT == 2:
            return
        for half in range(2):
            ps = P.ps()
            for j in range(4):
                kt = half * 4 + j
                P.tr(ps[:, j * T:(j + 1) * T], sq[0:T, kt * 128:(kt + 1) * 128], ident[0:T, 0:T])
            if NORM_CUT == 3:
                continue
            for j in range(4):
                kt = half * 4 + j
                if NORM_CUT == 15:
                    P.copy(XN[:, kt, 0:T], ps[:, j * T:(j + 1) * T])
                elif NORM_CUT == 16:
                    P.ts(XN[:, kt, 0:T], ps[:, j * T:(j + 1) * T], pc(l, PP_NORMW + kt), ALU.mult)
                elif NORM_CUT == 13:
                    P.copy(XN[:, kt, 0:T], ps[:, j * T:(j + 1) * T], eng=("act" if j % 2 == 0 else "dve"))
                elif NORM_CUT == 14:
                    P.act(XN[:, kt, 0:T], ps[:, j * T:(j + 1) * T], AF.Copy, scale=EPSB[:, 1:2])
                elif (NORM_CUT == 11) or (NORM_CUT not in (12,) and j % 2 == 0):
                    P.act(XN[:, kt, 0:T], ps[:, j * T:(j + 1) * T], AF.Copy, scale=pc(l, PP_NORMW + kt))
                else:
                    P.ts(XN[:, kt, 0:T], ps[:, j * T:(j + 1) * T], pc(l, PP_NORMW + kt), ALU.mult)

    mctx = {"gen": None, "todo": [], "l": 0, "T": 128}

    def merge_branch_steps(l, T, i):
        w_l = w_in[l].rearrange("(kt p) c -> p kt c", p=128)
        gt = []
        for half in range(2):
            wb = wnext()
            wv = wb[:, 0:4096].rearrange("p (k c) -> p k c", k=8)
            c0 = O_MERGE + i * 1024 + half * 512
            stream(wv, w_l[:, :, c0:c0 + 512])
            ps = P.ps()
            for kt in range(8):
                P.mm(ps[0:T, :], XN[:, kt, 0:T], wv[:, kt, :], start=(kt == 0), stop=(kt == 7))
            g = PTK[half]
            P.act(g[0:T, 0:512], ps[0:T, :], AF.Sigmoid)
            gt.append(g)
            yield
        wb = wnext()
        wv = wb[:, 0:4096].rearrange("p (c d) -> p c d", c=4)
        stream(wv, w_br[l, i].rearrange("(c p) d -> p c d", p=128))
        for half in range(2):
            hs = slice(half * 512, (half + 1) * 512)
            ps = P.ps()
            for ct in range(4):
                P.mm(ps[0:T, :], YB[i][:, ct, 0:T], wv[:, ct, hs], start=(ct == 0), stop=(ct == 3))
            g = gt[half]
            if i == 0:
                P.tt(MRGK[0:T, hs], ps[0:T, :], g[0:T, 0:512], ALU.mult)
            else:
                P.tt(g[0:T, 0:512], ps[0:T, :], g[0:T, 0:512], ALU.mult)
                P.tt(MRGK[0:T, hs], MRGK[0:T, hs], g[0:T, 0:512], ALU.add)
            if half == 0:
                yield

    def pump_merge(max_i, nsteps=1):
        for _ in range(nsteps):
            if mctx["gen"] is None:
                if not (mctx["todo"] and mctx["todo"][0] <= max_i):
                    return
                mctx["gen"] = merge_branch_steps(mctx["l"], mctx["T"], mctx["todo"].pop(0))
            try:
                next(mctx["gen"])
            except StopIteration:
                mctx["gen"] = None

    def merge_out(l, T):
        while mctx["gen"] is not None or mctx["todo"]:
            pump_merge(99)
        mt8 = P.al(8, T)
        for half in range(2):
            ps = P.ps()
            for j in range(4):
                kt = half * 4 + j
                P.tr(ps[:, j * T:(j + 1) * T], MRGK[0:T, kt * 128:(kt + 1) * 128], ident[0:T, 0:T])
            P.copy(mt8[:, half * 4:half * 4 + 4, :], ps[:, 0:4 * T].rearrange("p (j t) -> p j t", j=4),
                   eng=("act" if half == 0 else "dve"))
        for half in range(2):
            hs = slice(half * 512, (half + 1) * 512)
            wb = wnext()
            wv = wb[:, 0:4096].rearrange("p (k c) -> p k c", k=8)
            stream(wv, w_out[l].rearrange("(kt p) d -> p kt d", p=128)[:, :, hs])
            ps = P.ps()
            for kt in range(8):
                P.mm(ps[0:T, :], mt8[:, kt, :], wv[:, kt, :], start=(kt == 0), stop=(kt == 7))
            P.tt(XTK[0:T, hs], XTK[0:T, hs], ps[0:T, :], ALU.add)

    tiles = [("p", i) for i in range(n_ptiles)]
    if with_sample:
        tiles.append(("s", 0))
    for (kind, ti) in tiles:
        sample = kind == "s"
        T = NSEQ_S * QS if sample else 128
        nseq = NSEQ_S if sample else 1
        Q = QS if sample else 128
        first = (ti == 0)
        last = (ti == n_ptiles - 1)
        P.aoff = 0
        src = xs_d[:, :] if sample else xp_d[ti * 128:(ti + 1) * 128, :]
        P.dma(XTK[0:T, :], src)
        for l in range(L):
            P.aoff = 0
            xps4 = XPS[:, :, 0:nseq * (Q + 3)].rearrange("p b (s q) -> p b s q", s=nseq)
            xpm4 = XPM[:, :, 0:nseq * (Q + 3)].rearrange("p b (s q) -> p b s q", s=nseq)
            if sample:
                for (src_d, xp4, nb_, cols) in ((st_sconv, xps4, 8, [0, 128, 256, 384, 512, 576, 640, 704]),
                                                (st_mconv, xpm4, 4, [0, 128, 256, 384])):
                    nch = 768 if nb_ == 8 else 512
                    nat = P.al(nch)
                    P.dma(nat[0:48], src_d[l])
                    for bi in range(nb_):
                        rows = 128 if (nb_ == 4 or bi < 4) else 64
                        psx = P.ps()
                        P.tr(psx[0:rows, 0:48], nat[0:48, cols[bi]:cols[bi] + rows], ident[0:48, 0:48])
                        P.copy(xp4[0:rows, bi, :, 0:3], psx[0:rows, 0:48].rearrange("p (s j) -> p s j", s=NSEQ_S))
            else:
                P.copy(xps4[:, :, 0, 0:3], HIST_S[:, l, :, :], eng="act")
                P.copy(xpm4[:, :, 0, 0:3], HIST_M[:, l, :, :], eng="act")
            P.aoff = 0
            norm_in(l, T)
            mctx["todo"] = [0, 1, 2, 3] if "merge" in stages else []
            mctx["gen"] = None
            mctx["l"] = l
            mctx["T"] = T
            P.cur_tag = "%s%d_proj" % (kind, ti)
            pending[:] = proj_blocks(l, T, nseq, Q) if "proj" in stages else []
            for bi_, (nm_, fn_) in enumerate((("ssd", lambda: ssd_branch(l, T, nseq, Q, sample, first, last, None)),
                                              ("s5", lambda: s5_branch(l, T, nseq, Q, sample, last)),
                                              ("ml", lambda: ml_branch(l, T, nseq, Q, sample, last)),
                                              ("hg", lambda: hg_branch(l, T, nseq, Q, sample, last)))):
                P.aoff = 0
                flush_upto(bi_)
                P.cur_tag = "%s%d_%s" % (kind, ti, nm_)
                if nm_ in stages:
                    fn_()
                else:
                    P.memset(YB[bi_][:], 0.0)
            P.aoff = 0
            P.cur_tag = "%s%d_merge" % (kind, ti)
            flush_upto(99)
            if "merge" in stages:
                merge_out(l, T)
        P.aoff = 0
        sq = P.al(D)
        P.act(sq[0:T], XTK[0:T, :], AF.Square)
        ss = P.al(1)
        P.red(ss[0:T], sq[0:T], ALU.add)
        P.act(ss[0:T], ss[0:T], AF.Sqrt, bias=EPSB[0:T, 0:1], scale=1.0 / D)
        P.recip(ss[0:T], ss[0:T])
        xo = P.al(D)
        P.stt(xo[0:T], XTK[0:T, :], ss[0:T, 0:1], FNW[0:T, :], ALU.mult, ALU.mult)
        dst = y_s[:, :] if sample else y_p[ti * 128:(ti + 1) * 128, :]
        P.dma(dst, xo[0:T], q="act")

    P.emit()
    return nc, P


_CACHE = {}


def _prep_inputs(inp, n_ptiles=16):
    f = lambda a: np.ascontiguousarray(np.asarray(a, np.float32))
    consts = _host_consts()
    pp = _host_pp(inp)
    bbd, cbd = _host_s5_bd(inp)
    shared = {
        "w_in": f(inp["w_in"]), "w_br": f(inp["w_branch"]), "w_out": f(inp["w_out"]),
        "w_glu": f(inp["s5_glu_w"]), "w_mq": f(inp["ml_wq"]), "w_mk": f(inp["ml_wk"]), "w_mv": f(inp["ml_wv"]),
        "bbd": bbd, "cbd": cbd, "consts": consts, "pp": pp,
        "fnw": f(inp["final_norm_w"]).reshape(1, D),
    }
    maps = []
    for c in range(NCORE):
        b0 = c * NSEQ_S
        sl = slice(b0, b0 + NSEQ_S)
        m = dict(shared)
        m["xp"] = f(inp["x_prompt"][c, :n_ptiles * 128])
        m["xs"] = f(inp["x_sample"][sl]).reshape(NSEQ_S * QS, D)
        m["st_sconv"] = f(inp["state_ssd_conv"][:, sl]).reshape(L, NSEQ_S * 3, 768)
        m["st_ssd"] = f(inp["state_ssd"][:, sl])
        m["st_s5re"] = f(inp["state_s5_re"][:, sl]).reshape(L, NSEQ_S, 2048)
        m["st_s5im"] = f(inp["state_s5_im"][:, sl]).reshape(L, NSEQ_S, 2048)
        m["st_mconv"] = f(inp["state_mlstm_conv"][:, sl]).reshape(L, NSEQ_S * 3, 512)
        m["st_mC"] = f(inp["state_mlstm_C"][:, sl])
        m["st_mn"] = f(inp["state_mlstm_n"][:, sl])
        m["st_mm"] = f(inp["state_mlstm_m"][:, sl])
        m["st_hg"] = f(inp["state_hgrn"][:, sl])
        maps.append(m)
    return maps


def _gather(res, n_ptiles=16):
    R = res
    cat1 = lambda k, shp: np.stack([r[k] for r in R], axis=1).reshape(shp)
    cats = lambda k, shp: np.concatenate([r[k].reshape((L, NSEQ_S) + r[k].shape[2:]) if False else r[k] for r in R], axis=1)
    y_p = np.stack([r["y_p"] for r in R], axis=0)
    y_s = np.concatenate([r["y_s"].reshape(NSEQ_S, QS, D) for r in R], axis=0)
    B = NCORE

    def P_(k, tail):
        return np.stack([r[k].reshape((L,) + tail) for r in R], axis=1)

    def S_(k, tail):
        return np.concatenate([r[k].reshape((L, NSEQ_S) + tail) for r in R], axis=1)

    outs = (
        y_p, y_s,
        P_("o_sconv_p", (3, 768)), S_("o_sconv_s", (3, 768)),
        P_("o_ssd_p", (8, 64, 64)), S_("o_ssd_s", (8, 64, 64)),
        P_("o_s5re_p", (32, 64)), S_("o_s5re_s", (32, 64)),
        P_("o_s5im_p", (32, 64)), S_("o_s5im_s", (32, 64)),
        P_("o_mconv_p", (3, 512)), S_("o_mconv_s", (3, 512)),
        P_("o_mC_p", (4, 128, 128)), S_("o_mC_s", (4, 128, 128)),
        P_("o_mn_p", (4, 128)), S_("o_mn_s", (4, 128)),
        P_("o_mm_p", (4,)), S_("o_mm_s", (4,)),
        P_("o_hg_p", (4, 128, 128)), S_("o_hg_s", (4, 128, 128)),
    )
    return tuple(np.ascontiguousarray(o.astype(np.float32)) for o in outs)


def kernel(**inputs):
    n_ptiles = inputs["x_prompt"].shape[1] // 128
    nc, _ = build(n_ptiles=n_ptiles)
    maps = _prep_inputs(inputs, n_ptiles)
    res = run_bass_kernel_spmd(nc, maps, core_ids=list(range(NCORE)))
    return _gather(res.results, n_ptiles)
```
